# Optimizing a Trainium2 kernel written in Bass

```python
import math
import jax
import jax.numpy as jnp
from jax import lax
import numpy as np

D_MODEL = 4096
BATCH = 2
SEQ = 8192
DEPTH = 4

PLE_DIM = 256
N_MIXERS = 3
N_FOX = (DEPTH + 2) // 3
N_GDN = (DEPTH + 1) // 3
N_SSM = DEPTH // 3
NORM_EPS = 1e-6

FOX_HEADS = 32
FOX_HEAD_DIM = D_MODEL // FOX_HEADS
FOX_WIDTH = FOX_HEADS * FOX_HEAD_DIM
Q_BLOCK = 128

GDN_HEADS = 32
GDN_HEAD_DIM = D_MODEL // GDN_HEADS
GDN_WIDTH = GDN_HEADS * GDN_HEAD_DIM
GDN_CONV = 4
GDN_CHUNK = 64

SSM_WIDTH = D_MODEL
SSM_GROUP = 16
SSM_GROUPS = SSM_WIDTH // SSM_GROUP
SSM_STATE = 64
SSM_CHUNK_MAX = 1024
DT_MIN = 1e-3
DT_MAX = 1e-1

kernel_name = 'hybrid_fox_gdn_s5_ple_trunk'


def rmsnorm(x, g):
    xf = x.astype(jnp.float32)
    y = xf * lax.rsqrt(jnp.mean(xf * xf, axis=-1, keepdims=True) + NORM_EPS)
    return (y * g.astype(jnp.float32)).astype(x.dtype)


def l2norm(x):
    return x * lax.rsqrt(jnp.sum(x * x, axis=-1, keepdims=True) + NORM_EPS)


def fox_mixer(h, w_in, b_f, w_out):
    bsz, s, _ = h.shape
    H, dh, W = FOX_HEADS, FOX_HEAD_DIM, FOX_WIDTH
    proj = h @ w_in
    q, k, v, z, f_logit = jnp.split(proj, [W, 2 * W, 3 * W, 4 * W], axis=-1)
    to_heads = lambda t: t.reshape(bsz, s, H, dh).transpose(0, 2, 1, 3)
    q, k, v = to_heads(q), to_heads(k), to_heads(v)
    log_f = jax.nn.log_sigmoid(f_logit.astype(jnp.float32) + b_f.astype(jnp.float32))
    cum = jnp.cumsum(log_f, axis=1).transpose(0, 2, 1)
    nb = s // Q_BLOCK
    q_blocks = q.reshape(bsz, H, nb, Q_BLOCK, dh).transpose(2, 0, 1, 3, 4)
    c_blocks = cum.reshape(bsz, H, nb, Q_BLOCK).transpose(2, 0, 1, 3)
    key_pos = jnp.arange(s)
    scale = dh ** -0.5

    def attend_block(args):
        q_blk, c_blk, blk = args
        query_pos = blk * Q_BLOCK + jnp.arange(Q_BLOCK)
        logits = jnp.einsum('bhqd,bhkd->bhqk', q_blk, k, preferred_element_type=jnp.float32) * scale
        logits = logits + c_blk[..., :, None] - cum[..., None, :]
        logits = jnp.where(key_pos[None, :] <= query_pos[:, None], logits, -jnp.inf)
        probs = jax.nn.softmax(logits, axis=-1)
        return jnp.einsum('bhqk,bhkd->bhqd', probs.astype(v.dtype), v)

    o = lax.map(attend_block, (q_blocks, c_blocks, jnp.arange(nb)))
    o = o.transpose(1, 0, 3, 2, 4).reshape(bsz, s, W)
    return (o * jax.nn.silu(z)) @ w_out


def causal_conv_silu(x, w):
    k_width, s = w.shape[0], x.shape[1]
    xp = jnp.pad(x, ((0, 0), (k_width - 1, 0), (0, 0)))
    y = xp[:, 0:s] * w[0]
    for j in range(1, k_width):
        y = y + xp[:, j:j + s] * w[j]
    return jax.nn.silu(y)


def chunk_gated_delta_rule(q, k, v, beta, g):
    bsz, nh, s, dk = q.shape
    dv = v.shape[-1]
    c = GDN_CHUNK
    nc = s // c
    q, k, v = (t.reshape(bsz, nh, nc, c, t.shape[-1]) for t in (q, k, v))
    beta = beta.reshape(bsz, nh, nc, c)
    g = jnp.cumsum(g.reshape(bsz, nh, nc, c), axis=-1)
    incl = jnp.tril(jnp.ones((c, c), dtype=bool))
    strict = jnp.tril(jnp.ones((c, c), dtype=bool), k=-1)
    decay = jnp.exp(jnp.where(incl, g[..., :, None] - g[..., None, :], -jnp.inf))
    k_beta = k * beta[..., None]
    lower = jnp.where(strict, jnp.einsum('bhnid,bhnjd->bhnij', k_beta, k) * decay, 0.0)
    rhs = jnp.concatenate([v * beta[..., None], k_beta * jnp.exp(g)[..., None]], axis=-1)
    sol = lax.linalg.triangular_solve(lower + jnp.eye(c, dtype=lower.dtype), rhs,
                                      left_side=True, lower=True, unit_diagonal=True)
    u, w = sol[..., :dv], sol[..., dv:]
    intra = jnp.einsum('bhnid,bhnjd->bhnij', q, k) * decay
    g_last = g[..., -1]
    q_dec = q * jnp.exp(g)[..., None]
    k_dec = k * jnp.exp(g_last[..., None] - g)[..., None]
    xs = tuple(jnp.moveaxis(t, 2, 0) for t in (q_dec, k_dec, intra, u, w, jnp.exp(g_last)))

    def step(state, inp):
        qd, kd, a, u_c, w_c, gl = inp
        v_new = u_c - jnp.einsum('bhck,bhkv->bhcv', w_c, state)
        out = jnp.einsum('bhck,bhkv->bhcv', qd, state) + jnp.einsum('bhcj,bhjv->bhcv', a, v_new)
        state = state * gl[..., None, None] + jnp.einsum('bhck,bhcv->bhkv', kd, v_new)
        return state, out

    state0 = jnp.zeros((bsz, nh, dk, dv), jnp.float32)
    _, o = lax.scan(step, state0, xs)
    return jnp.moveaxis(o, 0, 2).reshape(bsz, nh, s, dv)


def gdn_mixer(h, w_in, conv_w, a_log, dt_bias, norm_w, w_out):
    bsz, s, _ = h.shape
    H, dh, W = GDN_HEADS, GDN_HEAD_DIM, GDN_WIDTH
    proj = h @ w_in
    qkv, z, b_logit, a_logit = jnp.split(proj, [3 * W, 4 * W, 4 * W + H], axis=-1)
    qkv = causal_conv_silu(qkv, conv_w).astype(jnp.float32)
    to_heads = lambda t: t.reshape(bsz, s, H, dh).transpose(0, 2, 1, 3)
    q, k, v = (to_heads(t) for t in jnp.split(qkv, 3, axis=-1))
    q = l2norm(q) * dh ** -0.5
    k = l2norm(k)
    beta = jax.nn.sigmoid(b_logit.astype(jnp.float32)).transpose(0, 2, 1)
    g = -(jnp.exp(a_log.astype(jnp.float32))
          * jax.nn.softplus(a_logit.astype(jnp.float32) + dt_bias.astype(jnp.float32))).transpose(0, 2, 1)
    o = chunk_gated_delta_rule(q, k, v, beta, g).transpose(0, 2, 1, 3)
    o = rmsnorm(o, norm_w) * jax.nn.silu(z.astype(jnp.float32).reshape(bsz, s, H, dh))
    return o.reshape(bsz, s, W).astype(h.dtype) @ w_out


def complex_linear_combine(e1, e2):
    a1r, a1i, b1r, b1i = e1
    a2r, a2i, b2r, b2i = e2
    return (a2r * a1r - a2i * a1i,
            a2r * a1i + a2i * a1r,
            a2r * b1r - a2i * b1i + b2r,
            a2r * b1i + a2i * b1r + b2i)


def ssm_mixer(h, w_in, lam_re, lam_im, b_re, b_im, c_re, c_im, log_step, d_skip, w_glu, b_glu, w_out):
    f32 = jnp.float32
    bsz, s, _ = h.shape
    G, N, P, E = SSM_GROUPS, SSM_GROUP, SSM_STATE, SSM_WIDTH
    u, z = jnp.split(h @ w_in, 2, axis=-1)
    uf = u.astype(f32)
    lam_re, lam_im, b_re, b_im, c_re, c_im = (t.astype(f32) for t in (lam_re, lam_im, b_re, b_im, c_re, c_im))
    step = jnp.exp(log_step.astype(f32))[:, None]
    mag = jnp.exp(lam_re * step)
    lb_re, lb_im = mag * jnp.cos(lam_im * step), mag * jnp.sin(lam_im * step)
    den = lam_re * lam_re + lam_im * lam_im
    num_re = lb_re - 1.0
    zoh_re = (num_re * lam_re + lb_im * lam_im) / den
    zoh_im = (lb_im * lam_re - num_re * lam_im) / den
    bb_re = zoh_re[..., None] * b_re - zoh_im[..., None] * b_im
    bb_im = zoh_re[..., None] * b_im + zoh_im[..., None] * b_re
    ck = math.gcd(s, SSM_CHUNK_MAX)
    nck = s // ck
    u_chunks = uf.reshape(bsz, nck, ck, G, N).transpose(1, 2, 0, 3, 4)
    a_re = jnp.broadcast_to(lb_re, (ck, 1, G, P))
    a_im = jnp.broadcast_to(lb_im, (ck, 1, G, P))

    def chunk_step(carry, u_c):
        h_re, h_im = carry
        bu_re = jnp.einsum('tbgn,gpn->tbgp', u_c, bb_re)
        bu_im = jnp.einsum('tbgn,gpn->tbgp', u_c, bb_im)
        pw_re, pw_im, x_re, x_im = lax.associative_scan(
            complex_linear_combine, (a_re, a_im, bu_re, bu_im), axis=0)
        x_re, x_im = (x_re + pw_re * h_re - pw_im * h_im,
                      x_im + pw_re * h_im + pw_im * h_re)
        y = jnp.einsum('tbgp,gnp->tbgn', x_re, c_re) - jnp.einsum('tbgp,gnp->tbgn', x_im, c_im)
        return (x_re[-1], x_im[-1]), y

    carry0 = (jnp.zeros((bsz, G, P), f32), jnp.zeros((bsz, G, P), f32))
    _, y = lax.scan(chunk_step, carry0, u_chunks)
    y = y.transpose(2, 0, 1, 3, 4).reshape(bsz, s, E) + d_skip.astype(f32) * uf
    y = jax.nn.gelu(y)
    y = y * jax.nn.sigmoid(y @ w_glu.astype(f32) + b_glu.astype(f32))
    y = y * jax.nn.silu(z.astype(f32))
    return y.astype(h.dtype) @ w_out


def setup_inputs(seed: int = 0) -> dict:
    key = jax.random.key(seed)
    ks = jax.random.split(key, 32)
    f32 = jnp.float32
    D = D_MODEL
    nrm = lambda k, shape, scale: jax.random.normal(k, shape, f32) * scale
    unif = lambda k, shape, lo, hi: jax.random.uniform(k, shape, f32, lo, hi)
    x = nrm(ks[0], (BATCH, SEQ, D), 1.0)
    p = nrm(ks[1], (DEPTH, BATCH, SEQ, PLE_DIM), 1.0)
    norm_mix = 1.0 + nrm(ks[2], (DEPTH, D), 0.01)
    fox_w_in = nrm(ks[3], (N_FOX, D, 4 * FOX_WIDTH + FOX_HEADS), D ** -0.5)
    fox_b_f = nrm(ks[4], (N_FOX, FOX_HEADS), 0.1)
    fox_w_out = nrm(ks[5], (N_FOX, FOX_WIDTH, D), FOX_WIDTH ** -0.5)
    gdn_w_in = nrm(ks[6], (N_GDN, D, 4 * GDN_WIDTH + 2 * GDN_HEADS), D ** -0.5)
    gdn_conv = nrm(ks[7], (N_GDN, GDN_CONV, 3 * GDN_WIDTH), GDN_CONV ** -0.5)
    gdn_a_log = jnp.log(unif(ks[8], (N_GDN, GDN_HEADS), 1.0, 16.0))
    dt = jnp.exp(unif(ks[9], (N_GDN, GDN_HEADS), math.log(DT_MIN), math.log(DT_MAX)))
    gdn_dt_bias = dt + jnp.log(-jnp.expm1(-dt))
    gdn_norm = 1.0 + nrm(ks[10], (N_GDN, GDN_HEAD_DIM), 0.01)
    gdn_w_out = nrm(ks[11], (N_GDN, GDN_WIDTH, D), GDN_WIDTH ** -0.5)
    G, N, P, E = SSM_GROUPS, SSM_GROUP, SSM_STATE, SSM_WIDTH
    ssm_w_in = nrm(ks[12], (N_SSM, D, 2 * E), D ** -0.5)
    ssm_lam_re = -0.5 + nrm(ks[13], (N_SSM, G, P), 0.01)
    ssm_lam_im = jnp.pi * jnp.arange(P, dtype=f32) + nrm(ks[14], (N_SSM, G, P), 0.01)
    ssm_b_re = nrm(ks[15], (N_SSM, G, P, N), (2 * N) ** -0.5)
    ssm_b_im = nrm(ks[16], (N_SSM, G, P, N), (2 * N) ** -0.5)
    ssm_c_re = nrm(ks[17], (N_SSM, G, N, P), P ** -0.5)
    ssm_c_im = nrm(ks[18], (N_SSM, G, N, P), P ** -0.5)
    ssm_log_step = unif(ks[19], (N_SSM, G), math.log(DT_MIN), math.log(DT_MAX))
    ssm_d = nrm(ks[20], (N_SSM, E), 1.0)
    ssm_w_glu = nrm(ks[21], (N_SSM, E, E), E ** -0.5)
    ssm_b_glu = nrm(ks[22], (N_SSM, E), 0.01)
    ssm_w_out = nrm(ks[23], (N_SSM, E, D), E ** -0.5)
    norm_ple = 1.0 + nrm(ks[24], (DEPTH, D), 0.01)
    ple_w_proj = nrm(ks[25], (DEPTH, PLE_DIM, D), PLE_DIM ** -0.5)
    ple_w_gate = nrm(ks[26], (DEPTH, D, D), D ** -0.5)
    final_norm = 1.0 + nrm(ks[27], (D,), 0.01)
    return {'x': x, 'p': p, 'norm_mix': norm_mix,
            'fox_w_in': fox_w_in, 'fox_b_f': fox_b_f, 'fox_w_out': fox_w_out,
            'gdn_w_in': gdn_w_in, 'gdn_conv': gdn_conv, 'gdn_a_log': gdn_a_log,
            'gdn_dt_bias': gdn_dt_bias, 'gdn_norm': gdn_norm, 'gdn_w_out': gdn_w_out,
            'ssm_w_in': ssm_w_in, 'ssm_lam_re': ssm_lam_re, 'ssm_lam_im': ssm_lam_im,
            'ssm_b_re': ssm_b_re, 'ssm_b_im': ssm_b_im, 'ssm_c_re': ssm_c_re, 'ssm_c_im': ssm_c_im,
            'ssm_log_step': ssm_log_step, 'ssm_d': ssm_d, 'ssm_w_glu': ssm_w_glu,
            'ssm_b_glu': ssm_b_glu, 'ssm_w_out': ssm_w_out,
            'norm_ple': norm_ple, 'ple_w_proj': ple_w_proj, 'ple_w_gate': ple_w_gate,
            'final_norm': final_norm}


def reference(x, p, norm_mix, fox_w_in, fox_b_f, fox_w_out, gdn_w_in, gdn_conv, gdn_a_log,
              gdn_dt_bias, gdn_norm, gdn_w_out, ssm_w_in, ssm_lam_re, ssm_lam_im, ssm_b_re,
              ssm_b_im, ssm_c_re, ssm_c_im, ssm_log_step, ssm_d, ssm_w_glu, ssm_b_glu, ssm_w_out,
              norm_ple, ple_w_proj, ple_w_gate, final_norm):
    h = x
    for i in range(DEPTH):
        kind, j = i % N_MIXERS, i // N_MIXERS
        hn = rmsnorm(h, norm_mix[i])
        if kind == 0:
            y = fox_mixer(hn, fox_w_in[j], fox_b_f[j], fox_w_out[j])
        elif kind == 1:
            y = gdn_mixer(hn, gdn_w_in[j], gdn_conv[j], gdn_a_log[j], gdn_dt_bias[j],
                          gdn_norm[j], gdn_w_out[j])
        else:
            y = ssm_mixer(hn, ssm_w_in[j], ssm_lam_re[j], ssm_lam_im[j], ssm_b_re[j], ssm_b_im[j],
                          ssm_c_re[j], ssm_c_im[j], ssm_log_step[j], ssm_d[j], ssm_w_glu[j],
                          ssm_b_glu[j], ssm_w_out[j])
        h = h + y
        gate = jax.nn.sigmoid(rmsnorm(h, norm_ple[i]) @ ple_w_gate[i])
        h = h + gate * (p[i] @ ple_w_proj[i])
    return rmsnorm(h, final_norm)
```

```python
import contextlib
import numpy as np
import ml_dtypes
import concourse.bass as bass
import concourse.mybir as mybir
from concourse.bass_utils import run_bass_kernel_spmd

F32 = mybir.dt.float32
BF16 = mybir.dt.bfloat16
AF = mybir.ActivationFunctionType
ALU = mybir.AluOpType
AX = mybir.AxisListType
NCORES = 8


class Buf:
    __slots__ = ("name", "w", "r", "sem", "ndma", "excl")

    def __init__(self, name, excl=False):
        self.name = name
        self.excl = excl
        self.w = {}
        self.r = {}
        self.sem = None
        self.ndma = 0


def _merge(dst, src):
    for k, v in src.items():
        if dst.get(k, 0) < v:
            dst[k] = v


class Sched:
    ENG = ("pe", "act", "dve", "pool", "sp")

    def __init__(self, nc, stack):
        self.nc = nc
        self.stack = stack
        self.prog = {e: [] for e in self.ENG}
        self.count = {e: 0 for e in self.ENG}
        self.seen = {e: {} for e in self.ENG}
        self.sem = {}
        for e in ("pe", "act", "dve", "pool"):
            self.sem[e] = stack.enter_context(nc.semaphore("c_" + e))
        self.nbuf = 0
        self.slots = []

    def sb(self, name, shape, dt):
        return self.stack.enter_context(self.nc.sbuf_tensor(name, list(shape), dt))

    def ps(self, name, shape=(128, 512), dt=F32):
        return self.stack.enter_context(self.nc.psum_tensor(name, list(shape), dt))

    def buf(self, name=None, excl=False):
        self.nbuf += 1
        return Buf(name or "b%d" % self.nbuf, excl)

    def _waits(self, eng, deps, skip_self=False):
        out = []
        seen = self.seen[eng]
        for sem, v in deps.items():
            if skip_self and sem is self.sem.get(eng):
                continue
            if seen.get(sem, 0) < v:
                seen[sem] = v
                out.append((sem, v))
        return out

    def _deps(self, reads, writes):
        deps = {}
        for b in reads:
            _merge(deps, b.w)
            if b.excl:
                _merge(deps, b.r)
        for b in writes:
            _merge(deps, b.w)
            _merge(deps, b.r)
        return deps

    def op(self, eng, fn, reads=(), writes=()):
        deps = self._deps(reads, writes)
        waits = self._waits(eng, deps, skip_self=(eng == "pe"))
        self.count[eng] += 1
        tok = {self.sem[eng]: self.count[eng]}
        self.prog[eng].append((waits, fn, (self.sem[eng], 1)))
        for b in reads:
            _merge(b.r, tok)
        for b in writes:
            b.w = dict(tok)
            b.r = {}

    def dma(self, q, out_ap, in_ap, slot, reads=(), writes=(), **kw):
        if slot.sem is None:
            slot.sem = self.stack.enter_context(self.nc.semaphore("d_" + slot.name))
            self.slots.append(slot)
        deps = self._deps(reads, writes)
        waits = self._waits(q, deps)
        slot.ndma += 1
        tok = {slot.sem: 16 * slot.ndma}

        def fn(e, out_ap=out_ap, in_ap=in_ap, kw=kw):
            return e.dma_start(out=out_ap, in_=in_ap, **kw)

        self.prog[q].append((waits, fn, (slot.sem, 16)))
        for b in reads:
            _merge(b.r, tok)
        for b in writes:
            b.w = dict(tok)
            b.r = {}

    def finish(self, final_bufs):
        deps = {}
        for b in final_bufs:
            _merge(deps, b.w)
        for sl in self.slots:
            _merge(deps, {sl.sem: 16 * sl.ndma})
        fin = self._waits("sp", deps)
        self.prog["sp"].append((fin, None, None))
        nc = self.nc
        prog = self.prog

        def play(e, lst):
            for waits, fn, inc in lst:
                for sem, v in waits:
                    e.wait_ge(sem, v)
                if fn is not None:
                    ins = fn(e)
                    ins.then_inc(inc[0], inc[1])

        with nc.Block() as block:
            @block.tensor
            def _(e):
                play(e, prog["pe"])

            @block.scalar
            def _(e):
                play(e, prog["act"])

            @block.vector
            def _(e):
                play(e, prog["dve"])

            @block.gpsimd
            def _(e):
                play(e, prog["pool"])

            @block.sync
            def _(e):
                play(e, prog["sp"])


class Gemm:
    def __init__(self, S, nK, nslots=3, nps=2, tag="g"):
        self.S = S
        self.nK = nK
        self.wt = [S.sb("%s_w%d" % (tag, i), [128, nK, 128], BF16) for i in range(nslots)]
        self.wb = [S.buf("%s_wb%d" % (tag, i)) for i in range(nslots)]
        self.pt = [S.ps("%s_p%d" % (tag, i)) for i in range(nps)]
        self.pb = [S.buf("%s_pb%d" % (tag, i), True) for i in range(nps)]
        self.wi = 0
        self.pi = 0

    def run(self, at, at_bufs, wsrc, nblocks, T, epi, nK=None, wq="pool"):
        S = self.S
        nK = nK or self.nK
        for nb in range(nblocks):
            wi = self.wi % len(self.wt)
            self.wi += 1
            wt, wb = self.wt[wi], self.wb[wi]
            S.dma(wq, wt[:, 0:nK, :], wsrc(nb), wb, writes=[wb])
            pi = self.pi % len(self.pt)
            self.pi += 1
            pt, pb = self.pt[pi], self.pb[pi]
            for kc in range(nK):
                def mm(e, kc=kc, wt=wt, pt=pt):
                    return e.matmul(pt[:, 0:T], wt[:, kc, :], at(kc),
                                    start=(kc == 0), stop=(kc == nK - 1))
                S.op("pe", mm, reads=[wb, at_bufs[kc]], writes=[pb])
            epi(nb, pb, pt[:, 0:T])


def blockw(W, nblk=None):
    K, N = W.shape
    return np.ascontiguousarray(
        W.reshape(K // 128, 128, N // 128, 128).transpose(2, 1, 0, 3))


def emit_B(S, d, D, TT, T, ssm=False, final=False, PD=256):
    nK = D // 128
    nP = PD // 128
    nt = TT // T
    h = S.sb("B_h", [128, nK, T], F32)
    xa = S.sb("B_xa", [128, nK, T], BF16)
    xb = S.sb("B_xb", [128, nK, T], BF16)
    xc = S.sb("B_xc", [128, nK, T], BF16) if ssm else None
    pt = S.sb("B_pT", [128, nP, T], BF16)
    ones = S.sb("B_ones", [128, 128], BF16)
    g2 = S.sb("B_g2", [128, nK], F32)
    gn = S.sb("B_gn", [128, nK], F32)
    bglu = S.sb("B_bglu", [128, nK], F32) if ssm else None
    rstd = S.sb("B_rstd", [128, T], F32)
    och = [S.sb("B_och%d" % i, [128, T], F32) for i in range(2)]
    t1 = [S.sb("B_t1_%d" % i, [128, T], F32) for i in range(2)]
    t2 = [S.sb("B_t2_%d" % i, [128, T], F32) for i in range(2)]
    sq = [S.sb("B_sq%d" % i, [128, T], BF16) for i in range(2)]
    wp = [S.sb("B_wp%d" % i, [128, nP, 128], BF16) for i in range(2)]
    fo = [S.sb("B_fo%d" % i, [128, T], F32) for i in range(2)] if final else None
    b_h = [S.buf("h%d" % i) for i in range(nK)]
    b_xa = [S.buf("xa%d" % i) for i in range(nK)]
    b_xb = [S.buf("xb%d" % i) for i in range(nK)]
    b_xc = [S.buf("xc%d" % i) for i in range(nK)]
    b_pt = S.buf("pt")
    b_const, b_rstd = S.buf("const"), S.buf("rstd")
    b_och = [S.buf("och%d" % i) for i in range(2)]
    b_t1 = [S.buf("t1_%d" % i) for i in range(2)]
    b_t2 = [S.buf("t2_%d" % i) for i in range(2)]
    b_sq = [S.buf("sq%d" % i) for i in range(2)]
    b_wp = [S.buf("wp%d" % i) for i in range(2)]
    b_fo = [S.buf("fo%d" % i) for i in range(2)]
    b_hd, b_hn = S.buf("hdram"), S.buf("hndram")
    pn = S.ps("B_pn")
    b_pn = S.buf("pn", True)
    pp = [S.ps("B_pp%d" % i) for i in range(2)]
    b_pp = [S.buf("pp%d" % i, True) for i in range(2)]
    G = Gemm(S, nK, nslots=3, nps=4 if ssm else 3, tag="B")

    S.op("pool", lambda e: e.memset(ones[:], 1.0), writes=[b_const])
    S.dma("sp", g2[:], d["g2"], b_const, writes=[b_const])
    S.dma("sp", gn[:], d["gn"], b_const, writes=[b_const])
    if ssm:
        S.dma("sp", bglu[:], d["bglu"], b_const, writes=[b_const])

    cnt = {"o": 0, "t": 0, "wp": 0}

    def norm_finish(b_src):
        S.op("dve", lambda e: e.tensor_scalar(rstd[:], pn[:, 0:T], 1.0 / D, 1e-6, ALU.mult, ALU.add),
             reads=[b_pn], writes=[b_rstd])
        S.op("act", lambda e: e.activation(out=rstd[:], in_=rstd[:], func=AF.Sqrt),
             reads=[b_rstd], writes=[b_rstd])
        S.op("dve", lambda e: e.reciprocal(rstd[:], rstd[:]), reads=[b_rstd], writes=[b_rstd])

    def sq_accum(nb):
        i = cnt["t"] % 2
        S.op("act", lambda e: e.activation(out=sq[i][:], in_=h[:, nb, :], func=AF.Square),
             reads=[b_h[nb]], writes=[b_sq[i]])
        S.op("pe", lambda e: e.matmul(pn[:, 0:T], ones[:], sq[i][:], start=(nb == 0), stop=(nb == nK - 1)),
             reads=[b_sq[i], b_const], writes=[b_pn])

    for ti in range(nt):
        ts = slice(ti * T, (ti + 1) * T)
        S.dma("sp", h[:], d["hT"][:, ts].rearrange("(c p) t -> p c t", p=128), b_h[0],
              reads=[b_hd], writes=b_h)
        S.dma("sp", xa[:], d["hnT"][:, ts].rearrange("(c p) t -> p c t", p=128), b_xa[0], writes=b_xa)
        S.dma("pool", pt[:], d["pT"][:, ts].rearrange("(c p) t -> p c t", p=128), b_pt, writes=[b_pt])
        if ssm:
            S.dma("pool", xc[:], d["oT"][:, ts].rearrange("(c p) t -> p c t", p=128), b_xc[0], writes=b_xc)

        def epi1(nb, pb, pap):
            i = cnt["o"] % 2
            cnt["o"] += 1
            S.dma("sp", och[i][:], d["oT"][nb * 128:(nb + 1) * 128, ts], b_och[i], writes=[b_och[i]])
            j = cnt["t"] % 2
            cnt["t"] += 1
            S.op("act", lambda e: e.activation(out=t1[j][:], in_=pap, func=AF.Silu), reads=[pb], writes=[b_t1[j]])
            S.op("dve", lambda e: e.tensor_tensor(xb[:, nb, :], t1[j][:], och[i][:], ALU.mult),
                 reads=[b_t1[j], b_och[i]], writes=[b_xb[nb]])

        def epi1_ssm_glu(nb, pb, pap):
            j = cnt["t"] % 2
            S.op("act", lambda e: e.activation(out=t2[j][:], in_=pap, func=AF.Sigmoid, bias=bglu[:, nb:nb + 1]),
                 reads=[pb, b_const], writes=[b_t2[j]])

        def epi1_ssm_z(nb, pb, pap):
            i = cnt["o"] % 2
            cnt["o"] += 1
            S.dma("sp", och[i][:], d["oT"][nb * 128:(nb + 1) * 128, ts], b_och[i], writes=[b_och[i]])
            j = cnt["t"] % 2
            cnt["t"] += 1
            S.op("act", lambda e: e.activation(out=t1[j][:], in_=pap, func=AF.Silu), reads=[pb], writes=[b_t1[j]])
            S.op("dve", lambda e: e.tensor_tensor(t1[j][:], t1[j][:], t2[j][:], ALU.mult),
                 reads=[b_t1[j], b_t2[j]], writes=[b_t1[j]])
            S.op("dve", lambda e: e.tensor_tensor(xb[:, nb, :], t1[j][:], och[i][:], ALU.mult),
                 reads=[b_t1[j], b_och[i]], writes=[b_xb[nb]])

        if not ssm:
            G.run(lambda kc: xa[:, kc, :], b_xa, lambda nb: d["Wz"][nb], nK, T, epi1)
        else:
            for nb in range(nK):
                G.run(lambda kc: xc[:, kc, :], b_xc, lambda _, nb=nb: d["Wglu"][nb], 1, T,
                      lambda _, pb, pap, nb=nb: epi1_ssm_glu(nb, pb, pap))
                G.run(lambda kc: xa[:, kc, :], b_xa, lambda _, nb=nb: d["Wz"][nb], 1, T,
                      lambda _, pb, pap, nb=nb: epi1_ssm_z(nb, pb, pap))

        def epi2(nb, pb, pap):
            S.op("dve", lambda e: e.tensor_tensor(h[:, nb, :], h[:, nb, :], pap, ALU.add),
                 reads=[pb, b_h[nb]], writes=[b_h[nb]])
            sq_accum(nb)
            cnt["t"] += 1

        G.run(lambda kc: xb[:, kc, :], b_xb, lambda nb: d["Wout"][nb], nK, T, epi2)
        norm_finish(None)
        for nb in range(nK):
            S.op("dve", lambda e, nb=nb: e.scalar_tensor_tensor(xa[:, nb, :], h[:, nb, :], g2[:, nb:nb + 1], rstd[:],
                                                                 ALU.mult, ALU.mult),
                 reads=[b_h[nb], b_rstd, b_const], writes=[b_xa[nb]])

        def epi3(nb, pb, pap):
            w = cnt["wp"] % 2
            cnt["wp"] += 1
            S.dma("pool", wp[w][:], d["Wp"][nb], b_wp[w], writes=[b_wp[w]])
            for kc in range(nP):
                S.op("pe", lambda e, kc=kc: e.matmul(pp[w][:, 0:T], wp[w][:, kc, :], pt[:, kc, :],
                                                     start=(kc == 0), stop=(kc == nP - 1)),
                     reads=[b_wp[w], b_pt], writes=[b_pp[w]])
            j = cnt["t"] % 2
            S.op("act", lambda e: e.activation(out=t1[j][:], in_=pap, func=AF.Sigmoid), reads=[pb], writes=[b_t1[j]])
            S.op("dve", lambda e: e.tensor_tensor(t1[j][:], t1[j][:], pp[w][:, 0:T], ALU.mult),
                 reads=[b_t1[j], b_pp[w]], writes=[b_t1[j]])
            S.op("pool", lambda e: e.tensor_tensor(h[:, nb, :], h[:, nb, :], t1[j][:], ALU.add),
                 reads=[b_t1[j], b_h[nb]], writes=[b_h[nb]])
            sq_accum(nb)
            cnt["t"] += 1

        G.run(lambda kc: xa[:, kc, :], b_xa, lambda nb: d["Wg"][nb], nK, T, epi3)
        norm_finish(None)
        if not final:
            for nb in range(nK):
                S.op("dve", lambda e, nb=nb: e.scalar_tensor_tensor(xb[:, nb, :], h[:, nb, :], gn[:, nb:nb + 1], rstd[:],
                                                                     ALU.mult, ALU.mult),
                     reads=[b_h[nb], b_rstd, b_const], writes=[b_xb[nb]])
            S.dma("sp", d["hnT_out"][:, ts].rearrange("(c p) t -> p c t", p=128), xb[:], b_xb[0],
                  reads=b_xb, writes=[b_hn])
            S.dma("sp", d["hT_out"][:, ts].rearrange("(c p) t -> p c t", p=128), h[:], b_h[0],
                  reads=b_h, writes=[b_hd])
        else:
            for nb in range(nK):
                i = nb % 2
                S.op("dve", lambda e, nb=nb, i=i: e.scalar_tensor_tensor(fo[i][:], h[:, nb, :], gn[:, nb:nb + 1], rstd[:],
                                                                          ALU.mult, ALU.mult),
                     reads=[b_h[nb], b_rstd, b_const], writes=[b_fo[i]])
                S.dma("sp", d["out"][nb * 128:(nb + 1) * 128, ts], fo[i][:], b_fo[i], reads=[b_fo[i]], writes=[b_hn])
    return [b_hd, b_hn]


def emit_N(S, d, D, TT, T):
    nK = D // 128
    h = S.sb("N_h", [128, nK, T], F32)
    xo = S.sb("N_xo", [128, nK, T], BF16)
    sq = [S.sb("N_sq%d" % i, [128, T], BF16) for i in range(2)]
    ones = S.sb("N_ones", [128, 128], BF16)
    g = S.sb("N_g", [128, nK], F32)
    rstd = S.sb("N_rstd", [128, T], F32)
    pn = S.ps("N_pn")
    b_h, b_xo, b_c, b_r, b_out = (S.buf() for _ in range(5))
    b_pn = S.buf("Npn", True)
    b_sq = [S.buf(), S.buf()]
    S.op("pool", lambda e: e.memset(ones[:], 1.0), writes=[b_c])
    S.dma("sp", g[:], d["g"], b_c, writes=[b_c])
    for ti in range(TT // T):
        ts = slice(ti * T, (ti + 1) * T)
        S.dma("sp", h[:], d["hT"][:, ts].rearrange("(c p) t -> p c t", p=128), b_h, writes=[b_h])
        for nb in range(nK):
            i = nb % 2
            S.op("act", lambda e, nb=nb, i=i: e.activation(out=sq[i][:], in_=h[:, nb, :], func=AF.Square),
                 reads=[b_h], writes=[b_sq[i]])
            S.op("pe", lambda e, nb=nb, i=i: e.matmul(pn[:, 0:T], ones[:], sq[i][:], start=(nb == 0), stop=(nb == nK - 1)),
                 reads=[b_sq[i], b_c], writes=[b_pn])
        S.op("dve", lambda e: e.tensor_scalar(rstd[:], pn[:, 0:T], 1.0 / D, 1e-6, ALU.mult, ALU.add),
             reads=[b_pn], writes=[b_r])
        S.op("act", lambda e: e.activation(out=rstd[:], in_=rstd[:], func=AF.Sqrt), reads=[b_r], writes=[b_r])
        S.op("dve", lambda e: e.reciprocal(rstd[:], rstd[:]), reads=[b_r], writes=[b_r])
        for nb in range(nK):
            S.op("dve", lambda e, nb=nb: e.scalar_tensor_tensor(xo[:, nb, :], h[:, nb, :], g[:, nb:nb + 1], rstd[:],
                                                                 ALU.mult, ALU.mult),
                 reads=[b_h, b_r, b_c], writes=[b_xo])
        S.dma("sp", d["hnT_out"][:, ts].rearrange("(c p) t -> p c t", p=128), xo[:], b_xo, reads=[b_xo], writes=[b_out])
    return [b_out]


def emit_Afox(S, d, D, NB, SL, NH):
    nc = S.nc
    nK = D // 128
    T = 512
    nseg = SL // T
    ntt = NB * nseg
    NP = NH * ntt
    assert NP <= 128
    nblk = SL // 128
    scale = 128 ** -0.5
    qkv_d = nc.dram_tensor("fox_qkv", [3 * NH, 128, NB * SL], BF16).ap()
    lf_d = nc.dram_tensor("fox_lf", [NH, ntt, T], F32).ap()
    cum_d = nc.dram_tensor("fox_cum", [NH, ntt, T], F32).ap()
    b_qkv, b_lf, b_cum, b_od = S.buf("qkv_d"), S.buf("lf_d"), S.buf("cum_d"), S.buf("oT_d")

    xa = S.sb("A_xa", [128, nK, T], BF16)
    b_xa = [S.buf("A_xa%d" % i) for i in range(nK)]
    st = [S.sb("A_st%d" % i, [128, T], BF16) for i in range(3)]
    b_st = [S.buf("A_st%d" % i) for i in range(3)]
    lft = [S.sb("A_lf%d" % i, [NH, T], F32) for i in range(2)]
    b_lft = [S.buf("A_lft%d" % i) for i in range(2)]
    negb = S.sb("A_negb", [NH, 1], F32)
    b_c = S.buf("A_const")
    S.dma("sp", negb[:], d["bf"], b_c, writes=[b_c])
    S.op("dve", lambda e: e.tensor_scalar(negb[:], negb[:], -1.0, None, ALU.mult), reads=[b_c], writes=[b_c])
    G = Gemm(S, nK, nslots=3, nps=2, tag="A")
    cnt = {"st": 0, "lf": 0}
    for tt in range(ntt):
        ts = slice(tt * T, (tt + 1) * T)
        S.dma("sp", xa[:], d["hnT"][:, ts].rearrange("(c p) t -> p c t", p=128), b_xa[0], writes=b_xa)

        def epi(nb, pb, pap, tt=tt, ts=ts):
            if nb < 3 * NH:
                i = cnt["st"] % 3
                cnt["st"] += 1
                if i % 2 == 0:
                    S.op("act", lambda e: e.activation(out=st[i][:], in_=pap, func=AF.Copy), reads=[pb], writes=[b_st[i]])
                else:
                    S.op("dve", lambda e: e.tensor_copy(st[i][:], pap), reads=[pb], writes=[b_st[i]])
                S.dma("sp", qkv_d[nb, :, ts], st[i][:], b_st[i], reads=[b_st[i]], writes=[b_qkv])
            else:
                i = cnt["lf"] % 2
                cnt["lf"] += 1
                S.op("act", lambda e: e.activation(out=lft[i][:], in_=pap[0:NH, :], func=AF.Exp, scale=-1.0,
                                                   bias=negb[:, 0:1]), reads=[pb, b_c], writes=[b_lft[i]])
                S.op("act", lambda e: e.activation(out=lft[i][:], in_=lft[i][:], func=AF.Ln, bias=1.0),
                     reads=[b_lft[i]], writes=[b_lft[i]])
                S.dma("sp", lf_d[:, tt, :], lft[i][:], b_lft[i], reads=[b_lft[i]], writes=[b_lf])

        G.run(lambda kc: xa[:, kc, :], b_xa, lambda nb: d["W"][nb], 3 * NH + 1, T, epi)

    LF = S.sb("A_LF", [NP, T], F32)
    onesf = S.sb("A_onesf", [NP, T], F32)
    mtri = S.sb("A_mtri", [NP, NP], F32)
    tot = S.sb("A_tot", [NP, 1], F32)
    off = S.sb("A_off", [NP, 1], F32)
    b_LF, b_m = S.buf("LF"), S.buf("m")
    ptr = [S.ps("A_ptr%d" % i) for i in range(2)]
    b_ptr = [S.buf("A_ptr%d" % i, True) for i in range(2)]
    S.dma("sp", LF[:], lf_d.rearrange("h t f -> (h t) f"), b_LF, reads=[b_lf], writes=[b_LF])
    S.dma("sp", mtri[:], d["mtri"], b_m, writes=[b_m])
    S.op("pool", lambda e: e.memset(onesf[:], 1.0), writes=[b_m])
    LC = S.sb("A_LC", [NP, T], F32)
    b_LC = S.buf("LC")
    S.op("dve", lambda e: e.tensor_tensor_scan(LC[:], onesf[:], LF[:], 0.0, ALU.mult, ALU.subtract),
         reads=[b_LF, b_m], writes=[b_LC])
    S.op("dve", lambda e: e.tensor_copy(tot[:], LC[:, T - 1:T]), reads=[b_LC], writes=[b_m])
    S.op("pe", lambda e: e.matmul(ptr[0][0:NP, 0:1], mtri[:], tot[:], start=True, stop=True),
         reads=[b_m], writes=[b_ptr[0]])
    S.op("dve", lambda e: e.tensor_copy(off[:], ptr[0][0:NP, 0:1]), reads=[b_ptr[0]], writes=[b_m])
    S.op("dve", lambda e: e.tensor_scalar(LC[:], LC[:], off[:, 0:1], None, ALU.add), reads=[b_LC, b_m], writes=[b_LC])
    S.dma("sp", cum_d.rearrange("h t f -> (h t) f"), LC[:], b_LC, reads=[b_LC], writes=[b_cum])

    QT = S.sb("A_QT", [128, SL], BF16)
    KT = S.sb("A_KT", [128, SL], BF16)
    VT = S.sb("A_VT", [128, SL], BF16)
    VP = S.sb("A_VP", [128, nblk, 132], BF16)
    CB = S.sb("A_CB", [128, SL], F32)
    crow = S.sb("A_crow", [nblk, 128], F32)
    negcs = S.sb("A_negcs", [128, nblk], F32)
    kmax = S.sb("A_kmax", [128, 1], F32)
    kmx = S.sb("A_kmx", [128, SL // T], F32)
    identb = S.sb("A_identb", [128, 128], BF16)
    identf = S.sb("A_identf", [128, 128], F32)
    onesb = S.sb("A_onesb", [128, 128], BF16)
    tri = S.sb("A_tri", [128, 128], F32)
    sqt = [S.sb("A_sqt%d" % i, [128, T], BF16) for i in range(2)]
    tmp = [S.sb("A_tmp%d" % i, [128, T], F32) for i in range(3)]
    PT = [S.sb("A_PT%d" % i, [128, T], BF16) for i in range(3)]
    rinv = S.sb("A_rinv", [128, 4], F32)
    ot = [S.sb("A_ot%d" % i, [128, 128], F32) for i in range(2)]
    oT = [S.sb("A_oT%d" % i, [128, T], F32) for i in range(2)]
    b_QT, b_KT, b_VT, b_VP, b_CB, b_ncs, b_km = (S.buf(n) for n in ("QT", "KT", "VT", "VP", "CB", "ncs", "km"))
    b_crow = S.buf("crow")
    b_sqt = [S.buf(), S.buf()]
    b_tmp = [S.buf() for _ in range(3)]
    b_PT = [S.buf() for _ in range(3)]
    b_rinv = S.buf("rinv")
    b_ot = [S.buf(), S.buf()]
    b_oT = [S.buf(), S.buf()]
    pS = G.pt
    b_pS = G.pb
    pO = [S.ps("A_pO%d" % i) for i in range(4)]
    b_pO = [S.buf("A_pO%d" % i, True) for i in range(4)]
    ptrb = ptr[1]
    pTb = pO[3][:].bitcast(BF16)
    b_pTb = b_pO[3]
    S.dma("sp", identb[:], d["identb"], b_c, writes=[b_c])
    S.dma("sp", identf[:], d["identf"], b_c, writes=[b_c])
    S.dma("sp", tri[:], d["tri"], b_c, writes=[b_c])
    S.op("pool", lambda e: e.memset(onesb[:], 1.0), writes=[b_c])
    S.op("pool", lambda e: e.memset(VP[:], 1.0), writes=[b_VP])
    k = {"s": 0, "t": 0, "p": 0, "o": 0, "oT": 0, "sq": 0}
    for b in range(NB):
        for hl in range(NH):
            tsl = slice(b * SL, (b + 1) * SL)
            S.dma("sp", QT[:], qkv_d[hl, :, tsl], b_QT, reads=[b_qkv], writes=[b_QT])
            S.dma("sp", KT[:], qkv_d[NH + hl, :, tsl], b_KT, reads=[b_qkv], writes=[b_KT])
            S.dma("sp", VT[:], qkv_d[2 * NH + hl, :, tsl], b_VT, reads=[b_qkv], writes=[b_VT])
            cum_row = cum_d[hl, b * nseg:(b + 1) * nseg, :]
            S.dma("sp", CB[:], cum_row.rearrange("s f -> (s f)").partition_broadcast(128), b_CB,
                  reads=[b_cum], writes=[b_CB])
            S.dma("sp", crow[:], cum_row.rearrange("s (j p) -> (s j) p", p=128), b_crow, reads=[b_cum], writes=[b_crow])
            S.op("pe", lambda e: e.transpose(ptr[1][:, 0:nblk], crow[:], identf[0:nblk, 0:nblk]),
                 reads=[b_crow, b_c], writes=[b_ptr[1]])
            S.op("dve", lambda e: e.tensor_scalar(negcs[:], ptr[1][:, 0:nblk], -1.0, None, ALU.mult),
                 reads=[b_ptr[1]], writes=[b_ncs])
            for j0 in range(0, nblk, 8):
                nj = min(8, nblk - j0)
                for jj in range(nj):
                    j = j0 + jj
                    S.op("pe", lambda e, j=j, jj=jj: e.transpose(pTb[:, jj * 128:(jj + 1) * 128], VT[:, j * 128:(j + 1) * 128], identb[:]),
                         reads=[b_VT, b_c], writes=[b_pTb])
                S.op("act", lambda e, j0=j0, nj=nj: e.activation(
                    out=VP[:, j0:j0 + nj, 0:128], in_=pTb[:, 0:nj * 128].rearrange("p (j d) -> p j d", d=128), func=AF.Copy),
                    reads=[b_pTb], writes=[b_VP])
            for c in range(SL // T):
                i = k["sq"] % 2
                k["sq"] += 1
                S.op("pool", lambda e, c=c, i=i: e.tensor_tensor(sqt[i][:], KT[:, c * T:(c + 1) * T], KT[:, c * T:(c + 1) * T], ALU.mult),
                     reads=[b_KT], writes=[b_sqt[i]])
                S.op("pe", lambda e, i=i: e.matmul(ptr[0][:, 0:T], onesb[:], sqt[i][:], start=True, stop=True),
                     reads=[b_sqt[i], b_c], writes=[b_ptr[0]])
                S.op("dve", lambda e, c=c: e.reduce_max(kmx[:, c:c + 1], ptr[0][:, 0:T], axis=AX.X),
                     reads=[b_ptr[0]], writes=[b_km])
            S.op("dve", lambda e: e.reduce_max(kmax[:], kmx[:], axis=AX.X), reads=[b_km], writes=[b_km])
            for c in range(SL // T):
                i = k["sq"] % 2
                k["sq"] += 1
                ti = k["t"] % 3
                k["t"] += 1
                S.op("pool", lambda e, c=c, i=i: e.tensor_tensor(sqt[i][:], QT[:, c * T:(c + 1) * T], QT[:, c * T:(c + 1) * T], ALU.mult),
                     reads=[b_QT], writes=[b_sqt[i]])
                S.op("pe", lambda e, i=i: e.matmul(ptr[0][:, 0:T], onesb[:], sqt[i][:], start=True, stop=True),
                     reads=[b_sqt[i], b_c], writes=[b_ptr[0]])
                S.op("act", lambda e, ti=ti: e.activation(out=tmp[ti][:], in_=ptr[0][:, 0:T], func=AF.Sqrt, scale=kmax[:, 0:1]),
                     reads=[b_ptr[0], b_km], writes=[b_tmp[ti]])
                S.op("dve", lambda e, c=c, ti=ti: e.scalar_tensor_tensor(CB[:, c * T:(c + 1) * T], tmp[ti][:], -scale,
                                                                         CB[:, c * T:(c + 1) * T], ALU.mult, ALU.add),
                     reads=[b_tmp[ti], b_CB], writes=[b_CB])
            def emit_qk(I, j):
                r = max(0, j - 4 * I)
                c0 = r * 128
                W_ = T - c0
                si = k["s"] % 2
                k["s"] += 1
                S.op("pe", lambda e: e.matmul(
                    pS[si][:, 0:W_], KT[:, j * 128:(j + 1) * 128], QT[:, I * T + c0:(I + 1) * T], start=True, stop=True),
                    reads=[b_KT, b_QT], writes=[b_pS[si]])
                return (I, j, r, c0, W_, si)

            tiles = [(I, j) for I in range(SL // T) for j in range(4 * I + 4)]
            nxt = emit_qk(*tiles[0])
            for n, (I, j) in enumerate(tiles):
                _, _, r, c0, W_, si = nxt
                if n + 1 < len(tiles):
                    nxt = emit_qk(*tiles[n + 1])
                ti = k["t"] % 3
                k["t"] += 1
                pi = k["p"] % 3
                k["p"] += 1
                S.op("dve", lambda e, I=I, c0=c0, W_=W_, si=si, ti=ti: e.scalar_tensor_tensor(
                    tmp[ti][:, 0:W_], pS[si][:, 0:W_], scale, CB[:, I * T + c0:(I + 1) * T], ALU.mult, ALU.add),
                    reads=[b_pS[si], b_CB], writes=[b_tmp[ti]])
                if j >= 4 * I:
                    S.op("pool", lambda e, ti=ti: e.tensor_tensor(tmp[ti][:, 0:128], tmp[ti][:, 0:128], tri[:], ALU.add),
                         reads=[b_tmp[ti], b_c], writes=[b_tmp[ti]])
                S.op("act", lambda e, j=j, W_=W_, ti=ti, pi=pi: e.activation(
                    out=PT[pi][:, 0:W_], in_=tmp[ti][:, 0:W_], func=AF.Exp, bias=negcs[:, j:j + 1]),
                    reads=[b_tmp[ti], b_ncs], writes=[b_PT[pi]])
                for u in range(r, 4):
                    S.op("pe", lambda e, j=j, u=u, r=r, pi=pi, I=I: e.matmul(
                        pO[u][:, 0:129], PT[pi][:, (u - r) * 128:(u - r + 1) * 128], VP[:, j, 0:129],
                        start=(j == 0), stop=(j == 4 * I + u)),
                        reads=[b_PT[pi], b_VP], writes=[b_pO[u]])
                if j != 4 * I + 3:
                    continue
                oi = k["oT"] % 2
                k["oT"] += 1
                for u in range(4):
                    S.op("dve", lambda e, u=u: e.reciprocal(rinv[:, u:u + 1], pO[u][:, 128:129]),
                         reads=[b_pO[u]], writes=[b_rinv])
                    o_i = k["o"] % 2
                    k["o"] += 1
                    S.op("act", lambda e, u=u, o_i=o_i: e.activation(out=ot[o_i][:], in_=pO[u][:, 0:128], func=AF.Copy,
                                                                   scale=rinv[:, u:u + 1]),
                         reads=[b_pO[u], b_rinv], writes=[b_ot[o_i]])
                    S.op("pe", lambda e, o_i=o_i: e.transpose(ptr[1][:, 0:128], ot[o_i][:], identf[:]),
                         reads=[b_ot[o_i], b_c], writes=[b_ptr[1]])
                    S.op("dve", lambda e, u=u, oi=oi: e.tensor_copy(oT[oi][:, u * 128:(u + 1) * 128], ptr[1][:, 0:128]),
                         reads=[b_ptr[1]], writes=[b_oT[oi]])
                S.dma("sp", d["oT"][hl * 128:(hl + 1) * 128, b * SL + I * T: b * SL + (I + 1) * T], oT[oi][:], b_oT[oi],
                      reads=[b_oT[oi]], writes=[b_od])
    return [b_od]


_STOP = [None]


class _Stop(Exception):
    pass


def _chk(k):
    if _STOP[0] == k:
        raise _Stop()


def emit_Agdn(S, d, D, NB, SL, NH):
    try:
        return _emit_Agdn(S, d, D, NB, SL, NH)
    except _Stop:
        return []


def _emit_Agdn(S, d, D, NB, SL, NH):
    nc = S.nc
    nK = D // 128
    T = 512
    C = 128
    ntt = NB * SL // T
    ngrp = SL // T
    R = NH * NB * SL // C
    qkv_d = nc.dram_tensor("gdn_qkv", [3 * NH, 128, NB * SL], F32).ap()
    bg_d = nc.dram_tensor("gdn_bg", [2, NH, NB * SL], F32).ap()
    gc_d = nc.dram_tensor("gdn_gc", [NH, NB * SL], F32).ap()
    b_qkv, b_bg, b_gc, b_od = S.buf("gqkv"), S.buf("gbg"), S.buf("ggc"), S.buf("goT")
    b_c = S.buf("gconst")

    def V(fn, r=(), w=()):
        S.op("dve", fn, reads=r, writes=w)

    def A_(fn, r=(), w=()):
        S.op("act", fn, reads=r, writes=w)

    def P_(fn, r=(), w=()):
        S.op("pool", fn, reads=r, writes=w)

    def M_(fn, r=(), w=()):
        S.op("pe", fn, reads=r, writes=w)

    xa = S.sb("G_xa", [128, nK, T], BF16)
    b_xa = [S.buf("G_xa%d" % i) for i in range(nK)]
    st = [S.sb("G_st%d" % i, [128, T], F32) for i in range(3)]
    b_st = [S.buf("G_st%d" % i) for i in range(3)]
    g8 = 2 * NH
    gt = [S.sb("G_gt%d" % i, [g8, T], F32) for i in range(2)]
    gs = [S.sb("G_gs%d" % i, [g8, T], F32) for i in range(2)]
    b_gt = [S.buf("G_gt%d" % i) for i in range(2)]
    b_gs = [S.buf("G_gs%d" % i) for i in range(2)]
    gbias = S.sb("G_gbias", [g8, 1], F32)
    gcoef = S.sb("G_gcoef", [g8, 1], F32)
    S.dma("sp", gbias[:], d["gbias"], b_c, writes=[b_c])
    S.dma("sp", gcoef[:], d["galog"], b_c, writes=[b_c])
    A_(lambda e: e.activation(out=gcoef[:], in_=gcoef[:], func=AF.Exp), [b_c], [b_c])
    V(lambda e: e.tensor_scalar(gcoef[:], gcoef[:], -1.0, None, ALU.mult), [b_c], [b_c])
    G = Gemm(S, nK, nslots=3, nps=2, tag="G")
    cnt = {"st": 0, "g": 0}
    for tt in range(ntt):
        ts = slice(tt * T, (tt + 1) * T)
        S.dma("sp", xa[:], d["hnT"][:, ts].rearrange("(c p) t -> p c t", p=128), b_xa[0], writes=b_xa)

        def epi(nb, pb, pap, ts=ts):
            if nb < 3 * NH:
                i = cnt["st"] % 3
                cnt["st"] += 1
                if i % 2 == 0:
                    A_(lambda e: e.activation(out=st[i][:], in_=pap, func=AF.Copy), [pb], [b_st[i]])
                else:
                    V(lambda e: e.tensor_copy(st[i][:], pap), [pb], [b_st[i]])
                S.dma("sp", qkv_d[nb, :, ts], st[i][:], b_st[i], reads=[b_st[i]], writes=[b_qkv])
            else:
                i = cnt["g"] % 2
                cnt["g"] += 1
                A_(lambda e: e.activation(out=gs[i][:], in_=pap[0:g8, :], func=AF.Sigmoid), [pb], [b_gs[i]])
                A_(lambda e: e.activation(out=gt[i][:], in_=pap[0:g8, :], func=AF.Exp, bias=gbias[:, 0:1]), [pb, b_c], [b_gt[i]])
                A_(lambda e: e.activation(out=gt[i][:], in_=gt[i][:], func=AF.Ln, bias=1.0), [b_gt[i]], [b_gt[i]])
                V(lambda e: e.tensor_scalar(gt[i][:], gt[i][:], gcoef[:, 0:1], None, ALU.mult), [b_gt[i], b_c], [b_gt[i]])
                S.dma("sp", bg_d[0, :, ts], gs[i][0:NH, :], b_gs[i], reads=[b_gs[i]], writes=[b_bg])
                S.dma("sp", bg_d[1, :, ts], gt[i][NH:g8, :], b_gt[i], reads=[b_gt[i]], writes=[b_bg])

        G.run(lambda kc: xa[:, kc, :], b_xa, lambda nb: d["W"][nb], 3 * NH + 1, T, epi)

    _chk(1)
    onesr = S.sb("G_onesr", [128, C], F32)
    P_(lambda e: e.memset(onesr[:], 1.0), [], [b_c])
    gr = [S.sb("G_gr%d" % i, [128, C], F32) for i in range(2)]
    gq = [S.sb("G_gq%d" % i, [128, C], F32) for i in range(2)]
    b_gr = [S.buf(), S.buf()]
    b_gq = [S.buf(), S.buf()]
    g_rows = bg_d[1].rearrange("h (n c) -> (h n) c", c=C)
    gc_rows = gc_d.rearrange("h (n c) -> (h n) c", c=C)
    for r0 in range(0, R, 128):
        nr = min(128, R - r0)
        i = (r0 // 128) % 2
        S.dma("sp", gr[i][0:nr, :], g_rows[r0:r0 + nr, :], b_gr[i], reads=[b_bg], writes=[b_gr[i]])
        V(lambda e, i=i, nr=nr: e.tensor_tensor_scan(gq[i][0:nr, :], onesr[0:nr, :], gr[i][0:nr, :], 0.0, ALU.mult, ALU.add),
          [b_gr[i], b_c], [b_gq[i]])
        S.dma("sp", gc_rows[r0:r0 + nr, :], gq[i][0:nr, :], b_gq[i], reads=[b_gq[i]], writes=[b_gc])

    _chk(2)
    def sbt(name, shape, dt=F32):
        return S.sb("G_" + name, shape, dt), S.buf("G_" + name)

    mstr, _ = sbt("mstr", [128, T]); mup, _ = sbt("mup", [128, T]); id4, _ = sbt("id4", [128, T])
    idf, _ = sbt("idf", [128, 128]); onesf, _ = sbt("onesf", [128, 128]); nwr, _ = sbt("nwr", [128, 128])
    for tle, key in ((mstr, "mstr4"), (mup, "mup4"), (id4, "id4"), (idf, "identf")):
        S.dma("sp", tle[:], d[key], b_c, writes=[b_c])
    S.dma("sp", nwr[:], d["nw"].partition_broadcast(128), b_c, writes=[b_c])
    P_(lambda e: e.memset(onesf[:], 1.0), [], [b_c])
    cw, b_cw = sbt("cw", [128, 3, 4])
    xin = [[sbt("x%d_%d" % (a, i), [128, T + 3]) for a in range(3)] for i in range(2)]
    GB = [sbt("GB%d" % i, [128, T]) for i in range(2)]
    rows = [sbt("rows%d" % i, [8, C]) for i in range(2)]
    yq, b_yq = sbt("yq", [128, T]); yk, b_yk = sbt("yk", [128, T]); yv, b_yv = sbt("yv", [128, T])
    sq, b_sq = sbt("sq", [128, T]); rr, b_rr = sbt("rr", [128, T])
    qn, b_qn = sbt("qn", [128, T]); kn, b_kn = sbt("kn", [128, T])
    kTM, b_kTM = sbt("kTM", [128, 4, C]); vTM, b_vTM = sbt("vTM", [128, 4, C])
    gcol, b_gcol = sbt("gcol", [128, 8])
    cols, b_cols = sbt("cols", [128, 24])
    bv, b_bv = sbt("bv", [128, 4, C]); kbg, b_kbg = sbt("kbg", [128, 4, C]); kdec, b_kdec = sbt("kdec", [128, 4, C])
    t1, b_t1 = sbt("t1", [128, T]); t2, b_t2 = sbt("t2", [128, T])
    E1, b_E1 = sbt("E1", [128, T]); E2, b_E2 = sbt("E2", [128, T]); eGB, b_eGB = sbt("eGB", [128, T])
    aT, b_aT = sbt("aT", [128, T]); qd, b_qd = sbt("qd", [128, T])
    Pm = [sbt("P%d" % i, [128, T]) for i in range(2)]
    PTm = [sbt("PT%d" % i, [128, T]) for i in range(2)]
    TTm = [sbt("TT%d" % i, [128, T]) for i in range(2)]
    u, b_u = sbt("u", [128, 4, C]); wT, b_wT = sbt("wT", [128, T])
    Sst, b_S = sbt("S", [128, C]); vnew, b_vnew = sbt("vnew", [128, C])
    ssq, b_ssq = sbt("ssq", [128, 1]); rstd, b_rstd = sbt("rstd", [128, 1]); junk, b_junk = sbt("junk", [128, C])
    oTM, b_oTM = sbt("oTM", [128, C])
    oFM = [sbt("oFM%d" % i, [128, T]) for i in range(2)]
    pk = [(S.ps("G_pk%d" % i), S.buf("G_pk%d" % i, True)) for i in range(6)]
    pk = [(G.pt[0], G.pb[0]), (G.pt[1], G.pb[1])] + pk
    (pA, b_pA), (pB, b_pB), (pC, b_pC), (pD, b_pD), (pE, b_pE), (pF, b_pF), (pG, b_pG), (pH, b_pH) = pk
    gi = 0
    for b in range(NB):
        for hl in range(NH):
            S.dma("sp", cw[:], d["cw"].rearrange("(a h) p j -> h p a j", h=NH)[hl], b_cw, writes=[b_cw])
            V(lambda e: e.memset(Sst[:], 0.0), [], [b_S])
            for g in range(ngrp):
                par = gi % 2
                gi += 1
                t0 = b * SL + g * T
                for a in range(3):
                    xt, xb_ = xin[par][a]
                    if g == 0:
                        P_(lambda e, xt=xt: e.memset(xt[:, 0:3], 0.0), [], [xb_])
                        S.dma("sp", xt[:, 3:T + 3], qkv_d[a * NH + hl, :, t0:t0 + T], xb_, reads=[b_qkv], writes=[xb_])
                    else:
                        S.dma("sp", xt[:], qkv_d[a * NH + hl, :, t0 - 3:t0 + T], xb_, reads=[b_qkv], writes=[xb_])
                GBt, b_GB = GB[par]
                S.dma("sp", GBt[:], gc_d[hl, t0:t0 + T].partition_broadcast(128), b_GB, reads=[b_gc], writes=[b_GB])
                rw, b_rw = rows[par]
                S.dma("sp", rw[0:4, :], gc_d[hl, t0:t0 + T].rearrange("(n c) -> n c", c=C), b_rw, reads=[b_gc], writes=[b_rw])
                S.dma("sp", rw[4:8, :], bg_d[0, hl, t0:t0 + T].rearrange("(n c) -> n c", c=C), b_rw, reads=[b_bg], writes=[b_rw])
                for a, (yt, yb) in enumerate(((yq, b_yq), (yk, b_yk), (yv, b_yv))):
                    xt, xb_ = xin[par][a]
                    V(lambda e, xt=xt, yt=yt, a=a: e.tensor_scalar(yt[:], xt[:, 3:T + 3], cw[:, a, 3:4], None, ALU.mult),
                      [xb_, b_cw], [yb])
                    for j in range(3):
                        V(lambda e, xt=xt, yt=yt, a=a, j=j: e.scalar_tensor_tensor(
                            yt[:], xt[:, j:j + T], cw[:, a, j:j + 1], yt[:], ALU.mult, ALU.add), [xb_, b_cw, yb], [yb])
                    A_(lambda e, yt=yt: e.activation(out=yt[:], in_=yt[:], func=AF.Silu), [yb], [yb])
                _chk(3)
                for (yt, yb, ot_, ob, sc, pp_, pb_) in ((yq, b_yq, qn, b_qn, 128 ** -0.5, pA, b_pA), (yk, b_yk, kn, b_kn, 1.0, pB, b_pB)):
                    P_(lambda e, yt=yt: e.tensor_tensor(sq[:], yt[:], yt[:], ALU.mult), [yb], [b_sq])
                    M_(lambda e, pp_=pp_: e.matmul(pp_[:, 0:T], onesf[:], sq[:], start=True, stop=True), [b_sq, b_c], [pb_])
                    V(lambda e, pp_=pp_: e.tensor_scalar(rr[:], pp_[:, 0:T], 1e-6, None, ALU.add), [pb_], [b_rr])
                    A_(lambda e: e.activation(out=rr[:], in_=rr[:], func=AF.Sqrt), [b_rr], [b_rr])
                    V(lambda e: e.reciprocal(rr[:], rr[:]), [b_rr], [b_rr])
                    V(lambda e, yt=yt, ot_=ot_, sc=sc: e.scalar_tensor_tensor(ot_[:], yt[:], sc, rr[:], ALU.mult, ALU.mult),
                      [yb, b_rr], [ob])
                _chk(4)
                M_(lambda e, rw=rw: e.transpose(pC[:, 0:8], rw[:], idf[0:8, 0:8]), [b_rw, b_c], [b_pC])
                V(lambda e: e.tensor_copy(gcol[:], pC[:, 0:8]), [b_pC], [b_gcol])
                V(lambda e: e.tensor_scalar(cols[:, 0:4], gcol[:, 4:8], -1.0, None, ALU.mult), [b_gcol], [b_cols])
                V(lambda e: e.tensor_scalar(cols[:, 4:8], gcol[:, 0:4], -1.0, None, ALU.mult), [b_gcol, b_cols], [b_cols])
                A_(lambda e: e.activation(out=cols[:, 8:12], in_=gcol[:, 0:4], func=AF.Exp), [b_gcol, b_cols], [b_cols])
                V(lambda e: e.tensor_tensor(cols[:, 8:12], cols[:, 8:12], gcol[:, 4:8], ALU.mult), [b_gcol, b_cols], [b_cols])
                V(lambda e, GBt=GBt: e.tensor_tensor(cols[:, 12:16], GBt[:].rearrange("p (n c) -> p n c", c=C)[:, :, C - 1], gcol[:, 0:4], ALU.subtract),
                  [b_GB, b_gcol, b_cols], [b_cols])
                A_(lambda e: e.activation(out=cols[:, 12:16], in_=cols[:, 12:16], func=AF.Exp), [b_cols], [b_cols])
                A_(lambda e, GBt=GBt: e.activation(out=cols[:, 16:20], in_=GBt[:].rearrange("p (n c) -> p n c", c=C)[:, :, C - 1], func=AF.Exp),
                   [b_GB, b_cols], [b_cols])
                _chk(5)
                for n in range(4):
                    M_(lambda e, n=n: e.transpose(pA[:, n * C:(n + 1) * C], kn[:, n * C:(n + 1) * C], idf[:]), [b_kn, b_c], [b_pA])
                    M_(lambda e, n=n: e.transpose(pB[:, n * C:(n + 1) * C], yv[:, n * C:(n + 1) * C], idf[:]), [b_yv, b_c], [b_pB])
                A_(lambda e: e.activation(out=kTM[:].rearrange("p n c -> p (n c)"), in_=pA[:, 0:T], func=AF.Copy), [b_pA], [b_kTM])
                A_(lambda e: e.activation(out=vTM[:].rearrange("p n c -> p (n c)"), in_=pB[:, 0:T], func=AF.Copy), [b_pB], [b_vTM])
                for n in range(4):
                    P_(lambda e, n=n: e.tensor_scalar(bv[:, n, :], vTM[:, n, :], gcol[:, 4 + n:5 + n], None, ALU.mult), [b_vTM, b_gcol], [b_bv])
                    P_(lambda e, n=n: e.tensor_scalar(kbg[:, n, :], kTM[:, n, :], cols[:, 8 + n:9 + n], None, ALU.mult), [b_kTM, b_cols], [b_kbg])
                    P_(lambda e, n=n: e.tensor_scalar(kdec[:, n, :], kTM[:, n, :], cols[:, 12 + n:13 + n], None, ALU.mult), [b_kTM, b_cols], [b_kdec])
                _chk(6)
                V(lambda e, GBt=GBt: e.scalar_tensor_tensor(t1[:], GBt[:], -1.0, mstr[:], ALU.mult, ALU.add), [b_GB, b_c], [b_t1])
                V(lambda e, GBt=GBt: e.tensor_tensor(t2[:], GBt[:], mup[:], ALU.add), [b_GB, b_c], [b_t2])
                for n in range(4):
                    cs = slice(n * C, (n + 1) * C)
                    A_(lambda e, n=n, cs=cs: e.activation(out=E1[:, cs], in_=t1[:, cs], func=AF.Exp, bias=gcol[:, n:n + 1]), [b_t1, b_gcol], [b_E1])
                    A_(lambda e, n=n, cs=cs: e.activation(out=E2[:, cs], in_=t2[:, cs], func=AF.Exp, bias=cols[:, 4 + n:5 + n]), [b_t2, b_cols], [b_E2])
                A_(lambda e, GBt=GBt: e.activation(out=eGB[:], in_=GBt[:], func=AF.Exp), [b_GB], [b_eGB])
                V(lambda e: e.tensor_tensor(qd[:], qn[:], eGB[:], ALU.mult), [b_qn, b_eGB], [b_qd])
                _chk(7)
                for n in range(4):
                    cs = slice(n * C, (n + 1) * C)
                    M_(lambda e, cs=cs: e.matmul(pC[:, cs], kn[:, cs], kn[:, cs], start=True, stop=True), [b_kn], [b_pC])
                    M_(lambda e, cs=cs: e.matmul(pD[:, cs], kn[:, cs], qn[:, cs], start=True, stop=True), [b_kn, b_qn], [b_pD])
                _chk(71)
                P0, b_P0 = Pm[0]
                PT0, b_PT0 = PTm[0]
                TT0, b_TT0 = TTm[0]
                for n in range(4):
                    cs = slice(n * C, (n + 1) * C)
                    V(lambda e, n=n, cs=cs: e.scalar_tensor_tensor(P0[:, cs], pC[:, cs], cols[:, n:n + 1], E1[:, cs], ALU.mult, ALU.mult),
                      [b_pC, b_cols, b_E1], [b_P0])
                V(lambda e: e.tensor_tensor(aT[:], pD[:, 0:T], E2[:], ALU.mult), [b_pD, b_E2], [b_aT])
                _chk(72)
                for n in range(4):
                    cs = slice(n * C, (n + 1) * C)
                    M_(lambda e, cs=cs: e.transpose(pE[:, cs], P0[:, cs], idf[:]), [b_P0, b_c], [b_pE])
                A_(lambda e: e.activation(out=PT0[:], in_=pE[:, 0:T], func=AF.Copy), [b_pE], [b_PT0])
                V(lambda e: e.tensor_tensor(TT0[:], pE[:, 0:T], id4[:], ALU.add), [b_pE, b_c], [b_TT0])
                _chk(73)
                cur = 0
                for step in range(6):
                    Pc, b_Pc = Pm[cur]; PTc, b_PTc = PTm[cur]; TTc, b_TTc = TTm[cur]
                    Pn, b_Pn = Pm[1 - cur]; PTn, b_PTn = PTm[1 - cur]; TTn, b_TTn = TTm[1 - cur]
                    last = step == 5
                    for n in range(4):
                        cs = slice(n * C, (n + 1) * C)
                        M_(lambda e, cs=cs, Pc=Pc, PTc=PTc: e.matmul(pC[:, cs], PTc[:, cs], Pc[:, cs], start=True, stop=True), [b_Pc, b_PTc], [b_pC])
                    A_(lambda e, Pn=Pn: e.activation(out=Pn[:], in_=pC[:, 0:T], func=AF.Copy), [b_pC], [b_Pn])
                    if not last:
                        for n in range(4):
                            cs = slice(n * C, (n + 1) * C)
                            M_(lambda e, cs=cs, Pc=Pc, PTc=PTc: e.matmul(pD[:, cs], Pc[:, cs], PTc[:, cs], start=True, stop=True), [b_Pc, b_PTc], [b_pD])
                        V(lambda e, PTn=PTn: e.tensor_copy(PTn[:], pD[:, 0:T]), [b_pD], [b_PTn])
                    for n in range(4):
                        cs = slice(n * C, (n + 1) * C)
                        M_(lambda e, cs=cs, Pn=Pn, TTc=TTc: e.matmul(pE[:, cs], Pn[:, cs], TTc[:, cs], start=True, stop=True), [b_Pn, b_TTc], [b_pE])
                    V(lambda e, TTn=TTn, TTc=TTc: e.tensor_tensor(TTn[:], pE[:, 0:T], TTc[:], ALU.add), [b_pE, b_TTc], [b_TTn])
                    cur = 1 - cur
                    _chk(74 + step)
                TTf, b_TTf = TTm[cur]
                _chk(8)
                for n in range(4):
                    cs = slice(n * C, (n + 1) * C)
                    M_(lambda e, n=n, cs=cs: e.matmul(pA[:, cs], TTf[:, cs], bv[:, n, :], start=True, stop=True), [b_TTf, b_bv], [b_pA])
                    M_(lambda e, n=n, cs=cs: e.matmul(pB[:, cs], kbg[:, n, :], TTf[:, cs], start=True, stop=True), [b_TTf, b_kbg], [b_pB])
                A_(lambda e: e.activation(out=u[:].rearrange("p n c -> p (n c)"), in_=pA[:, 0:T], func=AF.Copy), [b_pA], [b_u])
                V(lambda e: e.tensor_copy(wT[:], pB[:, 0:T]), [b_pB], [b_wT])
                _chk(9)
                oF, b_oF = oFM[par]
                for n in range(4):
                    cs = slice(n * C, (n + 1) * C)
                    M_(lambda e, cs=cs: e.matmul(pF[:, 0:C], wT[:, cs], Sst[:], start=True, stop=True), [b_wT, b_S], [b_pF])
                    V(lambda e, n=n: e.tensor_tensor(vnew[:], u[:, n, :], pF[:, 0:C], ALU.subtract), [b_u, b_pF], [b_vnew])
                    M_(lambda e, cs=cs: e.matmul(pG[:, 0:C], qd[:, cs], Sst[:], start=True, stop=False), [b_qd, b_S], [b_pG])
                    M_(lambda e, cs=cs: e.matmul(pG[:, 0:C], aT[:, cs], vnew[:], start=False, stop=True), [b_aT, b_vnew], [b_pG])
                    M_(lambda e, n=n: e.matmul(pF[:, C:2 * C], kdec[:, n, :], vnew[:], start=True, stop=True), [b_kdec, b_vnew], [b_pF])
                    V(lambda e, n=n: e.scalar_tensor_tensor(Sst[:], Sst[:], cols[:, 16 + n:17 + n], pF[:, C:2 * C], ALU.mult, ALU.add),
                      [b_S, b_cols, b_pF], [b_S])
                    A_(lambda e: e.activation(out=junk[:], in_=pG[:, 0:C], func=AF.Square, accum_out=ssq[:, 0:1]), [b_pG], [b_junk, b_ssq])
                    V(lambda e: e.tensor_scalar(rstd[:], ssq[:], 1.0 / C, 1e-6, ALU.mult, ALU.add), [b_ssq], [b_rstd])
                    A_(lambda e: e.activation(out=rstd[:], in_=rstd[:], func=AF.Sqrt), [b_rstd], [b_rstd])
                    V(lambda e: e.reciprocal(rstd[:], rstd[:]), [b_rstd], [b_rstd])
                    V(lambda e: e.scalar_tensor_tensor(oTM[:], pG[:, 0:C], rstd[:, 0:1], nwr[:], ALU.mult, ALU.mult),
                      [b_pG, b_rstd, b_c], [b_oTM])
                    M_(lambda e, cs=cs: e.transpose(pH[:, cs], oTM[:], idf[:]), [b_oTM, b_c], [b_pH])
                A_(lambda e, oF=oF: e.activation(out=oF[:], in_=pH[:, 0:T], func=AF.Copy), [b_pH], [b_oF])
                S.dma("sp", d["oT"][hl * 128:(hl + 1) * 128, t0:t0 + T], oF[:], b_oF, reads=[b_oF], writes=[b_od])
    return [b_od]


def emit_Assm(S, d, D, NB, SL, NPAIR):
    nc = S.nc
    nK = D // 128
    T = 512
    ntt = NB * SL // T
    nblk = NPAIR // 4
    TWO_PI = 2.0 * np.pi
    uT_d = nc.dram_tensor("ssm_uT", [nblk, 128, NB * SL], F32).ap()
    b_ud, b_od, b_c = S.buf("ssm_ud"), S.buf("ssm_od"), S.buf("ssm_c")

    def V(fn, r=(), w=()):
        S.op("dve", fn, reads=r, writes=w)

    def A_(fn, r=(), w=()):
        S.op("act", fn, reads=r, writes=w)

    def P_(fn, r=(), w=()):
        S.op("pool", fn, reads=r, writes=w)

    def M_(fn, r=(), w=()):
        S.op("pe", fn, reads=r, writes=w)

    xa = S.sb("S_xa", [128, nK, T], BF16)
    b_xa = [S.buf("S_xa%d" % i) for i in range(nK)]
    st = [S.sb("S_st%d" % i, [128, T], F32) for i in range(3)]
    b_st = [S.buf("S_st%d" % i) for i in range(3)]
    G = Gemm(S, nK, nslots=3, nps=2, tag="S")
    cnt = {"st": 0}
    for tt in range(ntt):
        ts = slice(tt * T, (tt + 1) * T)
        S.dma("sp", xa[:], d["hnT"][:, ts].rearrange("(c p) t -> p c t", p=128), b_xa[0], writes=b_xa)

        def epi(nb, pb, pap, ts=ts):
            i = cnt["st"] % 3
            cnt["st"] += 1
            if i % 2 == 0:
                A_(lambda e: e.activation(out=st[i][:], in_=pap, func=AF.Copy), [pb], [b_st[i]])
            else:
                V(lambda e: e.tensor_copy(st[i][:], pap), [pb], [b_st[i]])
            S.dma("sp", uT_d[nb, :, ts], st[i][:], b_st[i], reads=[b_st[i]], writes=[b_ud])

        G.run(lambda kc: xa[:, kc, :], b_xa, lambda nb: d["Wu"][nb], nblk, T, epi)

    def sbt(name, shape, dt=F32):
        return S.sb("S_" + name, shape, dt), S.buf("S_" + name)

    idf, _ = sbt("idf", [128, 128])
    S.dma("sp", idf[:], d["identf"], b_c, writes=[b_c])
    pr, b_pr = sbt("pr", [128, 40])
    pri, b_pri = S.sb("S_pri", [128, 1], mybir.dt.int32), None
    pin, b_pin = sbt("pin", [128, 3])
    pb4, b_pb4 = sbt("pb4", [128, 4, 16])
    bb, b_bb = sbt("bb", [128, 2, 16])
    blk, b_blk = sbt("blk", [128, 4, 32])
    BBt, b_BBt = sbt("BBt", [32, 2, 128])
    dsk, b_dsk = sbt("dsk", [32, 1]); dd, b_dd = sbt("dd", [32, 32])
    cosT, b_cos = sbt("cosT", [128, T]); sinT, b_sin = sbt("sinT", [128, T]); rT, b_rT = sbt("rT", [128, T])
    ut = [sbt("u%d" % i, [32, T]) for i in range(2)]
    z1, b_z1 = sbt("z1", [128, T]); z2, b_z2 = sbt("z2", [128, T])
    zre, b_zre = sbt("zre", [128, T]); zim, b_zim = sbt("zim", [128, T])
    wre, b_wre = sbt("wre", [128, T]); wim, b_wim = sbt("wim", [128, T])
    x1, b_x1 = sbt("x1", [128, T]); x2, b_x2 = sbt("x2", [128, T])
    xre, b_xre = sbt("xre", [128, T]); xim, b_xim = sbt("xim", [128, T])
    car, b_car = sbt("car", [128, 2])
    g1, b_g1 = sbt("g1", [32, T]); g2, b_g2 = sbt("g2", [32, T])
    yo = [sbt("yo%d" % i, [32, T]) for i in range(2)]
    pbu = [(S.ps("S_pbr%d" % i), S.buf("S_pbr%d" % i, True)) for i in range(2)]
    pbi = [(S.ps("S_pbi%d" % i), S.buf("S_pbi%d" % i, True)) for i in range(2)]
    py, b_py = S.ps("S_py"), S.buf("S_py", True)
    pt_, b_pt = G.pt[0], G.pb[0]
    c = lambda i: pr[:, i:i + 1]
    it = 0
    for j in range(NPAIR):
        S.dma("sp", pin[:, 0:1], d["lre"][j], b_pin, writes=[b_pin])
        S.dma("sp", pin[:, 1:2], d["lim"][j], b_pin, writes=[b_pin])
        S.dma("sp", pin[:, 2:3], d["lst"][j], b_pin, writes=[b_pin])
        for q, key in enumerate(("bre", "bim", "cre", "cim")):
            S.dma("sp", pb4[:, q, :], d[key][j], b_pb4, writes=[b_pb4])
        S.dma("sp", dsk[:], d["dsk"][j], b_dsk, writes=[b_dsk])
        R_, W_ = [b_pin, b_pr], [b_pr]
        A_(lambda e: e.activation(out=c(0), in_=pin[:, 2:3], func=AF.Exp), R_, W_)
        V(lambda e: e.tensor_tensor(c(1), pin[:, 0:1], c(0), ALU.mult), R_, W_)
        A_(lambda e: e.activation(out=c(2), in_=c(1), func=AF.Exp), R_, W_)
        V(lambda e: e.tensor_tensor(c(3), pin[:, 1:2], c(0), ALU.mult), R_, W_)
        V(lambda e: e.tensor_scalar(c(4), c(3), 1.0 / TWO_PI, None, ALU.mult), R_, W_)
        V(lambda e: e.tensor_copy(pri[:], c(4)), R_, W_)
        V(lambda e: e.tensor_copy(c(5), pri[:]), R_, W_)
        V(lambda e: e.scalar_tensor_tensor(c(6), c(5), -TWO_PI, c(3), ALU.mult, ALU.add), R_, W_)
        V(lambda e: e.tensor_scalar(c(7), c(6), 0.5, None, ALU.mult), R_, W_)
        V(lambda e: e.tensor_scalar(c(8), c(7), -1.0, None, ALU.mult), R_, W_)
        V(lambda e: e.tensor_tensor(c(8), c(8), c(7), ALU.max), R_, W_)
        V(lambda e: e.tensor_scalar(c(8), c(8), -1.0, np.pi / 2, ALU.mult, ALU.add), R_, W_)
        A_(lambda e: e.activation(out=c(9), in_=c(7), func=AF.Sin), R_, W_)
        A_(lambda e: e.activation(out=c(10), in_=c(8), func=AF.Sin), R_, W_)
        V(lambda e: e.tensor_tensor(c(11), c(9), c(10), ALU.mult), R_, W_)
        V(lambda e: e.tensor_scalar(c(11), c(11), 2.0, None, ALU.mult), R_, W_)
        V(lambda e: e.tensor_tensor(c(12), c(10), c(10), ALU.mult), R_, W_)
        V(lambda e: e.tensor_tensor(c(13), c(9), c(9), ALU.mult), R_, W_)
        V(lambda e: e.tensor_tensor(c(12), c(12), c(13), ALU.subtract), R_, W_)
        V(lambda e: e.tensor_tensor(c(14), c(2), c(12), ALU.mult), R_, W_)
        V(lambda e: e.tensor_tensor(c(15), c(2), c(11), ALU.mult), R_, W_)
        V(lambda e: e.tensor_tensor(c(16), pin[:, 0:1], pin[:, 0:1], ALU.mult), R_, W_)
        V(lambda e: e.tensor_tensor(c(17), pin[:, 1:2], pin[:, 1:2], ALU.mult), R_, W_)
        V(lambda e: e.tensor_tensor(c(16), c(16), c(17), ALU.add), R_, W_)
        V(lambda e: e.reciprocal(c(16), c(16)), R_, W_)
        V(lambda e: e.tensor_scalar(c(17), c(14), -1.0, None, ALU.add), R_, W_)
        V(lambda e: e.tensor_tensor(c(18), c(17), pin[:, 0:1], ALU.mult), R_, W_)
        V(lambda e: e.tensor_tensor(c(19), c(15), pin[:, 1:2], ALU.mult), R_, W_)
        V(lambda e: e.tensor_tensor(c(18), c(18), c(19), ALU.add), R_, W_)
        V(lambda e: e.tensor_tensor(c(18), c(18), c(16), ALU.mult), R_, W_)
        V(lambda e: e.tensor_tensor(c(19), c(15), pin[:, 0:1], ALU.mult), R_, W_)
        V(lambda e: e.tensor_tensor(c(20), c(17), pin[:, 1:2], ALU.mult), R_, W_)
        V(lambda e: e.tensor_tensor(c(19), c(19), c(20), ALU.subtract), R_, W_)
        V(lambda e: e.tensor_tensor(c(19), c(19), c(16), ALU.mult), R_, W_)
        V(lambda e: e.tensor_scalar(c(20), c(19), -1.0, None, ALU.mult), R_, W_)
        R2 = [b_pr, b_pb4, b_bb]
        V(lambda e: e.tensor_scalar(bb[:, 0, :], pb4[:, 0, :], c(18), None, ALU.mult), R2, [b_bb])
        V(lambda e: e.scalar_tensor_tensor(bb[:, 0, :], pb4[:, 1, :], c(20), bb[:, 0, :], ALU.mult, ALU.add), R2, [b_bb])
        V(lambda e: e.tensor_scalar(bb[:, 1, :], pb4[:, 1, :], c(18), None, ALU.mult), R2, [b_bb])
        V(lambda e: e.scalar_tensor_tensor(bb[:, 1, :], pb4[:, 0, :], c(19), bb[:, 1, :], ALU.mult, ALU.add), R2, [b_bb])
        R3 = [b_bb, b_pb4, b_blk]
        V(lambda e: e.memset(blk[:], 0.0), R3, [b_blk])
        for q, (src, sgn) in enumerate(((bb[:, 0, :], 1.0), (bb[:, 1, :], 1.0), (pb4[:, 2, :], 1.0), (pb4[:, 3, :], -1.0))):
            V(lambda e, q=q, src=src, sgn=sgn: e.tensor_scalar(blk[0:64, q, 0:16], src[0:64, :], sgn, None, ALU.mult), R3, [b_blk])
            V(lambda e, q=q, src=src, sgn=sgn: e.tensor_scalar(blk[64:128, q, 16:32], src[64:128, :], sgn, None, ALU.mult), R3, [b_blk])
        for q in range(2):
            M_(lambda e, q=q: e.transpose(pt_[0:32, q * 128:(q + 1) * 128], blk[:, q, :], idf[:]), [b_blk, b_c], [b_pt])
        V(lambda e: e.tensor_copy(BBt[:].rearrange("p q s -> p (q s)"), pt_[0:32, 0:256]), [b_pt], [b_BBt])
        V(lambda e: e.tensor_scalar(dd[:], idf[0:32, 0:32], dsk[:, 0:1], None, ALU.mult), [b_dsk, b_c], [b_dd])
        RT = [b_pr, b_cos, b_sin]
        V(lambda e: e.tensor_copy(cosT[:, 0:1], c(12)), RT, [b_cos])
        V(lambda e: e.tensor_copy(sinT[:, 0:1], c(11)), RT, [b_sin])
        m = 1
        while m < T:
            V(lambda e, m=m: e.tensor_scalar(c(21), sinT[:, m - 1:m], -1.0, None, ALU.mult), RT, [b_pr])
            V(lambda e, m=m: e.tensor_scalar(cosT[:, m:2 * m], cosT[:, 0:m], cosT[:, m - 1:m], None, ALU.mult), RT, [b_cos])
            V(lambda e, m=m: e.scalar_tensor_tensor(cosT[:, m:2 * m], sinT[:, 0:m], c(21), cosT[:, m:2 * m], ALU.mult, ALU.add), RT, [b_cos])
            V(lambda e, m=m: e.tensor_scalar(sinT[:, m:2 * m], sinT[:, 0:m], cosT[:, m - 1:m], None, ALU.mult), RT, [b_sin])
            V(lambda e, m=m: e.scalar_tensor_tensor(sinT[:, m:2 * m], cosT[:, 0:m], sinT[:, m - 1:m], sinT[:, m:2 * m], ALU.mult, ALU.add), RT, [b_sin])
            m *= 2
        V(lambda e: e.memset(rT[:], 1.0), [b_rT], [b_rT])
        V(lambda e: e.tensor_scalar(rT[:], rT[:], c(2), None, ALU.mult), [b_pr, b_rT], [b_rT])
        blkno, prow = j // 4, 32 * (j % 4)
        for b in range(NB):
            for ti in range(SL // T):
                t0 = b * SL + ti * T
                par = it % 2
                it += 1
                u_, b_u = ut[par]
                S.dma("sp", u_[:], uT_d[blkno, prow:prow + 32, t0:t0 + T], b_u, reads=[b_ud], writes=[b_u])
                (pre, b_pre), (pim, b_pim) = pbu[par], pbi[par]
                M_(lambda e, u_=u_, pre=pre: e.matmul(pre[:, 0:T], BBt[:, 0, :], u_[:], start=True, stop=True), [b_BBt, b_u], [b_pre])
                M_(lambda e, u_=u_, pim=pim: e.matmul(pim[:, 0:T], BBt[:, 1, :], u_[:], start=True, stop=True), [b_BBt, b_u], [b_pim])
                V(lambda e, pre=pre: e.tensor_tensor(z1[:], pre[:, 0:T], cosT[:], ALU.mult), [b_pre, b_cos], [b_z1])
                V(lambda e, pim=pim: e.tensor_tensor(z2[:], pim[:, 0:T], sinT[:], ALU.mult), [b_pim, b_sin], [b_z2])
                P_(lambda e: e.tensor_tensor(zre[:], z1[:], z2[:], ALU.add), [b_z1, b_z2], [b_zre])
                V(lambda e, pim=pim: e.tensor_tensor(z1[:], pim[:, 0:T], cosT[:], ALU.mult), [b_pim, b_cos, b_zre], [b_z1])
                V(lambda e, pre=pre: e.tensor_tensor(z2[:], pre[:, 0:T], sinT[:], ALU.mult), [b_pre, b_sin, b_zre], [b_z2])
                P_(lambda e: e.tensor_tensor(zim[:], z1[:], z2[:], ALU.subtract), [b_z1, b_z2], [b_zim])
                if ti == 0:
                    V(lambda e: e.memset(car[:], 0.0), [b_car], [b_car])
                V(lambda e: e.tensor_tensor_scan(wre[:], rT[:], zre[:], car[:, 0:1], ALU.mult, ALU.add), [b_rT, b_zre, b_car], [b_wre])
                V(lambda e: e.tensor_tensor_scan(wim[:], rT[:], zim[:], car[:, 1:2], ALU.mult, ALU.add), [b_rT, b_zim, b_car], [b_wim])
                P_(lambda e: e.tensor_tensor(x1[:], wre[:], cosT[:], ALU.mult), [b_wre, b_cos], [b_x1])
                P_(lambda e: e.tensor_tensor(x2[:], wim[:], sinT[:], ALU.mult), [b_wim, b_sin], [b_x2])
                P_(lambda e: e.tensor_tensor(xre[:], x1[:], x2[:], ALU.subtract), [b_x1, b_x2], [b_xre])
                P_(lambda e: e.tensor_tensor(x1[:], wim[:], cosT[:], ALU.mult), [b_wim, b_cos, b_xre], [b_x1])
                P_(lambda e: e.tensor_tensor(x2[:], wre[:], sinT[:], ALU.mult), [b_wre, b_sin, b_xre], [b_x2])
                P_(lambda e: e.tensor_tensor(xim[:], x1[:], x2[:], ALU.add), [b_x1, b_x2], [b_xim])
                V(lambda e: e.tensor_copy(car[:, 0:1], xre[:, T - 1:T]), [b_xre, b_car], [b_car])
                V(lambda e: e.tensor_copy(car[:, 1:2], xim[:, T - 1:T]), [b_xim, b_car], [b_car])
                M_(lambda e: e.matmul(py[0:32, 0:T], blk[:, 2, :], xre[:], start=True, stop=False), [b_blk, b_xre], [b_py])
                M_(lambda e: e.matmul(py[0:32, 0:T], blk[:, 3, :], xim[:], start=False, stop=False), [b_blk, b_xim], [b_py])
                M_(lambda e, u_=u_: e.matmul(py[0:32, 0:T], dd[:], u_[:], start=False, stop=True), [b_dd, b_u], [b_py])
                yo_, b_yo = yo[par]
                A_(lambda e: e.activation(out=g1[:], in_=py[0:32, 0:T], func=AF.Square), [b_py], [b_g1])
                V(lambda e: e.tensor_scalar(g1[:], g1[:], 0.044715, 1.0, ALU.mult, ALU.add), [b_g1], [b_g1])
                V(lambda e: e.tensor_tensor(g1[:], g1[:], py[0:32, 0:T], ALU.mult), [b_g1, b_py], [b_g1])
                A_(lambda e: e.activation(out=g2[:], in_=g1[:], func=AF.Tanh, scale=float(np.sqrt(2.0 / np.pi))), [b_g1], [b_g2])
                V(lambda e: e.tensor_scalar(g2[:], g2[:], 1.0, 0.5, ALU.add, ALU.mult), [b_g2], [b_g2])
                V(lambda e, yo_=yo_: e.tensor_tensor(yo_[:], g2[:], py[0:32, 0:T], ALU.mult), [b_g2, b_py], [b_yo])
                S.dma("sp", d["oT"][j * 32:(j + 1) * 32, t0:t0 + T], yo_[:], b_yo, reads=[b_yo], writes=[b_od])
    return [b_od]


D_MODEL, BATCH, SEQ, DEPTH, PLE = 4096, 2, 8192, 4, 256
NTOK = BATCH * SEQ
NCB = 2
TT = NTOK // NCB
HEADS, W_MIX = 32, 4096
HG = NCORES // BATCH
NH_LOC = HEADS // HG
CW = W_MIX // HG


def _new_nc():
    return bass.Bass("TRN2", target_bir_lowering=False)


def _decl(nc, ins, outs):
    d = {}
    for name, (shape, dt) in ins.items():
        d[name] = nc.dram_tensor(name, list(shape), dt, kind="ExternalInput").ap()
    for name, (shape, dt) in outs.items():
        d[name] = nc.dram_tensor(name, list(shape), dt, kind="ExternalOutput").ap()
    return d


def _build(emit, ins, outs, *args, **kw):
    nc = _new_nc()
    d = _decl(nc, ins, outs)
    with contextlib.ExitStack() as st:
        S = Sched(nc, st)
        fin = emit(S, d, *args, **kw)
        S.finish(fin)
    return nc


def _launch(nc, in_maps):
    res = run_bass_kernel_spmd(nc, in_maps, core_ids=list(range(len(in_maps))))
    return res.results


def colvec(g):
    return np.ascontiguousarray(np.asarray(g, np.float32).reshape(-1, 128).T)


def fox_consts(NH, NB, nseg):
    NP = NH * NB * nseg
    m = np.zeros((NP, NP), np.float32)
    for h in range(NH):
        for b in range(NB):
            base = (h * NB + b) * nseg
            for s1 in range(nseg):
                m[base + s1, base + s1 + 1:base + nseg] = 1.0
    p = np.arange(128)
    tri = np.where(p[None, :] >= p[:, None], 0.0, -1e30).astype(np.float32)
    return {"mtri": m, "tri": tri, "identf": np.eye(128, dtype=np.float32),
            "identb": np.eye(128).astype(ml_dtypes.bfloat16)}


def gdn_consts():
    p = np.arange(128)
    mstr = np.where(p[:, None] > p[None, :], 0.0, -1e30).astype(np.float32)
    mup = np.where(p[None, :] >= p[:, None], 0.0, -1e30).astype(np.float32)
    return {"mstr4": np.tile(mstr, (1, 4)), "mup4": np.tile(mup, (1, 4)),
            "id4": np.tile(np.eye(128, dtype=np.float32), (1, 4)), "identf": np.eye(128, dtype=np.float32)}


def kernel(x, p, norm_mix, fox_w_in, fox_b_f, fox_w_out, gdn_w_in, gdn_conv, gdn_a_log,
           gdn_dt_bias, gdn_norm, gdn_w_out, ssm_w_in, ssm_lam_re, ssm_lam_im, ssm_b_re,
           ssm_b_im, ssm_c_re, ssm_c_im, ssm_log_step, ssm_d, ssm_w_glu, ssm_b_glu, ssm_w_out,
           norm_ple, ple_w_proj, ple_w_gate, final_norm):
    f32 = np.float32
    D, nK = D_MODEL, D_MODEL // 128
    x = np.asarray(x, f32)
    p = np.asarray(p, f32)
    tok = [slice(c * TT, (c + 1) * TT) for c in range(NCB)]
    hT = np.ascontiguousarray(x.reshape(NTOK, D).T)
    hT_sh = [np.ascontiguousarray(hT[:, t]) for t in tok]
    del hT
    wblk = (nK, 128, nK, 128)

    ncN = _build(emit_N, {"hT": ((D, TT), F32), "g": ((128, nK), F32)}, {"hnT_out": ((D, TT), BF16)}, D, TT, 512)
    g0 = colvec(norm_mix[0])
    r = _launch(ncN, [{"hT": hT_sh[c], "g": g0} for c in range(NCB)])
    hn_sh = [r[c]["hnT_out"] for c in range(NCB)]

    b_ins = {"hT": ((D, TT), F32), "hnT": ((D, TT), BF16), "oT": ((D, TT), F32), "pT": ((PLE, TT), F32),
             "g2": ((128, nK), F32), "gn": ((128, nK), F32), "Wz": (wblk, F32), "Wout": (wblk, F32),
             "Wg": (wblk, F32), "Wp": ((nK, 128, PLE // 128, 128), F32)}
    progs = {}
    out = None
    for i in range(DEPTH):
        kind, j = i % 3, i // 3
        hn_full = np.concatenate(hn_sh, axis=1)
        hn_b = [np.ascontiguousarray(hn_full[:, bi * SEQ:(bi + 1) * SEQ]) for bi in range(BATCH)]
        if kind == 0:
            w_in = np.asarray(fox_w_in[j], f32)
            if "fox" not in progs:
                NP = NH_LOC * (SEQ // 512)
                progs["fox"] = _build(
                    emit_Afox,
                    {"hnT": ((D, SEQ), BF16), "W": ((3 * NH_LOC + 1, 128, nK, 128), F32), "bf": ((NH_LOC, 1), F32),
                     "mtri": ((NP, NP), F32), "tri": ((128, 128), F32), "identf": ((128, 128), F32),
                     "identb": ((128, 128), BF16)},
                    {"oT": ((NH_LOC * 128, SEQ), F32)}, D, 1, SEQ, NH_LOC)
            cst = fox_consts(NH_LOC, 1, SEQ // 512)
            maps = []
            for c in range(NCORES):
                bi, hg = c // HG, c % HG
                cs = slice(CW * hg, CW * (hg + 1))
                wf = np.zeros((D, 128), f32)
                wf[:, :NH_LOC] = w_in[:, 4 * W_MIX + NH_LOC * hg: 4 * W_MIX + NH_LOC * (hg + 1)]
                Wc = np.concatenate([w_in[:, 0 * W_MIX:1 * W_MIX][:, cs], w_in[:, 1 * W_MIX:2 * W_MIX][:, cs],
                                     w_in[:, 2 * W_MIX:3 * W_MIX][:, cs], wf], axis=1)
                mm = {"hnT": hn_b[bi], "W": blockw(Wc),
                      "bf": np.asarray(fox_b_f[j], f32)[NH_LOC * hg:NH_LOC * (hg + 1)].reshape(NH_LOC, 1)}
                mm.update(cst)
                maps.append(mm)
            r = _launch(progs["fox"], maps)
            Wz = blockw(w_in[:, 3 * W_MIX:4 * W_MIX])
            Wout = blockw(np.asarray(fox_w_out[j], f32))
        elif kind == 1:
            w_in = np.asarray(gdn_w_in[j], f32)
            if "gdn" not in progs:
                progs["gdn"] = _build(
                    emit_Agdn,
                    {"hnT": ((D, SEQ), BF16), "W": ((3 * NH_LOC + 1, 128, nK, 128), F32), "cw": ((3 * NH_LOC, 128, 4), F32),
                     "gbias": ((2 * NH_LOC, 1), F32), "galog": ((2 * NH_LOC, 1), F32), "nw": ((128,), F32),
                     "mstr4": ((128, 512), F32), "mup4": ((128, 512), F32), "id4": ((128, 512), F32),
                     "identf": ((128, 128), F32)},
                    {"oT": ((NH_LOC * 128, SEQ), F32)}, D, 1, SEQ, NH_LOC)
            cst = gdn_consts()
            conv = np.asarray(gdn_conv[j], f32)
            maps = []
            for c in range(NCORES):
                bi, hg = c // HG, c % HG
                cs = slice(CW * hg, CW * (hg + 1))
                hs = slice(NH_LOC * hg, NH_LOC * (hg + 1))
                wg = np.zeros((D, 128), f32)
                wg[:, :NH_LOC] = w_in[:, 4 * W_MIX:4 * W_MIX + HEADS][:, hs]
                wg[:, NH_LOC:2 * NH_LOC] = w_in[:, 4 * W_MIX + HEADS:4 * W_MIX + 2 * HEADS][:, hs]
                Wc = np.concatenate([w_in[:, 0 * W_MIX:1 * W_MIX][:, cs], w_in[:, 1 * W_MIX:2 * W_MIX][:, cs],
                                     w_in[:, 2 * W_MIX:3 * W_MIX][:, cs], wg], axis=1)
                cwc = np.concatenate([conv[:, 0 * W_MIX:1 * W_MIX][:, cs], conv[:, 1 * W_MIX:2 * W_MIX][:, cs],
                                      conv[:, 2 * W_MIX:3 * W_MIX][:, cs]], axis=1)
                zer = np.zeros(NH_LOC, f32)
                mm = {"hnT": hn_b[bi], "W": blockw(Wc), "cw": np.ascontiguousarray(cwc.T.reshape(3 * NH_LOC, 128, 4)),
                      "gbias": np.concatenate([zer, np.asarray(gdn_dt_bias[j], f32)[hs]]).reshape(-1, 1),
                      "galog": np.concatenate([zer, np.asarray(gdn_a_log[j], f32)[hs]]).reshape(-1, 1),
                      "nw": np.asarray(gdn_norm[j], f32)}
                mm.update(cst)
                maps.append(mm)
            r = _launch(progs["gdn"], maps)
            Wz = blockw(w_in[:, 3 * W_MIX:4 * W_MIX])
            Wout = blockw(np.asarray(gdn_w_out[j], f32))
        else:
            w_in = np.asarray(ssm_w_in[j], f32)
            NPAIR = CW // 32
            if "ssm" not in progs:
                ins = {"hnT": ((D, SEQ), BF16), "Wu": ((NPAIR // 4, 128, nK, 128), F32), "identf": ((128, 128), F32),
                       "dsk": ((NPAIR, 32, 1), F32)}
                for k in ("lre", "lim", "lst"):
                    ins[k] = ((NPAIR, 128, 1), F32)
                for k in ("bre", "bim", "cre", "cim"):
                    ins[k] = ((NPAIR, 128, 16), F32)
                progs["ssm"] = _build(emit_Assm, ins, {"oT": ((NPAIR * 32, SEQ), F32)}, D, 1, SEQ, NPAIR)
            maps = []
            for c in range(NCORES):
                bi, hg = c // HG, c % HG
                cs = slice(CW * hg, CW * (hg + 1))
                gs = slice(2 * NPAIR * hg, 2 * NPAIR * (hg + 1))
                mm = {"hnT": hn_b[bi], "Wu": blockw(w_in[:, :W_MIX][:, cs]), "identf": np.eye(128, dtype=f32),
                      "lre": np.ascontiguousarray(np.asarray(ssm_lam_re[j], f32)[gs]).reshape(NPAIR, 128, 1),
                      "lim": np.ascontiguousarray(np.asarray(ssm_lam_im[j], f32)[gs]).reshape(NPAIR, 128, 1),
                      "lst": np.repeat(np.asarray(ssm_log_step[j], f32)[gs], 64).reshape(NPAIR, 128, 1),
                      "bre": np.ascontiguousarray(np.asarray(ssm_b_re[j], f32)[gs]).reshape(NPAIR, 128, 16),
                      "bim": np.ascontiguousarray(np.asarray(ssm_b_im[j], f32)[gs]).reshape(NPAIR, 128, 16),
                      "cre": np.ascontiguousarray(np.asarray(ssm_c_re[j], f32)[gs].transpose(0, 2, 1)).reshape(NPAIR, 128, 16),
                      "cim": np.ascontiguousarray(np.asarray(ssm_c_im[j], f32)[gs].transpose(0, 2, 1)).reshape(NPAIR, 128, 16),
                      "dsk": np.ascontiguousarray(np.asarray(ssm_d[j], f32)[cs]).reshape(NPAIR, 32, 1)}
                maps.append(mm)
            r = _launch(progs["ssm"], maps)
            Wz = blockw(w_in[:, W_MIX:2 * W_MIX])
            Wout = blockw(np.asarray(ssm_w_out[j], f32))
        del hn_full, hn_b
        oT_full = np.concatenate([np.concatenate([r[bi * HG + hg]["oT"] for hg in range(HG)], axis=0)
                                  for bi in range(BATCH)], axis=1)
        final = i == DEPTH - 1
        ssm = kind == 2
        key = ("B", ssm, final)
        if key not in progs:
            ins = dict(b_ins)
            if ssm:
                ins["Wglu"] = (wblk, F32)
                ins["bglu"] = ((128, nK), F32)
            outs = {"out": ((D, TT), F32)} if final else {"hT_out": ((D, TT), F32), "hnT_out": ((D, TT), BF16)}
            progs[key] = _build(emit_B, ins, outs, D, TT, 512, ssm=ssm, final=final, PD=PLE)
        pT = np.ascontiguousarray(p[i].reshape(NTOK, PLE).T)
        shared = {"g2": colvec(norm_ple[i]), "gn": colvec(final_norm if final else norm_mix[i + 1]),
                  "Wz": Wz, "Wout": Wout, "Wg": blockw(np.asarray(ple_w_gate[i], f32)),
                  "Wp": blockw(np.asarray(ple_w_proj[i], f32))}
        if ssm:
            shared["Wglu"] = blockw(np.asarray(ssm_w_glu[j], f32))
            shared["bglu"] = colvec(ssm_b_glu[j])
        maps = []
        for c in range(NCB):
            mm = {"hT": hT_sh[c], "hnT": hn_sh[c], "oT": np.ascontiguousarray(oT_full[:, tok[c]]),
                  "pT": np.ascontiguousarray(pT[:, tok[c]])}
            mm.update(shared)
            maps.append(mm)
        del oT_full
        r = _launch(progs[key], maps)
        if final:
            outT = np.concatenate([r[c]["out"] for c in range(NCB)], axis=1)
            out = np.ascontiguousarray(outT.T).reshape(BATCH, SEQ, D)
        else:
            hT_sh = [r[c]["hT_out"] for c in range(NCB)]
            hn_sh = [r[c]["hnT_out"] for c in range(NCB)]
    return out.astype(np.float32)
```

```python
import contextlib
import numpy as np
import ml_dtypes
import concourse.bass as bass
import concourse.mybir as mybir
from concourse.bass_utils import run_bass_kernel_spmd

F32 = mybir.dt.float32
BF16 = mybir.dt.bfloat16
AF = mybir.ActivationFunctionType
ALU = mybir.AluOpType
AX = mybir.AxisListType
NCORES = 8


class Buf:
    __slots__ = ("name", "w", "r", "sem", "ndma", "excl")

    def __init__(self, name, excl=False):
        self.name = name
        self.excl = excl
        self.w = {}
        self.r = {}
        self.sem = None
        self.ndma = 0


def _merge(dst, src):
    for k, v in src.items():
        if dst.get(k, 0) < v:
            dst[k] = v


class Sched:
    ENG = ("pe", "act", "dve", "pool", "sp")

    def __init__(self, nc, stack):
        self.nc = nc
        self.stack = stack
        self.prog = {e: [] for e in self.ENG}
        self.count = {e: 0 for e in self.ENG}
        self.seen = {e: {} for e in self.ENG}
        self.sem = {}
        for e in ("pe", "act", "dve", "pool"):
            self.sem[e] = stack.enter_context(nc.semaphore("c_" + e))
        self.nbuf = 0
        self.slots = []

    def sb(self, name, shape, dt):
        return self.stack.enter_context(self.nc.sbuf_tensor(name, list(shape), dt))

    def ps(self, name, shape=(128, 512), dt=F32):
        return self.stack.enter_context(self.nc.psum_tensor(name, list(shape), dt))

    def buf(self, name=None, excl=False):
        self.nbuf += 1
        return Buf(name or "b%d" % self.nbuf, excl)

    def _waits(self, eng, deps, skip_self=False):
        out = []
        seen = self.seen[eng]
        for sem, v in deps.items():
            if skip_self and sem is self.sem.get(eng):
                continue
            if seen.get(sem, 0) < v:
                seen[sem] = v
                out.append((sem, v))
        return out

    def _deps(self, reads, writes):
        deps = {}
        for b in reads:
            _merge(deps, b.w)
            if b.excl:
                _merge(deps, b.r)
        for b in writes:
            _merge(deps, b.w)
            _merge(deps, b.r)
        return deps

    def op(self, eng, fn, reads=(), writes=()):
        deps = self._deps(reads, writes)
        waits = self._waits(eng, deps, skip_self=(eng == "pe"))
        self.count[eng] += 1
        tok = {self.sem[eng]: self.count[eng]}
        self.prog[eng].append((waits, fn, (self.sem[eng], 1)))
        for b in reads:
            _merge(b.r, tok)
        for b in writes:
            b.w = dict(tok)
            b.r = {}

    def dma(self, q, out_ap, in_ap, slot, reads=(), writes=(), **kw):
        if slot.sem is None:
            slot.sem = self.stack.enter_context(self.nc.semaphore("d_" + slot.name))
            self.slots.append(slot)
        deps = self._deps(reads, writes)
        waits = self._waits(q, deps)
        slot.ndma += 1
        tok = {slot.sem: 16 * slot.ndma}

        def fn(e, out_ap=out_ap, in_ap=in_ap, kw=kw):
            return e.dma_start(out=out_ap, in_=in_ap, **kw)

        self.prog[q].append((waits, fn, (slot.sem, 16)))
        for b in reads:
            _merge(b.r, tok)
        for b in writes:
            b.w = dict(tok)
            b.r = {}

    def finish(self, final_bufs):
        deps = {}
        for b in final_bufs:
            _merge(deps, b.w)
        for sl in self.slots:
            _merge(deps, {sl.sem: 16 * sl.ndma})
        fin = self._waits("sp", deps)
        self.prog["sp"].append((fin, None, None))
        nc = self.nc
        prog = self.prog

        def play(e, lst):
            for waits, fn, inc in lst:
                for sem, v in waits:
                    e.wait_ge(sem, v)
                if fn is not None:
                    ins = fn(e)
                    ins.then_inc(inc[0], inc[1])

        with nc.Block() as block:
            @block.tensor
            def _(e):
                play(e, prog["pe"])

            @block.scalar
            def _(e):
                play(e, prog["act"])

            @block.vector
            def _(e):
                play(e, prog["dve"])

            @block.gpsimd
            def _(e):
                play(e, prog["pool"])

            @block.sync
            def _(e):
                play(e, prog["sp"])


class Gemm:
    def __init__(self, S, nK, nslots=3, nps=2, tag="g"):
        self.S = S
        self.nK = nK
        self.wt = [S.sb("%s_w%d" % (tag, i), [128, nK, 128], BF16) for i in range(nslots)]
        self.wb = [S.buf("%s_wb%d" % (tag, i)) for i in range(nslots)]
        self.pt = [S.ps("%s_p%d" % (tag, i)) for i in range(nps)]
        self.pb = [S.buf("%s_pb%d" % (tag, i), True) for i in range(nps)]
        self.wi = 0
        self.pi = 0

    def precast(self, name, w_ap, nblk, nK=None):
        S = self.S
        nK = nK or self.nK
        wb16 = S.nc.dram_tensor(name, [nblk, 128, nK, 128], BF16).ap()
        buf = S.buf(name)
        for nb in range(nblk):
            S.dma("pool", wb16[nb], w_ap[nb], buf)
        buf.w = {buf.sem: 16 * buf.ndma}
        return wb16, buf

    def run(self, at, at_bufs, wsrc, nblocks, T, epi, nK=None, wq="sp", wbuf=None):
        S = self.S
        nK = nK or self.nK
        for nb in range(nblocks):
            wi = self.wi % len(self.wt)
            self.wi += 1
            wt, wb = self.wt[wi], self.wb[wi]
            S.dma(wq, wt[:, 0:nK, :], wsrc(nb), wb, reads=[wbuf] if wbuf is not None else [], writes=[wb])
            pi = self.pi % len(self.pt)
            self.pi += 1
            pt, pb = self.pt[pi], self.pb[pi]
            for kc in range(nK):
                def mm(e, kc=kc, wt=wt, pt=pt):
                    return e.matmul(pt[:, 0:T], wt[:, kc, :], at(kc),
                                    start=(kc == 0), stop=(kc == nK - 1))
                S.op("pe", mm, reads=[wb, at_bufs[kc]], writes=[pb])
            epi(nb, pb, pt[:, 0:T])


def blockw(W, nblk=None):
    K, N = W.shape
    return np.ascontiguousarray(
        W.reshape(K // 128, 128, N // 128, 128).transpose(2, 1, 0, 3))


def emit_B(S, d, D, TT, T, ssm=False, final=False, PD=256):
    nK = D // 128
    nP = PD // 128
    nt = TT // T
    h = S.sb("B_h", [128, nK, T], F32)
    xa = S.sb("B_xa", [128, nK, T], BF16)
    xb = S.sb("B_xb", [128, nK, T], BF16)
    xc = S.sb("B_xc", [128, nK, T], BF16) if ssm else None
    pt = S.sb("B_pT", [128, nP, T], BF16)
    ones = S.sb("B_ones", [128, 128], BF16)
    g2 = S.sb("B_g2", [128, nK], F32)
    gn = S.sb("B_gn", [128, nK], F32)
    bglu = S.sb("B_bglu", [128, nK], F32) if ssm else None
    rstd = S.sb("B_rstd", [128, T], F32)
    och = [S.sb("B_och%d" % i, [128, T], F32) for i in range(2)]
    t1 = [S.sb("B_t1_%d" % i, [128, T], F32) for i in range(2)]
    t2 = [S.sb("B_t2_%d" % i, [128, T], F32) for i in range(2)]
    sq = [S.sb("B_sq%d" % i, [128, T], BF16) for i in range(2)]
    wp = [S.sb("B_wp%d" % i, [128, nP, 128], BF16) for i in range(2)]
    fo = [S.sb("B_fo%d" % i, [128, T], F32) for i in range(2)] if final else None
    b_h = [S.buf("h%d" % i) for i in range(nK)]
    b_xa = [S.buf("xa%d" % i) for i in range(nK)]
    b_xb = [S.buf("xb%d" % i) for i in range(nK)]
    b_xc = [S.buf("xc%d" % i) for i in range(nK)]
    b_pt = S.buf("pt")
    b_const, b_rstd = S.buf("const"), S.buf("rstd")
    b_och = [S.buf("och%d" % i) for i in range(2)]
    b_t1 = [S.buf("t1_%d" % i) for i in range(2)]
    b_t2 = [S.buf("t2_%d" % i) for i in range(2)]
    b_sq = [S.buf("sq%d" % i) for i in range(2)]
    b_wp = [S.buf("wp%d" % i) for i in range(2)]
    b_fo = [S.buf("fo%d" % i) for i in range(2)]
    b_hd, b_hn = S.buf("hdram"), S.buf("hndram")
    pn = S.ps("B_pn")
    b_pn = S.buf("pn", True)
    pp = [S.ps("B_pp%d" % i) for i in range(2)]
    b_pp = [S.buf("pp%d" % i, True) for i in range(2)]
    G = Gemm(S, nK, nslots=3, nps=4 if ssm else 3, tag="B")
    Wz16, b_Wz = G.precast("B_Wz16", d["Wz"], nK)
    if ssm:
        Wglu16, b_Wglu = G.precast("B_Wglu16", d["Wglu"], nK)
    Wout16, b_Wout = G.precast("B_Wout16", d["Wout"], nK)
    Wg16, b_Wg = G.precast("B_Wg16", d["Wg"], nK)
    Wp16, b_Wp = G.precast("B_Wp16", d["Wp"], nK, nK=nP)

    S.op("pool", lambda e: e.memset(ones[:], 1.0), writes=[b_const])
    S.dma("pool", g2[:], d["g2"], b_const, writes=[b_const])
    S.dma("pool", gn[:], d["gn"], b_const, writes=[b_const])
    if ssm:
        S.dma("pool", bglu[:], d["bglu"], b_const, writes=[b_const])

    cnt = {"o": 0, "t": 0, "wp": 0}

    def norm_finish(b_src):
        S.op("dve", lambda e: e.tensor_scalar(rstd[:], pn[:, 0:T], 1.0 / D, 1e-6, ALU.mult, ALU.add),
             reads=[b_pn], writes=[b_rstd])
        S.op("act", lambda e: e.activation(out=rstd[:], in_=rstd[:], func=AF.Sqrt),
             reads=[b_rstd], writes=[b_rstd])
        S.op("dve", lambda e: e.reciprocal(rstd[:], rstd[:]), reads=[b_rstd], writes=[b_rstd])

    def sq_accum(nb):
        i = cnt["t"] % 2
        S.op("act", lambda e: e.activation(out=sq[i][:], in_=h[:, nb, :], func=AF.Square),
             reads=[b_h[nb]], writes=[b_sq[i]])
        S.op("pe", lambda e: e.matmul(pn[:, 0:T], ones[:], sq[i][:], start=(nb == 0), stop=(nb == nK - 1)),
             reads=[b_sq[i], b_const], writes=[b_pn])

    for ti in range(nt):
        ts = slice(ti * T, (ti + 1) * T)
        S.dma("pool", h[:], d["hT"][:, ts].rearrange("(c p) t -> p c t", p=128), b_h[0],
              reads=[b_hd], writes=b_h)
        S.dma("pool", xa[:], d["hnT"][:, ts].rearrange("(c p) t -> p c t", p=128), b_xa[0], writes=b_xa)
        S.dma("pool", pt[:], d["pT"][:, ts].rearrange("(c p) t -> p c t", p=128), b_pt, writes=[b_pt])
        if ssm:
            S.dma("pool", xc[:], d["oT"][:, ts].rearrange("(c p) t -> p c t", p=128), b_xc[0], writes=b_xc)

        def epi1(nb, pb, pap):
            i = cnt["o"] % 2
            cnt["o"] += 1
            S.dma("pool", och[i][:], d["oT"][nb * 128:(nb + 1) * 128, ts], b_och[i], writes=[b_och[i]])
            j = cnt["t"] % 2
            cnt["t"] += 1
            S.op("act", lambda e: e.activation(out=t1[j][:], in_=pap, func=AF.Silu), reads=[pb], writes=[b_t1[j]])
            S.op("dve", lambda e: e.tensor_tensor(xb[:, nb, :], t1[j][:], och[i][:], ALU.mult),
                 reads=[b_t1[j], b_och[i]], writes=[b_xb[nb]])

        def epi1_ssm_glu(nb, pb, pap):
            j = cnt["t"] % 2
            S.op("act", lambda e: e.activation(out=t2[j][:], in_=pap, func=AF.Sigmoid, bias=bglu[:, nb:nb + 1]),
                 reads=[pb, b_const], writes=[b_t2[j]])

        def epi1_ssm_z(nb, pb, pap):
            i = cnt["o"] % 2
            cnt["o"] += 1
            S.dma("pool", och[i][:], d["oT"][nb * 128:(nb + 1) * 128, ts], b_och[i], writes=[b_och[i]])
            j = cnt["t"] % 2
            cnt["t"] += 1
            S.op("act", lambda e: e.activation(out=t1[j][:], in_=pap, func=AF.Silu), reads=[pb], writes=[b_t1[j]])
            S.op("dve", lambda e: e.tensor_tensor(t1[j][:], t1[j][:], t2[j][:], ALU.mult),
                 reads=[b_t1[j], b_t2[j]], writes=[b_t1[j]])
            S.op("dve", lambda e: e.tensor_tensor(xb[:, nb, :], t1[j][:], och[i][:], ALU.mult),
                 reads=[b_t1[j], b_och[i]], writes=[b_xb[nb]])

        if not ssm:
            G.run(lambda kc: xa[:, kc, :], b_xa, lambda nb: Wz16[nb], nK, T, epi1, wbuf=b_Wz)
        else:
            for nb in range(nK):
                G.run(lambda kc: xc[:, kc, :], b_xc, lambda _, nb=nb: Wglu16[nb], 1, T,
                      lambda _, pb, pap, nb=nb: epi1_ssm_glu(nb, pb, pap), wbuf=b_Wglu)
                G.run(lambda kc: xa[:, kc, :], b_xa, lambda _, nb=nb: Wz16[nb], 1, T,
                      lambda _, pb, pap, nb=nb: epi1_ssm_z(nb, pb, pap), wbuf=b_Wz)

        def epi2(nb, pb, pap):
            S.op("dve", lambda e: e.tensor_tensor(h[:, nb, :], h[:, nb, :], pap, ALU.add),
                 reads=[pb, b_h[nb]], writes=[b_h[nb]])
            sq_accum(nb)
            cnt["t"] += 1

        G.run(lambda kc: xb[:, kc, :], b_xb, lambda nb: Wout16[nb], nK, T, epi2, wbuf=b_Wout)
        norm_finish(None)
        for nb in range(nK):
            S.op("dve", lambda e, nb=nb: e.scalar_tensor_tensor(xa[:, nb, :], h[:, nb, :], g2[:, nb:nb + 1], rstd[:],
                                                                 ALU.mult, ALU.mult),
                 reads=[b_h[nb], b_rstd, b_const], writes=[b_xa[nb]])

        def epi3(nb, pb, pap):
            w = cnt["wp"] % 2
            cnt["wp"] += 1
            S.dma("sp", wp[w][:], Wp16[nb], b_wp[w], reads=[b_Wp], writes=[b_wp[w]])
            for kc in range(nP):
                S.op("pe", lambda e, kc=kc: e.matmul(pp[w][:, 0:T], wp[w][:, kc, :], pt[:, kc, :],
                                                     start=(kc == 0), stop=(kc == nP - 1)),
                     reads=[b_wp[w], b_pt], writes=[b_pp[w]])
            j = cnt["t"] % 2
            S.op("act", lambda e: e.activation(out=t1[j][:], in_=pap, func=AF.Sigmoid), reads=[pb], writes=[b_t1[j]])
            S.op("dve", lambda e: e.tensor_tensor(t1[j][:], t1[j][:], pp[w][:, 0:T], ALU.mult),
                 reads=[b_t1[j], b_pp[w]], writes=[b_t1[j]])
            S.op("dve", lambda e: e.tensor_tensor(h[:, nb, :], h[:, nb, :], t1[j][:], ALU.add),
                 reads=[b_t1[j], b_h[nb]], writes=[b_h[nb]])
            sq_accum(nb)
            cnt["t"] += 1

        G.run(lambda kc: xa[:, kc, :], b_xa, lambda nb: Wg16[nb], nK, T, epi3, wbuf=b_Wg)
        norm_finish(None)
        if not final:
            for nb in range(nK):
                S.op("dve", lambda e, nb=nb: e.scalar_tensor_tensor(xb[:, nb, :], h[:, nb, :], gn[:, nb:nb + 1], rstd[:],
                                                                     ALU.mult, ALU.mult),
                     reads=[b_h[nb], b_rstd, b_const], writes=[b_xb[nb]])
            S.dma("pool", d["hnT_out"][:, ts].rearrange("(c p) t -> p c t", p=128), xb[:], b_xb[0],
                  reads=b_xb, writes=[b_hn])
            S.dma("pool", d["hT_out"][:, ts].rearrange("(c p) t -> p c t", p=128), h[:], b_h[0],
                  reads=b_h, writes=[b_hd])
        else:
            for nb in range(nK):
                i = nb % 2
                S.op("dve", lambda e, nb=nb, i=i: e.scalar_tensor_tensor(fo[i][:], h[:, nb, :], gn[:, nb:nb + 1], rstd[:],
                                                                          ALU.mult, ALU.mult),
                     reads=[b_h[nb], b_rstd, b_const], writes=[b_fo[i]])
                S.dma("pool", d["out"][nb * 128:(nb + 1) * 128, ts], fo[i][:], b_fo[i], reads=[b_fo[i]], writes=[b_hn])
    return [b_hd, b_hn]


def emit_N(S, d, D, TT, T):
    nK = D // 128
    h = S.sb("N_h", [128, nK, T], F32)
    xo = S.sb("N_xo", [128, nK, T], BF16)
    sq = [S.sb("N_sq%d" % i, [128, T], BF16) for i in range(2)]
    ones = S.sb("N_ones", [128, 128], BF16)
    g = S.sb("N_g", [128, nK], F32)
    rstd = S.sb("N_rstd", [128, T], F32)
    pn = S.ps("N_pn")
    b_h, b_xo, b_c, b_r, b_out = (S.buf() for _ in range(5))
    b_pn = S.buf("Npn", True)
    b_sq = [S.buf(), S.buf()]
    S.op("pool", lambda e: e.memset(ones[:], 1.0), writes=[b_c])
    S.dma("sp", g[:], d["g"], b_c, writes=[b_c])
    for ti in range(TT // T):
        ts = slice(ti * T, (ti + 1) * T)
        S.dma("sp", h[:], d["hT"][:, ts].rearrange("(c p) t -> p c t", p=128), b_h, writes=[b_h])
        for nb in range(nK):
            i = nb % 2
            S.op("act", lambda e, nb=nb, i=i: e.activation(out=sq[i][:], in_=h[:, nb, :], func=AF.Square),
                 reads=[b_h], writes=[b_sq[i]])
            S.op("pe", lambda e, nb=nb, i=i: e.matmul(pn[:, 0:T], ones[:], sq[i][:], start=(nb == 0), stop=(nb == nK - 1)),
                 reads=[b_sq[i], b_c], writes=[b_pn])
        S.op("dve", lambda e: e.tensor_scalar(rstd[:], pn[:, 0:T], 1.0 / D, 1e-6, ALU.mult, ALU.add),
             reads=[b_pn], writes=[b_r])
        S.op("act", lambda e: e.activation(out=rstd[:], in_=rstd[:], func=AF.Sqrt), reads=[b_r], writes=[b_r])
        S.op("dve", lambda e: e.reciprocal(rstd[:], rstd[:]), reads=[b_r], writes=[b_r])
        for nb in range(nK):
            S.op("dve", lambda e, nb=nb: e.scalar_tensor_tensor(xo[:, nb, :], h[:, nb, :], g[:, nb:nb + 1], rstd[:],
                                                                 ALU.mult, ALU.mult),
                 reads=[b_h, b_r, b_c], writes=[b_xo])
        S.dma("sp", d["hnT_out"][:, ts].rearrange("(c p) t -> p c t", p=128), xo[:], b_xo, reads=[b_xo], writes=[b_out])
    return [b_out]


def emit_Afox(S, d, D, NB, SL, NH):
    nc = S.nc
    nK = D // 128
    T = 512
    nseg = SL // T
    ntt = NB * nseg
    NP = NH * ntt
    assert NP <= 128
    nblk = SL // 128
    scale = 128 ** -0.5
    qkv_d = nc.dram_tensor("fox_qkv", [3 * NH, 128, NB * SL], BF16).ap()
    lf_d = nc.dram_tensor("fox_lf", [NH, ntt, T], F32).ap()
    cum_d = nc.dram_tensor("fox_cum", [NH, ntt, T], F32).ap()
    b_qkv, b_lf, b_cum, b_od = S.buf("qkv_d"), S.buf("lf_d"), S.buf("cum_d"), S.buf("oT_d")

    xa = S.sb("A_xa", [128, nK, T], BF16)
    b_xa = [S.buf("A_xa%d" % i) for i in range(nK)]
    st = [S.sb("A_st%d" % i, [128, T], BF16) for i in range(3)]
    b_st = [S.buf("A_st%d" % i) for i in range(3)]
    lft = [S.sb("A_lf%d" % i, [NH, T], F32) for i in range(2)]
    b_lft = [S.buf("A_lft%d" % i) for i in range(2)]
    negb = S.sb("A_negb", [NH, 1], F32)
    b_c = S.buf("A_const")
    S.dma("pool", negb[:], d["bf"], b_c, writes=[b_c])
    S.op("dve", lambda e: e.tensor_scalar(negb[:], negb[:], -1.0, None, ALU.mult), reads=[b_c], writes=[b_c])
    G = Gemm(S, nK, nslots=3, nps=2, tag="A")
    W16, b_W16 = G.precast("fox_W16", d["W"], 3 * NH + 1)
    cnt = {"st": 0, "lf": 0}
    for tt in range(ntt):
        ts = slice(tt * T, (tt + 1) * T)
        S.dma("pool", xa[:], d["hnT"][:, ts].rearrange("(c p) t -> p c t", p=128), b_xa[0], writes=b_xa)

        def epi(nb, pb, pap, tt=tt, ts=ts):
            if nb < 3 * NH:
                i = cnt["st"] % 3
                cnt["st"] += 1
                if i % 2 == 0:
                    S.op("act", lambda e: e.activation(out=st[i][:], in_=pap, func=AF.Copy), reads=[pb], writes=[b_st[i]])
                else:
                    S.op("dve", lambda e: e.tensor_copy(st[i][:], pap), reads=[pb], writes=[b_st[i]])
                S.dma("pool", qkv_d[nb, :, ts], st[i][:], b_st[i], reads=[b_st[i]], writes=[b_qkv])
            else:
                i = cnt["lf"] % 2
                cnt["lf"] += 1
                S.op("act", lambda e: e.activation(out=lft[i][:], in_=pap[0:NH, :], func=AF.Exp, scale=-1.0,
                                                   bias=negb[:, 0:1]), reads=[pb, b_c], writes=[b_lft[i]])
                S.op("act", lambda e: e.activation(out=lft[i][:], in_=lft[i][:], func=AF.Ln, bias=1.0),
                     reads=[b_lft[i]], writes=[b_lft[i]])
                S.dma("pool", lf_d[:, tt, :], lft[i][:], b_lft[i], reads=[b_lft[i]], writes=[b_lf])

        G.run(lambda kc: xa[:, kc, :], b_xa, lambda nb: W16[nb], 3 * NH + 1, T, epi, wbuf=b_W16)

    LF = S.sb("A_LF", [NP, T], F32)
    onesf = S.sb("A_onesf", [NP, T], F32)
    mtri = S.sb("A_mtri", [NP, NP], F32)
    tot = S.sb("A_tot", [NP, 1], F32)
    off = S.sb("A_off", [NP, 1], F32)
    b_LF, b_m = S.buf("LF"), S.buf("m")
    ptr = [S.ps("A_ptr%d" % i) for i in range(2)]
    b_ptr = [S.buf("A_ptr%d" % i, True) for i in range(2)]
    S.dma("sp", LF[:], lf_d.rearrange("h t f -> (h t) f"), b_LF, reads=[b_lf], writes=[b_LF])
    S.dma("sp", mtri[:], d["mtri"], b_m, writes=[b_m])
    S.op("pool", lambda e: e.memset(onesf[:], 1.0), writes=[b_m])
    LC = S.sb("A_LC", [NP, T], F32)
    b_LC = S.buf("LC")
    S.op("dve", lambda e: e.tensor_tensor_scan(LC[:], onesf[:], LF[:], 0.0, ALU.mult, ALU.subtract),
         reads=[b_LF, b_m], writes=[b_LC])
    S.op("dve", lambda e: e.tensor_copy(tot[:], LC[:, T - 1:T]), reads=[b_LC], writes=[b_m])
    S.op("pe", lambda e: e.matmul(ptr[0][0:NP, 0:1], mtri[:], tot[:], start=True, stop=True),
         reads=[b_m], writes=[b_ptr[0]])
    S.op("dve", lambda e: e.tensor_copy(off[:], ptr[0][0:NP, 0:1]), reads=[b_ptr[0]], writes=[b_m])
    S.op("dve", lambda e: e.tensor_scalar(LC[:], LC[:], off[:, 0:1], None, ALU.add), reads=[b_LC, b_m], writes=[b_LC])
    S.dma("sp", cum_d.rearrange("h t f -> (h t) f"), LC[:], b_LC, reads=[b_LC], writes=[b_cum])

    QT = S.sb("A_QT", [128, SL], BF16)
    KT = S.sb("A_KT", [128, SL], BF16)
    VT = S.sb("A_VT", [128, SL], BF16)
    VP = S.sb("A_VP", [128, nblk, 132], BF16)
    CB = S.sb("A_CB", [128, SL], F32)
    crow = S.sb("A_crow", [nblk, 128], F32)
    negcs = S.sb("A_negcs", [128, nblk], F32)
    kmax = S.sb("A_kmax", [128, 1], F32)
    kmx = S.sb("A_kmx", [128, SL // T], F32)
    identb = S.sb("A_identb", [128, 128], BF16)
    identf = S.sb("A_identf", [128, 128], F32)
    onesb = S.sb("A_onesb", [128, 128], BF16)
    tri = S.sb("A_tri", [128, 128], F32)
    sqt = [S.sb("A_sqt%d" % i, [128, T], BF16) for i in range(2)]
    tmp = [S.sb("A_tmp%d" % i, [128, T], F32) for i in range(3)]
    PT = [S.sb("A_PT%d" % i, [128, T], BF16) for i in range(3)]
    rinv = S.sb("A_rinv", [128, 4], F32)
    ot = [S.sb("A_ot%d" % i, [128, 128], F32) for i in range(2)]
    oT = [S.sb("A_oT%d" % i, [128, T], F32) for i in range(2)]
    b_QT, b_KT, b_VT, b_VP, b_CB, b_ncs, b_km = (S.buf(n) for n in ("QT", "KT", "VT", "VP", "CB", "ncs", "km"))
    b_crow = S.buf("crow")
    b_sqt = [S.buf(), S.buf()]
    b_tmp = [S.buf() for _ in range(3)]
    b_PT = [S.buf() for _ in range(3)]
    b_rinv = S.buf("rinv")
    b_ot = [S.buf(), S.buf()]
    b_oT = [S.buf(), S.buf()]
    pS = G.pt
    b_pS = G.pb
    pO = [S.ps("A_pO%d" % i) for i in range(4)]
    b_pO = [S.buf("A_pO%d" % i, True) for i in range(4)]
    ptrb = ptr[1]
    pTb = pO[3][:].bitcast(BF16)
    b_pTb = b_pO[3]
    S.dma("sp", identb[:], d["identb"], b_c, writes=[b_c])
    S.dma("sp", identf[:], d["identf"], b_c, writes=[b_c])
    S.dma("sp", tri[:], d["tri"], b_c, writes=[b_c])
    S.op("pool", lambda e: e.memset(onesb[:], 1.0), writes=[b_c])
    S.op("pool", lambda e: e.memset(VP[:], 1.0), writes=[b_VP])
    k = {"s": 0, "t": 0, "p": 0, "o": 0, "oT": 0, "sq": 0}
    for b in range(NB):
        for hl in range(NH):
            tsl = slice(b * SL, (b + 1) * SL)
            S.dma("sp", QT[:], qkv_d[hl, :, tsl], b_QT, reads=[b_qkv], writes=[b_QT])
            S.dma("sp", KT[:], qkv_d[NH + hl, :, tsl], b_KT, reads=[b_qkv], writes=[b_KT])
            S.dma("sp", VT[:], qkv_d[2 * NH + hl, :, tsl], b_VT, reads=[b_qkv], writes=[b_VT])
            cum_row = cum_d[hl, b * nseg:(b + 1) * nseg, :]
            S.dma("sp", CB[:], cum_row.rearrange("s f -> (s f)").partition_broadcast(128), b_CB,
                  reads=[b_cum], writes=[b_CB])
            S.dma("sp", crow[:], cum_row.rearrange("s (j p) -> (s j) p", p=128), b_crow, reads=[b_cum], writes=[b_crow])
            S.op("pe", lambda e: e.transpose(ptr[1][:, 0:nblk], crow[:], identf[0:nblk, 0:nblk]),
                 reads=[b_crow, b_c], writes=[b_ptr[1]])
            S.op("dve", lambda e: e.tensor_scalar(negcs[:], ptr[1][:, 0:nblk], -1.0, None, ALU.mult),
                 reads=[b_ptr[1]], writes=[b_ncs])
            for j0 in range(0, nblk, 8):
                nj = min(8, nblk - j0)
                for jj in range(nj):
                    j = j0 + jj
                    S.op("pe", lambda e, j=j, jj=jj: e.transpose(pTb[:, jj * 128:(jj + 1) * 128], VT[:, j * 128:(j + 1) * 128], identb[:]),
                         reads=[b_VT, b_c], writes=[b_pTb])
                S.op("act", lambda e, j0=j0, nj=nj: e.activation(
                    out=VP[:, j0:j0 + nj, 0:128], in_=pTb[:, 0:nj * 128].rearrange("p (j d) -> p j d", d=128), func=AF.Copy),
                    reads=[b_pTb], writes=[b_VP])
            for c in range(SL // T):
                i = k["sq"] % 2
                k["sq"] += 1
                S.op("pool", lambda e, c=c, i=i: e.tensor_tensor(sqt[i][:], KT[:, c * T:(c + 1) * T], KT[:, c * T:(c + 1) * T], ALU.mult),
                     reads=[b_KT], writes=[b_sqt[i]])
                S.op("pe", lambda e, i=i: e.matmul(ptr[0][:, 0:T], onesb[:], sqt[i][:], start=True, stop=True),
                     reads=[b_sqt[i], b_c], writes=[b_ptr[0]])
                S.op("dve", lambda e, c=c: e.reduce_max(kmx[:, c:c + 1], ptr[0][:, 0:T], axis=AX.X),
                     reads=[b_ptr[0]], writes=[b_km])
            S.op("dve", lambda e: e.reduce_max(kmax[:], kmx[:], axis=AX.X), reads=[b_km], writes=[b_km])
            for c in range(SL // T):
                i = k["sq"] % 2
                k["sq"] += 1
                ti = k["t"] % 3
                k["t"] += 1
                S.op("pool", lambda e, c=c, i=i: e.tensor_tensor(sqt[i][:], QT[:, c * T:(c + 1) * T], QT[:, c * T:(c + 1) * T], ALU.mult),
                     reads=[b_QT], writes=[b_sqt[i]])
                S.op("pe", lambda e, i=i: e.matmul(ptr[0][:, 0:T], onesb[:], sqt[i][:], start=True, stop=True),
                     reads=[b_sqt[i], b_c], writes=[b_ptr[0]])
                S.op("act", lambda e, ti=ti: e.activation(out=tmp[ti][:], in_=ptr[0][:, 0:T], func=AF.Sqrt, scale=kmax[:, 0:1]),
                     reads=[b_ptr[0], b_km], writes=[b_tmp[ti]])
                S.op("dve", lambda e, c=c, ti=ti: e.scalar_tensor_tensor(CB[:, c * T:(c + 1) * T], tmp[ti][:], -scale,
                                                                         CB[:, c * T:(c + 1) * T], ALU.mult, ALU.add),
                     reads=[b_tmp[ti], b_CB], writes=[b_CB])
            def emit_qk(I, j):
                r = max(0, j - 4 * I)
                c0 = r * 128
                W_ = T - c0
                si = k["s"] % 2
                k["s"] += 1
                S.op("pe", lambda e: e.matmul(
                    pS[si][:, 0:W_], KT[:, j * 128:(j + 1) * 128], QT[:, I * T + c0:(I + 1) * T], start=True, stop=True),
                    reads=[b_KT, b_QT], writes=[b_pS[si]])
                return (I, j, r, c0, W_, si)

            tiles = [(I, j) for I in range(SL // T) for j in range(4 * I + 4)]
            nxt = emit_qk(*tiles[0])
            for n, (I, j) in enumerate(tiles):
                _, _, r, c0, W_, si = nxt
                if n + 1 < len(tiles):
                    nxt = emit_qk(*tiles[n + 1])
                ti = k["t"] % 3
                k["t"] += 1
                pi = k["p"] % 3
                k["p"] += 1
                S.op("dve", lambda e, I=I, c0=c0, W_=W_, si=si, ti=ti: e.scalar_tensor_tensor(
                    tmp[ti][:, 0:W_], pS[si][:, 0:W_], scale, CB[:, I * T + c0:(I + 1) * T], ALU.mult, ALU.add),
                    reads=[b_pS[si], b_CB], writes=[b_tmp[ti]])
                if j >= 4 * I:
                    S.op("pool", lambda e, ti=ti: e.tensor_tensor(tmp[ti][:, 0:128], tmp[ti][:, 0:128], tri[:], ALU.add),
                         reads=[b_tmp[ti], b_c], writes=[b_tmp[ti]])
                S.op("act", lambda e, j=j, W_=W_, ti=ti, pi=pi: e.activation(
                    out=PT[pi][:, 0:W_], in_=tmp[ti][:, 0:W_], func=AF.Exp, bias=negcs[:, j:j + 1]),
                    reads=[b_tmp[ti], b_ncs], writes=[b_PT[pi]])
                for u in range(r, 4):
                    S.op("pe", lambda e, j=j, u=u, r=r, pi=pi, I=I: e.matmul(
                        pO[u][:, 0:129], PT[pi][:, (u - r) * 128:(u - r + 1) * 128], VP[:, j, 0:129],
                        start=(j == 0), stop=(j == 4 * I + u)),
                        reads=[b_PT[pi], b_VP], writes=[b_pO[u]])
                if j != 4 * I + 3:
                    continue
                oi = k["oT"] % 2
                k["oT"] += 1
                for u in range(4):
                    S.op("dve", lambda e, u=u: e.reciprocal(rinv[:, u:u + 1], pO[u][:, 128:129]),
                         reads=[b_pO[u]], writes=[b_rinv])
                    o_i = k["o"] % 2
                    k["o"] += 1
                    S.op("act", lambda e, u=u, o_i=o_i: e.activation(out=ot[o_i][:], in_=pO[u][:, 0:128], func=AF.Copy,
                                                                   scale=rinv[:, u:u + 1]),
                         reads=[b_pO[u], b_rinv], writes=[b_ot[o_i]])
                    S.op("pe", lambda e, o_i=o_i: e.transpose(ptr[1][:, 0:128], ot[o_i][:], identf[:]),
                         reads=[b_ot[o_i], b_c], writes=[b_ptr[1]])
                    S.op("dve", lambda e, u=u, oi=oi: e.tensor_copy(oT[oi][:, u * 128:(u + 1) * 128], ptr[1][:, 0:128]),
                         reads=[b_ptr[1]], writes=[b_oT[oi]])
                S.dma("sp", d["oT"][hl * 128:(hl + 1) * 128, b * SL + I * T: b * SL + (I + 1) * T], oT[oi][:], b_oT[oi],
                      reads=[b_oT[oi]], writes=[b_od])
    return [b_od]


_STOP = [None]


class _Stop(Exception):
    pass


def _chk(k):
    if _STOP[0] == k:
        raise _Stop()


def emit_Agdn(S, d, D, NB, SL, NH):
    try:
        return _emit_Agdn(S, d, D, NB, SL, NH)
    except _Stop:
        return []


def _emit_Agdn(S, d, D, NB, SL, NH):
    nc = S.nc
    nK = D // 128
    T = 512
    C = 128
    ntt = NB * SL // T
    ngrp = SL // T
    R = NH * NB * SL // C
    qkv_d = nc.dram_tensor("gdn_qkv", [3 * NH, 128, NB * SL], F32).ap()
    bg_d = nc.dram_tensor("gdn_bg", [2, NH, NB * SL], F32).ap()
    gc_d = nc.dram_tensor("gdn_gc", [NH, NB * SL], F32).ap()
    b_qkv, b_bg, b_gc, b_od = S.buf("gqkv"), S.buf("gbg"), S.buf("ggc"), S.buf("goT")
    b_c = S.buf("gconst")

    def V(fn, r=(), w=()):
        S.op("dve", fn, reads=r, writes=w)

    def A_(fn, r=(), w=()):
        S.op("act", fn, reads=r, writes=w)

    def P_(fn, r=(), w=()):
        S.op("pool", fn, reads=r, writes=w)

    def M_(fn, r=(), w=()):
        S.op("pe", fn, reads=r, writes=w)

    xa = S.sb("G_xa", [128, nK, T], BF16)
    b_xa = [S.buf("G_xa%d" % i) for i in range(nK)]
    st = [S.sb("G_st%d" % i, [128, T], F32) for i in range(3)]
    b_st = [S.buf("G_st%d" % i) for i in range(3)]
    g8 = 2 * NH
    gt = [S.sb("G_gt%d" % i, [g8, T], F32) for i in range(2)]
    gs = [S.sb("G_gs%d" % i, [g8, T], F32) for i in range(2)]
    b_gt = [S.buf("G_gt%d" % i) for i in range(2)]
    b_gs = [S.buf("G_gs%d" % i) for i in range(2)]
    gbias = S.sb("G_gbias", [g8, 1], F32)
    gcoef = S.sb("G_gcoef", [g8, 1], F32)
    S.dma("pool", gbias[:], d["gbias"], b_c, writes=[b_c])
    S.dma("pool", gcoef[:], d["galog"], b_c, writes=[b_c])
    A_(lambda e: e.activation(out=gcoef[:], in_=gcoef[:], func=AF.Exp), [b_c], [b_c])
    V(lambda e: e.tensor_scalar(gcoef[:], gcoef[:], -1.0, None, ALU.mult), [b_c], [b_c])
    G = Gemm(S, nK, nslots=3, nps=2, tag="G")
    W16, b_W16 = G.precast("gdn_W16", d["W"], 3 * NH + 1)
    cnt = {"st": 0, "g": 0}
    for tt in range(ntt):
        ts = slice(tt * T, (tt + 1) * T)
        S.dma("pool", xa[:], d["hnT"][:, ts].rearrange("(c p) t -> p c t", p=128), b_xa[0], writes=b_xa)

        def epi(nb, pb, pap, ts=ts):
            if nb < 3 * NH:
                i = cnt["st"] % 3
                cnt["st"] += 1
                if i % 2 == 0:
                    A_(lambda e: e.activation(out=st[i][:], in_=pap, func=AF.Copy), [pb], [b_st[i]])
                else:
                    V(lambda e: e.tensor_copy(st[i][:], pap), [pb], [b_st[i]])
                S.dma("pool", qkv_d[nb, :, ts], st[i][:], b_st[i], reads=[b_st[i]], writes=[b_qkv])
            else:
                i = cnt["g"] % 2
                cnt["g"] += 1
                A_(lambda e: e.activation(out=gs[i][:], in_=pap[0:g8, :], func=AF.Sigmoid), [pb], [b_gs[i]])
                A_(lambda e: e.activation(out=gt[i][:], in_=pap[0:g8, :], func=AF.Exp, bias=gbias[:, 0:1]), [pb, b_c], [b_gt[i]])
                A_(lambda e: e.activation(out=gt[i][:], in_=gt[i][:], func=AF.Ln, bias=1.0), [b_gt[i]], [b_gt[i]])
                V(lambda e: e.tensor_scalar(gt[i][:], gt[i][:], gcoef[:, 0:1], None, ALU.mult), [b_gt[i], b_c], [b_gt[i]])
                S.dma("pool", bg_d[0, :, ts], gs[i][0:NH, :], b_gs[i], reads=[b_gs[i]], writes=[b_bg])
                S.dma("pool", bg_d[1, :, ts], gt[i][NH:g8, :], b_gt[i], reads=[b_gt[i]], writes=[b_bg])

        G.run(lambda kc: xa[:, kc, :], b_xa, lambda nb: W16[nb], 3 * NH + 1, T, epi, wbuf=b_W16)

    _chk(1)
    onesr = S.sb("G_onesr", [128, C], F32)
    P_(lambda e: e.memset(onesr[:], 1.0), [], [b_c])
    gr = [S.sb("G_gr%d" % i, [128, C], F32) for i in range(2)]
    gq = [S.sb("G_gq%d" % i, [128, C], F32) for i in range(2)]
    b_gr = [S.buf(), S.buf()]
    b_gq = [S.buf(), S.buf()]
    g_rows = bg_d[1].rearrange("h (n c) -> (h n) c", c=C)
    gc_rows = gc_d.rearrange("h (n c) -> (h n) c", c=C)
    for r0 in range(0, R, 128):
        nr = min(128, R - r0)
        i = (r0 // 128) % 2
        S.dma("sp", gr[i][0:nr, :], g_rows[r0:r0 + nr, :], b_gr[i], reads=[b_bg], writes=[b_gr[i]])
        V(lambda e, i=i, nr=nr: e.tensor_tensor_scan(gq[i][0:nr, :], onesr[0:nr, :], gr[i][0:nr, :], 0.0, ALU.mult, ALU.add),
          [b_gr[i], b_c], [b_gq[i]])
        S.dma("sp", gc_rows[r0:r0 + nr, :], gq[i][0:nr, :], b_gq[i], reads=[b_gq[i]], writes=[b_gc])

    _chk(2)
    def sbt(name, shape, dt=F32):
        return S.sb("G_" + name, shape, dt), S.buf("G_" + name)

    mstr, _ = sbt("mstr", [128, T]); mup, _ = sbt("mup", [128, T]); id4, _ = sbt("id4", [128, T])
    idf, _ = sbt("idf", [128, 128]); onesf, _ = sbt("onesf", [128, 128]); nwr, _ = sbt("nwr", [128, 128])
    for tle, key in ((mstr, "mstr4"), (mup, "mup4"), (id4, "id4"), (idf, "identf")):
        S.dma("sp", tle[:], d[key], b_c, writes=[b_c])
    S.dma("sp", nwr[:], d["nw"].partition_broadcast(128), b_c, writes=[b_c])
    P_(lambda e: e.memset(onesf[:], 1.0), [], [b_c])
    cw, b_cw = sbt("cw", [128, 3, 4])
    xin = [[sbt("x%d_%d" % (a, i), [128, T + 3]) for a in range(3)] for i in range(2)]
    GB = [sbt("GB%d" % i, [128, T]) for i in range(2)]
    rows = [sbt("rows%d" % i, [8, C]) for i in range(2)]
    yq, b_yq = sbt("yq", [128, T]); yk, b_yk = sbt("yk", [128, T]); yv, b_yv = sbt("yv", [128, T])
    sq, b_sq = sbt("sq", [128, T]); rr, b_rr = sbt("rr", [128, T])
    qn, b_qn = sbt("qn", [128, T]); kn, b_kn = sbt("kn", [128, T])
    kTM, b_kTM = sbt("kTM", [128, 4, C]); vTM, b_vTM = sbt("vTM", [128, 4, C])
    gcol, b_gcol = sbt("gcol", [128, 8])
    cols, b_cols = sbt("cols", [128, 24])
    bv, b_bv = sbt("bv", [128, 4, C]); kbg, b_kbg = sbt("kbg", [128, 4, C]); kdec, b_kdec = sbt("kdec", [128, 4, C])
    t1, b_t1 = sbt("t1", [128, T]); t2, b_t2 = sbt("t2", [128, T])
    E1, b_E1 = sbt("E1", [128, T]); E2, b_E2 = sbt("E2", [128, T]); eGB, b_eGB = sbt("eGB", [128, T])
    aT, b_aT = sbt("aT", [128, T]); qd, b_qd = sbt("qd", [128, T])
    Pm = [sbt("P%d" % i, [128, T]) for i in range(2)]
    PTm = [sbt("PT%d" % i, [128, T]) for i in range(2)]
    TTm = [sbt("TT%d" % i, [128, T]) for i in range(2)]
    u, b_u = sbt("u", [128, 4, C]); wT, b_wT = sbt("wT", [128, T])
    Sst, b_S = sbt("S", [128, C]); vnew, b_vnew = sbt("vnew", [128, C])
    ssq, b_ssq = sbt("ssq", [128, 1]); rstd, b_rstd = sbt("rstd", [128, 1]); junk, b_junk = sbt("junk", [128, C])
    oTM, b_oTM = sbt("oTM", [128, C])
    oFM = [sbt("oFM%d" % i, [128, T]) for i in range(2)]
    pk = [(S.ps("G_pk%d" % i), S.buf("G_pk%d" % i, True)) for i in range(6)]
    pk = [(G.pt[0], G.pb[0]), (G.pt[1], G.pb[1])] + pk
    (pA, b_pA), (pB, b_pB), (pC, b_pC), (pD, b_pD), (pE, b_pE), (pF, b_pF), (pG, b_pG), (pH, b_pH) = pk
    gi = 0
    for b in range(NB):
        for hl in range(NH):
            S.dma("sp", cw[:], d["cw"].rearrange("(a h) p j -> h p a j", h=NH)[hl], b_cw, writes=[b_cw])
            V(lambda e: e.memset(Sst[:], 0.0), [], [b_S])
            for g in range(ngrp):
                par = gi % 2
                gi += 1
                t0 = b * SL + g * T
                for a in range(3):
                    xt, xb_ = xin[par][a]
                    if g == 0:
                        P_(lambda e, xt=xt: e.memset(xt[:, 0:3], 0.0), [], [xb_])
                        S.dma("sp", xt[:, 3:T + 3], qkv_d[a * NH + hl, :, t0:t0 + T], xb_, reads=[b_qkv], writes=[xb_])
                    else:
                        S.dma("sp", xt[:], qkv_d[a * NH + hl, :, t0 - 3:t0 + T], xb_, reads=[b_qkv], writes=[xb_])
                GBt, b_GB = GB[par]
                S.dma("sp", GBt[:], gc_d[hl, t0:t0 + T].partition_broadcast(128), b_GB, reads=[b_gc], writes=[b_GB])
                rw, b_rw = rows[par]
                S.dma("sp", rw[0:4, :], gc_d[hl, t0:t0 + T].rearrange("(n c) -> n c", c=C), b_rw, reads=[b_gc], writes=[b_rw])
                S.dma("sp", rw[4:8, :], bg_d[0, hl, t0:t0 + T].rearrange("(n c) -> n c", c=C), b_rw, reads=[b_bg], writes=[b_rw])
                for a, (yt, yb) in enumerate(((yq, b_yq), (yk, b_yk), (yv, b_yv))):
                    xt, xb_ = xin[par][a]
                    V(lambda e, xt=xt, yt=yt, a=a: e.tensor_scalar(yt[:], xt[:, 3:T + 3], cw[:, a, 3:4], None, ALU.mult),
                      [xb_, b_cw], [yb])
                    for j in range(3):
                        V(lambda e, xt=xt, yt=yt, a=a, j=j: e.scalar_tensor_tensor(
                            yt[:], xt[:, j:j + T], cw[:, a, j:j + 1], yt[:], ALU.mult, ALU.add), [xb_, b_cw, yb], [yb])
                    A_(lambda e, yt=yt: e.activation(out=yt[:], in_=yt[:], func=AF.Silu), [yb], [yb])
                _chk(3)
                for (yt, yb, ot_, ob, sc, pp_, pb_) in ((yq, b_yq, qn, b_qn, 128 ** -0.5, pA, b_pA), (yk, b_yk, kn, b_kn, 1.0, pB, b_pB)):
                    P_(lambda e, yt=yt: e.tensor_tensor(sq[:], yt[:], yt[:], ALU.mult), [yb], [b_sq])
                    M_(lambda e, pp_=pp_: e.matmul(pp_[:, 0:T], onesf[:], sq[:], start=True, stop=True), [b_sq, b_c], [pb_])
                    V(lambda e, pp_=pp_: e.tensor_scalar(rr[:], pp_[:, 0:T], 1e-6, None, ALU.add), [pb_], [b_rr])
                    A_(lambda e: e.activation(out=rr[:], in_=rr[:], func=AF.Sqrt), [b_rr], [b_rr])
                    V(lambda e: e.reciprocal(rr[:], rr[:]), [b_rr], [b_rr])
                    V(lambda e, yt=yt, ot_=ot_, sc=sc: e.scalar_tensor_tensor(ot_[:], yt[:], sc, rr[:], ALU.mult, ALU.mult),
                      [yb, b_rr], [ob])
                _chk(4)
                M_(lambda e, rw=rw: e.transpose(pC[:, 0:8], rw[:], idf[0:8, 0:8]), [b_rw, b_c], [b_pC])
                V(lambda e: e.tensor_copy(gcol[:], pC[:, 0:8]), [b_pC], [b_gcol])
                V(lambda e: e.tensor_scalar(cols[:, 0:4], gcol[:, 4:8], -1.0, None, ALU.mult), [b_gcol], [b_cols])
                V(lambda e: e.tensor_scalar(cols[:, 4:8], gcol[:, 0:4], -1.0, None, ALU.mult), [b_gcol, b_cols], [b_cols])
                A_(lambda e: e.activation(out=cols[:, 8:12], in_=gcol[:, 0:4], func=AF.Exp), [b_gcol, b_cols], [b_cols])
                V(lambda e: e.tensor_tensor(cols[:, 8:12], cols[:, 8:12], gcol[:, 4:8], ALU.mult), [b_gcol, b_cols], [b_cols])
                V(lambda e, GBt=GBt: e.tensor_tensor(cols[:, 12:16], GBt[:].rearrange("p (n c) -> p n c", c=C)[:, :, C - 1], gcol[:, 0:4], ALU.subtract),
                  [b_GB, b_gcol, b_cols], [b_cols])
                A_(lambda e: e.activation(out=cols[:, 12:16], in_=cols[:, 12:16], func=AF.Exp), [b_cols], [b_cols])
                A_(lambda e, GBt=GBt: e.activation(out=cols[:, 16:20], in_=GBt[:].rearrange("p (n c) -> p n c", c=C)[:, :, C - 1], func=AF.Exp),
                   [b_GB, b_cols], [b_cols])
                _chk(5)
                for n in range(4):
                    M_(lambda e, n=n: e.transpose(pA[:, n * C:(n + 1) * C], kn[:, n * C:(n + 1) * C], idf[:]), [b_kn, b_c], [b_pA])
                    M_(lambda e, n=n: e.transpose(pB[:, n * C:(n + 1) * C], yv[:, n * C:(n + 1) * C], idf[:]), [b_yv, b_c], [b_pB])
                A_(lambda e: e.activation(out=kTM[:].rearrange("p n c -> p (n c)"), in_=pA[:, 0:T], func=AF.Copy), [b_pA], [b_kTM])
                A_(lambda e: e.activation(out=vTM[:].rearrange("p n c -> p (n c)"), in_=pB[:, 0:T], func=AF.Copy), [b_pB], [b_vTM])
                for n in range(4):
                    P_(lambda e, n=n: e.tensor_scalar(bv[:, n, :], vTM[:, n, :], gcol[:, 4 + n:5 + n], None, ALU.mult), [b_vTM, b_gcol], [b_bv])
                    P_(lambda e, n=n: e.tensor_scalar(kbg[:, n, :], kTM[:, n, :], cols[:, 8 + n:9 + n], None, ALU.mult), [b_kTM, b_cols], [b_kbg])
                    P_(lambda e, n=n: e.tensor_scalar(kdec[:, n, :], kTM[:, n, :], cols[:, 12 + n:13 + n], None, ALU.mult), [b_kTM, b_cols], [b_kdec])
                _chk(6)
                V(lambda e, GBt=GBt: e.scalar_tensor_tensor(t1[:], GBt[:], -1.0, mstr[:], ALU.mult, ALU.add), [b_GB, b_c], [b_t1])
                V(lambda e, GBt=GBt: e.tensor_tensor(t2[:], GBt[:], mup[:], ALU.add), [b_GB, b_c], [b_t2])
                for n in range(4):
                    cs = slice(n * C, (n + 1) * C)
                    A_(lambda e, n=n, cs=cs: e.activation(out=E1[:, cs], in_=t1[:, cs], func=AF.Exp, bias=gcol[:, n:n + 1]), [b_t1, b_gcol], [b_E1])
                    A_(lambda e, n=n, cs=cs: e.activation(out=E2[:, cs], in_=t2[:, cs], func=AF.Exp, bias=cols[:, 4 + n:5 + n]), [b_t2, b_cols], [b_E2])
                A_(lambda e, GBt=GBt: e.activation(out=eGB[:], in_=GBt[:], func=AF.Exp), [b_GB], [b_eGB])
                V(lambda e: e.tensor_tensor(qd[:], qn[:], eGB[:], ALU.mult), [b_qn, b_eGB], [b_qd])
                _chk(7)
                for n in range(4):
                    cs = slice(n * C, (n + 1) * C)
                    M_(lambda e, cs=cs: e.matmul(pC[:, cs], kn[:, cs], kn[:, cs], start=True, stop=True), [b_kn], [b_pC])
                    M_(lambda e, cs=cs: e.matmul(pD[:, cs], kn[:, cs], qn[:, cs], start=True, stop=True), [b_kn, b_qn], [b_pD])
                _chk(71)
                P0, b_P0 = Pm[0]
                PT0, b_PT0 = PTm[0]
                TT0, b_TT0 = TTm[0]
                for n in range(4):
                    cs = slice(n * C, (n + 1) * C)
                    V(lambda e, n=n, cs=cs: e.scalar_tensor_tensor(P0[:, cs], pC[:, cs], cols[:, n:n + 1], E1[:, cs], ALU.mult, ALU.mult),
                      [b_pC, b_cols, b_E1], [b_P0])
                V(lambda e: e.tensor_tensor(aT[:], pD[:, 0:T], E2[:], ALU.mult), [b_pD, b_E2], [b_aT])
                _chk(72)
                for n in range(4):
                    cs = slice(n * C, (n + 1) * C)
                    M_(lambda e, cs=cs: e.transpose(pE[:, cs], P0[:, cs], idf[:]), [b_P0, b_c], [b_pE])
                A_(lambda e: e.activation(out=PT0[:], in_=pE[:, 0:T], func=AF.Copy), [b_pE], [b_PT0])
                V(lambda e: e.tensor_tensor(TT0[:], pE[:, 0:T], id4[:], ALU.add), [b_pE, b_c], [b_TT0])
                _chk(73)
                cur = 0
                for step in range(6):
                    Pc, b_Pc = Pm[cur]; PTc, b_PTc = PTm[cur]; TTc, b_TTc = TTm[cur]
                    Pn, b_Pn = Pm[1 - cur]; PTn, b_PTn = PTm[1 - cur]; TTn, b_TTn = TTm[1 - cur]
                    last = step == 5
                    for n in range(4):
                        cs = slice(n * C, (n + 1) * C)
                        M_(lambda e, cs=cs, Pc=Pc, PTc=PTc: e.matmul(pC[:, cs], PTc[:, cs], Pc[:, cs], start=True, stop=True), [b_Pc, b_PTc], [b_pC])
                    A_(lambda e, Pn=Pn: e.activation(out=Pn[:], in_=pC[:, 0:T], func=AF.Copy), [b_pC], [b_Pn])
                    if not last:
                        for n in range(4):
                            cs = slice(n * C, (n + 1) * C)
                            M_(lambda e, cs=cs, Pc=Pc, PTc=PTc: e.matmul(pD[:, cs], Pc[:, cs], PTc[:, cs], start=True, stop=True), [b_Pc, b_PTc], [b_pD])
                        V(lambda e, PTn=PTn: e.tensor_copy(PTn[:], pD[:, 0:T]), [b_pD], [b_PTn])
                    for n in range(4):
                        cs = slice(n * C, (n + 1) * C)
                        M_(lambda e, cs=cs, Pn=Pn, TTc=TTc: e.matmul(pE[:, cs], Pn[:, cs], TTc[:, cs], start=True, stop=True), [b_Pn, b_TTc], [b_pE])
                    V(lambda e, TTn=TTn, TTc=TTc: e.tensor_tensor(TTn[:], pE[:, 0:T], TTc[:], ALU.add), [b_pE, b_TTc], [b_TTn])
                    cur = 1 - cur
                    _chk(74 + step)
                TTf, b_TTf = TTm[cur]
                _chk(8)
                for n in range(4):
                    cs = slice(n * C, (n + 1) * C)
                    M_(lambda e, n=n, cs=cs: e.matmul(pA[:, cs], TTf[:, cs], bv[:, n, :], start=True, stop=True), [b_TTf, b_bv], [b_pA])
                    M_(lambda e, n=n, cs=cs: e.matmul(pB[:, cs], kbg[:, n, :], TTf[:, cs], start=True, stop=True), [b_TTf, b_kbg], [b_pB])
                A_(lambda e: e.activation(out=u[:].rearrange("p n c -> p (n c)"), in_=pA[:, 0:T], func=AF.Copy), [b_pA], [b_u])
                V(lambda e: e.tensor_copy(wT[:], pB[:, 0:T]), [b_pB], [b_wT])
                _chk(9)
                oF, b_oF = oFM[par]
                for n in range(4):
                    cs = slice(n * C, (n + 1) * C)
                    M_(lambda e, cs=cs: e.matmul(pF[:, 0:C], wT[:, cs], Sst[:], start=True, stop=True), [b_wT, b_S], [b_pF])
                    V(lambda e, n=n: e.tensor_tensor(vnew[:], u[:, n, :], pF[:, 0:C], ALU.subtract), [b_u, b_pF], [b_vnew])
                    M_(lambda e, cs=cs: e.matmul(pG[:, 0:C], qd[:, cs], Sst[:], start=True, stop=False), [b_qd, b_S], [b_pG])
                    M_(lambda e, cs=cs: e.matmul(pG[:, 0:C], aT[:, cs], vnew[:], start=False, stop=True), [b_aT, b_vnew], [b_pG])
                    M_(lambda e, n=n: e.matmul(pF[:, C:2 * C], kdec[:, n, :], vnew[:], start=True, stop=True), [b_kdec, b_vnew], [b_pF])
                    V(lambda e, n=n: e.scalar_tensor_tensor(Sst[:], Sst[:], cols[:, 16 + n:17 + n], pF[:, C:2 * C], ALU.mult, ALU.add),
                      [b_S, b_cols, b_pF], [b_S])
                    A_(lambda e: e.activation(out=junk[:], in_=pG[:, 0:C], func=AF.Square, accum_out=ssq[:, 0:1]), [b_pG], [b_junk, b_ssq])
                    V(lambda e: e.tensor_scalar(rstd[:], ssq[:], 1.0 / C, 1e-6, ALU.mult, ALU.add), [b_ssq], [b_rstd])
                    A_(lambda e: e.activation(out=rstd[:], in_=rstd[:], func=AF.Sqrt), [b_rstd], [b_rstd])
                    V(lambda e: e.reciprocal(rstd[:], rstd[:]), [b_rstd], [b_rstd])
                    V(lambda e: e.scalar_tensor_tensor(oTM[:], pG[:, 0:C], rstd[:, 0:1], nwr[:], ALU.mult, ALU.mult),
                      [b_pG, b_rstd, b_c], [b_oTM])
                    M_(lambda e, cs=cs: e.transpose(pH[:, cs], oTM[:], idf[:]), [b_oTM, b_c], [b_pH])
                A_(lambda e, oF=oF: e.activation(out=oF[:], in_=pH[:, 0:T], func=AF.Copy), [b_pH], [b_oF])
                S.dma("sp", d["oT"][hl * 128:(hl + 1) * 128, t0:t0 + T], oF[:], b_oF, reads=[b_oF], writes=[b_od])
    return [b_od]


def emit_Assm(S, d, D, NB, SL, NPAIR):
    nc = S.nc
    nK = D // 128
    T = 512
    ntt = NB * SL // T
    nblk = NPAIR // 4
    TWO_PI = 2.0 * np.pi
    uT_d = nc.dram_tensor("ssm_uT", [nblk, 128, NB * SL], F32).ap()
    b_ud, b_od, b_c = S.buf("ssm_ud"), S.buf("ssm_od"), S.buf("ssm_c")

    def V(fn, r=(), w=()):
        S.op("dve", fn, reads=r, writes=w)

    def A_(fn, r=(), w=()):
        S.op("act", fn, reads=r, writes=w)

    def P_(fn, r=(), w=()):
        S.op("pool", fn, reads=r, writes=w)

    def M_(fn, r=(), w=()):
        S.op("pe", fn, reads=r, writes=w)

    xa = S.sb("S_xa", [128, nK, T], BF16)
    b_xa = [S.buf("S_xa%d" % i) for i in range(nK)]
    st = [S.sb("S_st%d" % i, [128, T], F32) for i in range(3)]
    b_st = [S.buf("S_st%d" % i) for i in range(3)]
    G = Gemm(S, nK, nslots=3, nps=2, tag="S")
    W16, b_W16 = G.precast("ssm_W16", d["Wu"], nblk)
    cnt = {"st": 0}
    for tt in range(ntt):
        ts = slice(tt * T, (tt + 1) * T)
        S.dma("pool", xa[:], d["hnT"][:, ts].rearrange("(c p) t -> p c t", p=128), b_xa[0], writes=b_xa)

        def epi(nb, pb, pap, ts=ts):
            i = cnt["st"] % 3
            cnt["st"] += 1
            if i % 2 == 0:
                A_(lambda e: e.activation(out=st[i][:], in_=pap, func=AF.Copy), [pb], [b_st[i]])
            else:
                V(lambda e: e.tensor_copy(st[i][:], pap), [pb], [b_st[i]])
            S.dma("pool", uT_d[nb, :, ts], st[i][:], b_st[i], reads=[b_st[i]], writes=[b_ud])

        G.run(lambda kc: xa[:, kc, :], b_xa, lambda nb: W16[nb], nblk, T, epi, wbuf=b_W16)

    def sbt(name, shape, dt=F32):
        return S.sb("S_" + name, shape, dt), S.buf("S_" + name)

    idf, _ = sbt("idf", [128, 128])
    S.dma("sp", idf[:], d["identf"], b_c, writes=[b_c])
    pr, b_pr = sbt("pr", [128, 40])
    pri, b_pri = S.sb("S_pri", [128, 1], mybir.dt.int32), None
    pin, b_pin = sbt("pin", [128, 3])
    pb4, b_pb4 = sbt("pb4", [128, 4, 16])
    bb, b_bb = sbt("bb", [128, 2, 16])
    blk, b_blk = sbt("blk", [128, 4, 32])
    BBt, b_BBt = sbt("BBt", [32, 2, 128])
    dsk, b_dsk = sbt("dsk", [32, 1]); dd, b_dd = sbt("dd", [32, 32])
    cosT, b_cos = sbt("cosT", [128, T]); sinT, b_sin = sbt("sinT", [128, T]); rT, b_rT = sbt("rT", [128, T])
    ut = [sbt("u%d" % i, [32, T]) for i in range(2)]
    z1, b_z1 = sbt("z1", [128, T]); z2, b_z2 = sbt("z2", [128, T])
    zre, b_zre = sbt("zre", [128, T]); zim, b_zim = sbt("zim", [128, T])
    wre, b_wre = sbt("wre", [128, T]); wim, b_wim = sbt("wim", [128, T])
    x1, b_x1 = sbt("x1", [128, T]); x2, b_x2 = sbt("x2", [128, T])
    xre, b_xre = sbt("xre", [128, T]); xim, b_xim = sbt("xim", [128, T])
    car, b_car = sbt("car", [128, 2])
    g1, b_g1 = sbt("g1", [32, T]); g2, b_g2 = sbt("g2", [32, T])
    yo = [sbt("yo%d" % i, [32, T]) for i in range(2)]
    pbu = [(S.ps("S_pbr%d" % i), S.buf("S_pbr%d" % i, True)) for i in range(2)]
    pbi = [(S.ps("S_pbi%d" % i), S.buf("S_pbi%d" % i, True)) for i in range(2)]
    py, b_py = S.ps("S_py"), S.buf("S_py", True)
    pt_, b_pt = G.pt[0], G.pb[0]
    c = lambda i: pr[:, i:i + 1]
    it = 0
    for j in range(NPAIR):
        S.dma("sp", pin[:, 0:1], d["lre"][j], b_pin, writes=[b_pin])
        S.dma("sp", pin[:, 1:2], d["lim"][j], b_pin, writes=[b_pin])
        S.dma("sp", pin[:, 2:3], d["lst"][j], b_pin, writes=[b_pin])
        for q, key in enumerate(("bre", "bim", "cre", "cim")):
            S.dma("sp", pb4[:, q, :], d[key][j], b_pb4, writes=[b_pb4])
        S.dma("sp", dsk[:], d["dsk"][j], b_dsk, writes=[b_dsk])
        R_, W_ = [b_pin, b_pr], [b_pr]
        A_(lambda e: e.activation(out=c(0), in_=pin[:, 2:3], func=AF.Exp), R_, W_)
        V(lambda e: e.tensor_tensor(c(1), pin[:, 0:1], c(0), ALU.mult), R_, W_)
        A_(lambda e: e.activation(out=c(2), in_=c(1), func=AF.Exp), R_, W_)
        V(lambda e: e.tensor_tensor(c(3), pin[:, 1:2], c(0), ALU.mult), R_, W_)
        V(lambda e: e.tensor_scalar(c(4), c(3), 1.0 / TWO_PI, None, ALU.mult), R_, W_)
        V(lambda e: e.tensor_copy(pri[:], c(4)), R_, W_)
        V(lambda e: e.tensor_copy(c(5), pri[:]), R_, W_)
        V(lambda e: e.scalar_tensor_tensor(c(6), c(5), -TWO_PI, c(3), ALU.mult, ALU.add), R_, W_)
        V(lambda e: e.tensor_scalar(c(7), c(6), 0.5, None, ALU.mult), R_, W_)
        V(lambda e: e.tensor_scalar(c(8), c(7), -1.0, None, ALU.mult), R_, W_)
        V(lambda e: e.tensor_tensor(c(8), c(8), c(7), ALU.max), R_, W_)
        V(lambda e: e.tensor_scalar(c(8), c(8), -1.0, np.pi / 2, ALU.mult, ALU.add), R_, W_)
        A_(lambda e: e.activation(out=c(9), in_=c(7), func=AF.Sin), R_, W_)
        A_(lambda e: e.activation(out=c(10), in_=c(8), func=AF.Sin), R_, W_)
        V(lambda e: e.tensor_tensor(c(11), c(9), c(10), ALU.mult), R_, W_)
        V(lambda e: e.tensor_scalar(c(11), c(11), 2.0, None, ALU.mult), R_, W_)
        V(lambda e: e.tensor_tensor(c(12), c(10), c(10), ALU.mult), R_, W_)
        V(lambda e: e.tensor_tensor(c(13), c(9), c(9), ALU.mult), R_, W_)
        V(lambda e: e.tensor_tensor(c(12), c(12), c(13), ALU.subtract), R_, W_)
        V(lambda e: e.tensor_tensor(c(14), c(2), c(12), ALU.mult), R_, W_)
        V(lambda e: e.tensor_tensor(c(15), c(2), c(11), ALU.mult), R_, W_)
        V(lambda e: e.tensor_tensor(c(16), pin[:, 0:1], pin[:, 0:1], ALU.mult), R_, W_)
        V(lambda e: e.tensor_tensor(c(17), pin[:, 1:2], pin[:, 1:2], ALU.mult), R_, W_)
        V(lambda e: e.tensor_tensor(c(16), c(16), c(17), ALU.add), R_, W_)
        V(lambda e: e.reciprocal(c(16), c(16)), R_, W_)
        V(lambda e: e.tensor_scalar(c(17), c(14), -1.0, None, ALU.add), R_, W_)
        V(lambda e: e.tensor_tensor(c(18), c(17), pin[:, 0:1], ALU.mult), R_, W_)
        V(lambda e: e.tensor_tensor(c(19), c(15), pin[:, 1:2], ALU.mult), R_, W_)
        V(lambda e: e.tensor_tensor(c(18), c(18), c(19), ALU.add), R_, W_)
        V(lambda e: e.tensor_tensor(c(18), c(18), c(16), ALU.mult), R_, W_)
        V(lambda e: e.tensor_tensor(c(19), c(15), pin[:, 0:1], ALU.mult), R_, W_)
        V(lambda e: e.tensor_tensor(c(20), c(17), pin[:, 1:2], ALU.mult), R_, W_)
        V(lambda e: e.tensor_tensor(c(19), c(19), c(20), ALU.subtract), R_, W_)
        V(lambda e: e.tensor_tensor(c(19), c(19), c(16), ALU.mult), R_, W_)
        V(lambda e: e.tensor_scalar(c(20), c(19), -1.0, None, ALU.mult), R_, W_)
        R2 = [b_pr, b_pb4, b_bb]
        V(lambda e: e.tensor_scalar(bb[:, 0, :], pb4[:, 0, :], c(18), None, ALU.mult), R2, [b_bb])
        V(lambda e: e.scalar_tensor_tensor(bb[:, 0, :], pb4[:, 1, :], c(20), bb[:, 0, :], ALU.mult, ALU.add), R2, [b_bb])
        V(lambda e: e.tensor_scalar(bb[:, 1, :], pb4[:, 1, :], c(18), None, ALU.mult), R2, [b_bb])
        V(lambda e: e.scalar_tensor_tensor(bb[:, 1, :], pb4[:, 0, :], c(19), bb[:, 1, :], ALU.mult, ALU.add), R2, [b_bb])
        R3 = [b_bb, b_pb4, b_blk]
        V(lambda e: e.memset(blk[:], 0.0), R3, [b_blk])
        for q, (src, sgn) in enumerate(((bb[:, 0, :], 1.0), (bb[:, 1, :], 1.0), (pb4[:, 2, :], 1.0), (pb4[:, 3, :], -1.0))):
            V(lambda e, q=q, src=src, sgn=sgn: e.tensor_scalar(blk[0:64, q, 0:16], src[0:64, :], sgn, None, ALU.mult), R3, [b_blk])
            V(lambda e, q=q, src=src, sgn=sgn: e.tensor_scalar(blk[64:128, q, 16:32], src[64:128, :], sgn, None, ALU.mult), R3, [b_blk])
        for q in range(2):
            M_(lambda e, q=q: e.transpose(pt_[0:32, q * 128:(q + 1) * 128], blk[:, q, :], idf[:]), [b_blk, b_c], [b_pt])
        V(lambda e: e.tensor_copy(BBt[:].rearrange("p q s -> p (q s)"), pt_[0:32, 0:256]), [b_pt], [b_BBt])
        V(lambda e: e.tensor_scalar(dd[:], idf[0:32, 0:32], dsk[:, 0:1], None, ALU.mult), [b_dsk, b_c], [b_dd])
        RT = [b_pr, b_cos, b_sin]
        V(lambda e: e.tensor_copy(cosT[:, 0:1], c(12)), RT, [b_cos])
        V(lambda e: e.tensor_copy(sinT[:, 0:1], c(11)), RT, [b_sin])
        m = 1
        while m < T:
            V(lambda e, m=m: e.tensor_scalar(c(21), sinT[:, m - 1:m], -1.0, None, ALU.mult), RT, [b_pr])
            V(lambda e, m=m: e.tensor_scalar(cosT[:, m:2 * m], cosT[:, 0:m], cosT[:, m - 1:m], None, ALU.mult), RT, [b_cos])
            V(lambda e, m=m: e.scalar_tensor_tensor(cosT[:, m:2 * m], sinT[:, 0:m], c(21), cosT[:, m:2 * m], ALU.mult, ALU.add), RT, [b_cos])
            V(lambda e, m=m: e.tensor_scalar(sinT[:, m:2 * m], sinT[:, 0:m], cosT[:, m - 1:m], None, ALU.mult), RT, [b_sin])
            V(lambda e, m=m: e.scalar_tensor_tensor(sinT[:, m:2 * m], cosT[:, 0:m], sinT[:, m - 1:m], sinT[:, m:2 * m], ALU.mult, ALU.add), RT, [b_sin])
            m *= 2
        V(lambda e: e.memset(rT[:], 1.0), [b_rT], [b_rT])
        V(lambda e: e.tensor_scalar(rT[:], rT[:], c(2), None, ALU.mult), [b_pr, b_rT], [b_rT])
        blkno, prow = j // 4, 32 * (j % 4)
        for b in range(NB):
            for ti in range(SL // T):
                t0 = b * SL + ti * T
                par = it % 2
                it += 1
                u_, b_u = ut[par]
                S.dma("sp", u_[:], uT_d[blkno, prow:prow + 32, t0:t0 + T], b_u, reads=[b_ud], writes=[b_u])
                (pre, b_pre), (pim, b_pim) = pbu[par], pbi[par]
                M_(lambda e, u_=u_, pre=pre: e.matmul(pre[:, 0:T], BBt[:, 0, :], u_[:], start=True, stop=True), [b_BBt, b_u], [b_pre])
                M_(lambda e, u_=u_, pim=pim: e.matmul(pim[:, 0:T], BBt[:, 1, :], u_[:], start=True, stop=True), [b_BBt, b_u], [b_pim])
                V(lambda e, pre=pre: e.tensor_tensor(z1[:], pre[:, 0:T], cosT[:], ALU.mult), [b_pre, b_cos], [b_z1])
                V(lambda e, pim=pim: e.tensor_tensor(z2[:], pim[:, 0:T], sinT[:], ALU.mult), [b_pim, b_sin], [b_z2])
                P_(lambda e: e.tensor_tensor(zre[:], z1[:], z2[:], ALU.add), [b_z1, b_z2], [b_zre])
                V(lambda e, pim=pim: e.tensor_tensor(z1[:], pim[:, 0:T], cosT[:], ALU.mult), [b_pim, b_cos, b_zre], [b_z1])
                V(lambda e, pre=pre: e.tensor_tensor(z2[:], pre[:, 0:T], sinT[:], ALU.mult), [b_pre, b_sin, b_zre], [b_z2])
                P_(lambda e: e.tensor_tensor(zim[:], z1[:], z2[:], ALU.subtract), [b_z1, b_z2], [b_zim])
                if ti == 0:
                    V(lambda e: e.memset(car[:], 0.0), [b_car], [b_car])
                V(lambda e: e.tensor_tensor_scan(wre[:], rT[:], zre[:], car[:, 0:1], ALU.mult, ALU.add), [b_rT, b_zre, b_car], [b_wre])
                V(lambda e: e.tensor_tensor_scan(wim[:], rT[:], zim[:], car[:, 1:2], ALU.mult, ALU.add), [b_rT, b_zim, b_car], [b_wim])
                P_(lambda e: e.tensor_tensor(x1[:], wre[:], cosT[:], ALU.mult), [b_wre, b_cos], [b_x1])
                P_(lambda e: e.tensor_tensor(x2[:], wim[:], sinT[:], ALU.mult), [b_wim, b_sin], [b_x2])
                P_(lambda e: e.tensor_tensor(xre[:], x1[:], x2[:], ALU.subtract), [b_x1, b_x2], [b_xre])
                P_(lambda e: e.tensor_tensor(x1[:], wim[:], cosT[:], ALU.mult), [b_wim, b_cos, b_xre], [b_x1])
                P_(lambda e: e.tensor_tensor(x2[:], wre[:], sinT[:], ALU.mult), [b_wre, b_sin, b_xre], [b_x2])
                P_(lambda e: e.tensor_tensor(xim[:], x1[:], x2[:], ALU.add), [b_x1, b_x2], [b_xim])
                V(lambda e: e.tensor_copy(car[:, 0:1], xre[:, T - 1:T]), [b_xre, b_car], [b_car])
                V(lambda e: e.tensor_copy(car[:, 1:2], xim[:, T - 1:T]), [b_xim, b_car], [b_car])
                M_(lambda e: e.matmul(py[0:32, 0:T], blk[:, 2, :], xre[:], start=True, stop=False), [b_blk, b_xre], [b_py])
                M_(lambda e: e.matmul(py[0:32, 0:T], blk[:, 3, :], xim[:], start=False, stop=False), [b_blk, b_xim], [b_py])
                M_(lambda e, u_=u_: e.matmul(py[0:32, 0:T], dd[:], u_[:], start=False, stop=True), [b_dd, b_u], [b_py])
                yo_, b_yo = yo[par]
                A_(lambda e: e.activation(out=g1[:], in_=py[0:32, 0:T], func=AF.Square), [b_py], [b_g1])
                V(lambda e: e.tensor_scalar(g1[:], g1[:], 0.044715, 1.0, ALU.mult, ALU.add), [b_g1], [b_g1])
                V(lambda e: e.tensor_tensor(g1[:], g1[:], py[0:32, 0:T], ALU.mult), [b_g1, b_py], [b_g1])
                A_(lambda e: e.activation(out=g2[:], in_=g1[:], func=AF.Tanh, scale=float(np.sqrt(2.0 / np.pi))), [b_g1], [b_g2])
                V(lambda e: e.tensor_scalar(g2[:], g2[:], 1.0, 0.5, ALU.add, ALU.mult), [b_g2], [b_g2])
                V(lambda e, yo_=yo_: e.tensor_tensor(yo_[:], g2[:], py[0:32, 0:T], ALU.mult), [b_g2, b_py], [b_yo])
                S.dma("sp", d["oT"][j * 32:(j + 1) * 32, t0:t0 + T], yo_[:], b_yo, reads=[b_yo], writes=[b_od])
    return [b_od]


D_MODEL, BATCH, SEQ, DEPTH, PLE = 4096, 2, 8192, 4, 256
NTOK = BATCH * SEQ
NCB = 4
TT = NTOK // NCB
HEADS, W_MIX = 32, 4096
HG = NCORES // BATCH
NH_LOC = HEADS // HG
CW = W_MIX // HG


def _new_nc():
    return bass.Bass("TRN2", target_bir_lowering=False)


def _decl(nc, ins, outs):
    d = {}
    for name, (shape, dt) in ins.items():
        d[name] = nc.dram_tensor(name, list(shape), dt, kind="ExternalInput").ap()
    for name, (shape, dt) in outs.items():
        d[name] = nc.dram_tensor(name, list(shape), dt, kind="ExternalOutput").ap()
    return d


def _build(emit, ins, outs, *args, **kw):
    nc = _new_nc()
    d = _decl(nc, ins, outs)
    with contextlib.ExitStack() as st:
        S = Sched(nc, st)
        fin = emit(S, d, *args, **kw)
        S.finish(fin)
    return nc


def _launch(nc, in_maps):
    res = run_bass_kernel_spmd(nc, in_maps, core_ids=list(range(len(in_maps))))
    return res.results


def colvec(g):
    return np.ascontiguousarray(np.asarray(g, np.float32).reshape(-1, 128).T)


def fox_consts(NH, NB, nseg):
    NP = NH * NB * nseg
    m = np.zeros((NP, NP), np.float32)
    for h in range(NH):
        for b in range(NB):
            base = (h * NB + b) * nseg
            for s1 in range(nseg):
                m[base + s1, base + s1 + 1:base + nseg] = 1.0
    p = np.arange(128)
    tri = np.where(p[None, :] >= p[:, None], 0.0, -1e30).astype(np.float32)
    return {"mtri": m, "tri": tri, "identf": np.eye(128, dtype=np.float32),
            "identb": np.eye(128).astype(ml_dtypes.bfloat16)}


def gdn_consts():
    p = np.arange(128)
    mstr = np.where(p[:, None] > p[None, :], 0.0, -1e30).astype(np.float32)
    mup = np.where(p[None, :] >= p[:, None], 0.0, -1e30).astype(np.float32)
    return {"mstr4": np.tile(mstr, (1, 4)), "mup4": np.tile(mup, (1, 4)),
            "id4": np.tile(np.eye(128, dtype=np.float32), (1, 4)), "identf": np.eye(128, dtype=np.float32)}


def kernel(x, p, norm_mix, fox_w_in, fox_b_f, fox_w_out, gdn_w_in, gdn_conv, gdn_a_log,
           gdn_dt_bias, gdn_norm, gdn_w_out, ssm_w_in, ssm_lam_re, ssm_lam_im, ssm_b_re,
           ssm_b_im, ssm_c_re, ssm_c_im, ssm_log_step, ssm_d, ssm_w_glu, ssm_b_glu, ssm_w_out,
           norm_ple, ple_w_proj, ple_w_gate, final_norm):
    f32 = np.float32
    D, nK = D_MODEL, D_MODEL // 128
    x = np.asarray(x, f32)
    p = np.asarray(p, f32)
    tok = [slice(c * TT, (c + 1) * TT) for c in range(NCB)]
    hT = np.ascontiguousarray(x.reshape(NTOK, D).T)
    hT_sh = [np.ascontiguousarray(hT[:, t]) for t in tok]
    del hT
    wblk = (nK, 128, nK, 128)

    ncN = _build(emit_N, {"hT": ((D, TT), F32), "g": ((128, nK), F32)}, {"hnT_out": ((D, TT), BF16)}, D, TT, 512)
    g0 = colvec(norm_mix[0])
    r = _launch(ncN, [{"hT": hT_sh[c], "g": g0} for c in range(NCB)])
    hn_sh = [r[c]["hnT_out"] for c in range(NCB)]

    b_ins = {"hT": ((D, TT), F32), "hnT": ((D, TT), BF16), "oT": ((D, TT), F32), "pT": ((PLE, TT), F32),
             "g2": ((128, nK), F32), "gn": ((128, nK), F32), "Wz": (wblk, F32), "Wout": (wblk, F32),
             "Wg": (wblk, F32), "Wp": ((nK, 128, PLE // 128, 128), F32)}
    progs = {}
    out = None
    for i in range(DEPTH):
        kind, j = i % 3, i // 3
        hn_full = np.concatenate(hn_sh, axis=1)
        hn_b = [np.ascontiguousarray(hn_full[:, bi * SEQ:(bi + 1) * SEQ]) for bi in range(BATCH)]
        if kind == 0:
            w_in = np.asarray(fox_w_in[j], f32)
            if "fox" not in progs:
                NP = NH_LOC * (SEQ // 512)
                progs["fox"] = _build(
                    emit_Afox,
                    {"hnT": ((D, SEQ), BF16), "W": ((3 * NH_LOC + 1, 128, nK, 128), F32), "bf": ((NH_LOC, 1), F32),
                     "mtri": ((NP, NP), F32), "tri": ((128, 128), F32), "identf": ((128, 128), F32),
                     "identb": ((128, 128), BF16)},
                    {"oT": ((NH_LOC * 128, SEQ), F32)}, D, 1, SEQ, NH_LOC)
            cst = fox_consts(NH_LOC, 1, SEQ // 512)
            maps = []
            for c in range(NCORES):
                bi, hg = c // HG, c % HG
                cs = slice(CW * hg, CW * (hg + 1))
                wf = np.zeros((D, 128), f32)
                wf[:, :NH_LOC] = w_in[:, 4 * W_MIX + NH_LOC * hg: 4 * W_MIX + NH_LOC * (hg + 1)]
                Wc = np.concatenate([w_in[:, 0 * W_MIX:1 * W_MIX][:, cs], w_in[:, 1 * W_MIX:2 * W_MIX][:, cs],
                                     w_in[:, 2 * W_MIX:3 * W_MIX][:, cs], wf], axis=1)
                mm = {"hnT": hn_b[bi], "W": blockw(Wc),
                      "bf": np.asarray(fox_b_f[j], f32)[NH_LOC * hg:NH_LOC * (hg + 1)].reshape(NH_LOC, 1)}
                mm.update(cst)
                maps.append(mm)
            r = _launch(progs["fox"], maps)
            Wz = blockw(w_in[:, 3 * W_MIX:4 * W_MIX])
            Wout = blockw(np.asarray(fox_w_out[j], f32))
        elif kind == 1:
            w_in = np.asarray(gdn_w_in[j], f32)
            if "gdn" not in progs:
                progs["gdn"] = _build(
                    emit_Agdn,
                    {"hnT": ((D, SEQ), BF16), "W": ((3 * NH_LOC + 1, 128, nK, 128), F32), "cw": ((3 * NH_LOC, 128, 4), F32),
                     "gbias": ((2 * NH_LOC, 1), F32), "galog": ((2 * NH_LOC, 1), F32), "nw": ((128,), F32),
                     "mstr4": ((128, 512), F32), "mup4": ((128, 512), F32), "id4": ((128, 512), F32),
                     "identf": ((128, 128), F32)},
                    {"oT": ((NH_LOC * 128, SEQ), F32)}, D, 1, SEQ, NH_LOC)
            cst = gdn_consts()
            conv = np.asarray(gdn_conv[j], f32)
            maps = []
            for c in range(NCORES):
                bi, hg = c // HG, c % HG
                cs = slice(CW * hg, CW * (hg + 1))
                hs = slice(NH_LOC * hg, NH_LOC * (hg + 1))
                wg = np.zeros((D, 128), f32)
                wg[:, :NH_LOC] = w_in[:, 4 * W_MIX:4 * W_MIX + HEADS][:, hs]
                wg[:, NH_LOC:2 * NH_LOC] = w_in[:, 4 * W_MIX + HEADS:4 * W_MIX + 2 * HEADS][:, hs]
                Wc = np.concatenate([w_in[:, 0 * W_MIX:1 * W_MIX][:, cs], w_in[:, 1 * W_MIX:2 * W_MIX][:, cs],
                                     w_in[:, 2 * W_MIX:3 * W_MIX][:, cs], wg], axis=1)
                cwc = np.concatenate([conv[:, 0 * W_MIX:1 * W_MIX][:, cs], conv[:, 1 * W_MIX:2 * W_MIX][:, cs],
                                      conv[:, 2 * W_MIX:3 * W_MIX][:, cs]], axis=1)
                zer = np.zeros(NH_LOC, f32)
                mm = {"hnT": hn_b[bi], "W": blockw(Wc), "cw": np.ascontiguousarray(cwc.T.reshape(3 * NH_LOC, 128, 4)),
                      "gbias": np.concatenate([zer, np.asarray(gdn_dt_bias[j], f32)[hs]]).reshape(-1, 1),
                      "galog": np.concatenate([zer, np.asarray(gdn_a_log[j], f32)[hs]]).reshape(-1, 1),
                      "nw": np.asarray(gdn_norm[j], f32)}
                mm.update(cst)
                maps.append(mm)
            r = _launch(progs["gdn"], maps)
            Wz = blockw(w_in[:, 3 * W_MIX:4 * W_MIX])
            Wout = blockw(np.asarray(gdn_w_out[j], f32))
        else:
            w_in = np.asarray(ssm_w_in[j], f32)
            NPAIR = CW // 32
            if "ssm" not in progs:
                ins = {"hnT": ((D, SEQ), BF16), "Wu": ((NPAIR // 4, 128, nK, 128), F32), "identf": ((128, 128), F32),
                       "dsk": ((NPAIR, 32, 1), F32)}
                for k in ("lre", "lim", "lst"):
                    ins[k] = ((NPAIR, 128, 1), F32)
                for k in ("bre", "bim", "cre", "cim"):
                    ins[k] = ((NPAIR, 128, 16), F32)
                progs["ssm"] = _build(emit_Assm, ins, {"oT": ((NPAIR * 32, SEQ), F32)}, D, 1, SEQ, NPAIR)
            maps = []
            for c in range(NCORES):
                bi, hg = c // HG, c % HG
                cs = slice(CW * hg, CW * (hg + 1))
                gs = slice(2 * NPAIR * hg, 2 * NPAIR * (hg + 1))
                mm = {"hnT": hn_b[bi], "Wu": blockw(w_in[:, :W_MIX][:, cs]), "identf": np.eye(128, dtype=f32),
                      "lre": np.ascontiguousarray(np.asarray(ssm_lam_re[j], f32)[gs]).reshape(NPAIR, 128, 1),
                      "lim": np.ascontiguousarray(np.asarray(ssm_lam_im[j], f32)[gs]).reshape(NPAIR, 128, 1),
                      "lst": np.repeat(np.asarray(ssm_log_step[j], f32)[gs], 64).reshape(NPAIR, 128, 1),
                      "bre": np.ascontiguousarray(np.asarray(ssm_b_re[j], f32)[gs]).reshape(NPAIR, 128, 16),
                      "bim": np.ascontiguousarray(np.asarray(ssm_b_im[j], f32)[gs]).reshape(NPAIR, 128, 16),
                      "cre": np.ascontiguousarray(np.asarray(ssm_c_re[j], f32)[gs].transpose(0, 2, 1)).reshape(NPAIR, 128, 16),
                      "cim": np.ascontiguousarray(np.asarray(ssm_c_im[j], f32)[gs].transpose(0, 2, 1)).reshape(NPAIR, 128, 16),
                      "dsk": np.ascontiguousarray(np.asarray(ssm_d[j], f32)[cs]).reshape(NPAIR, 32, 1)}
                maps.append(mm)
            r = _launch(progs["ssm"], maps)
            Wz = blockw(w_in[:, W_MIX:2 * W_MIX])
            Wout = blockw(np.asarray(ssm_w_out[j], f32))
        del hn_full, hn_b
        oT_full = np.concatenate([np.concatenate([r[bi * HG + hg]["oT"] for hg in range(HG)], axis=0)
                                  for bi in range(BATCH)], axis=1)
        final = i == DEPTH - 1
        ssm = kind == 2
        key = ("B", ssm, final)
        if key not in progs:
            ins = dict(b_ins)
            if ssm:
                ins["Wglu"] = (wblk, F32)
                ins["bglu"] = ((128, nK), F32)
            outs = {"out": ((D, TT), F32)} if final else {"hT_out": ((D, TT), F32), "hnT_out": ((D, TT), BF16)}
            progs[key] = _build(emit_B, ins, outs, D, TT, 512, ssm=ssm, final=final, PD=PLE)
        pT = np.ascontiguousarray(p[i].reshape(NTOK, PLE).T)
        shared = {"g2": colvec(norm_ple[i]), "gn": colvec(final_norm if final else norm_mix[i + 1]),
                  "Wz": Wz, "Wout": Wout, "Wg": blockw(np.asarray(ple_w_gate[i], f32)),
                  "Wp": blockw(np.asarray(ple_w_proj[i], f32))}
        if ssm:
            shared["Wglu"] = blockw(np.asarray(ssm_w_glu[j], f32))
            shared["bglu"] = colvec(ssm_b_glu[j])
        maps = []
        for c in range(NCB):
            mm = {"hT": hT_sh[c], "hnT": hn_sh[c], "oT": np.ascontiguousarray(oT_full[:, tok[c]]),
                  "pT": np.ascontiguousarray(pT[:, tok[c]])}
            mm.update(shared)
            maps.append(mm)
        del oT_full
        r = _launch(progs[key], maps)
        if final:
            outT = np.concatenate([r[c]["out"] for c in range(NCB)], axis=1)
            out = np.ascontiguousarray(outT.T).reshape(BATCH, SEQ, D)
        else:
            hT_sh = [r[c]["hT_out"] for c in range(NCB)]
            hn_sh = [r[c]["hnT_out"] for c in range(NCB)]
    return out.astype(np.float32)
```

```python
import contextlib
import numpy as np
import ml_dtypes
import concourse.bass as bass
import concourse.mybir as mybir
from concourse.bass_utils import run_bass_kernel_spmd

F32 = mybir.dt.float32
BF16 = mybir.dt.bfloat16
AF = mybir.ActivationFunctionType
ALU = mybir.AluOpType
AX = mybir.AxisListType
NCORES = 8


class Buf:
    __slots__ = ("name", "w", "r", "sem", "ndma", "excl")

    def __init__(self, name, excl=False):
        self.name = name
        self.excl = excl
        self.w = {}
        self.r = {}
        self.sem = None
        self.ndma = 0


def _merge(dst, src):
    for k, v in src.items():
        if dst.get(k, 0) < v:
            dst[k] = v


class Sched:
    ENG = ("pe", "act", "dve", "pool", "sp")

    def __init__(self, nc, stack):
        self.nc = nc
        self.stack = stack
        self.prog = {e: [] for e in self.ENG}
        self.count = {e: 0 for e in self.ENG}
        self.seen = {e: {} for e in self.ENG}
        self.sem = {}
        for e in ("pe", "act", "dve", "pool"):
            self.sem[e] = stack.enter_context(nc.semaphore("c_" + e))
        self.nbuf = 0
        self.slots = []

    def sb(self, name, shape, dt):
        return self.stack.enter_context(self.nc.sbuf_tensor(name, list(shape), dt))

    def ps(self, name, shape=(128, 512), dt=F32):
        return self.stack.enter_context(self.nc.psum_tensor(name, list(shape), dt))

    def buf(self, name=None, excl=False):
        self.nbuf += 1
        return Buf(name or "b%d" % self.nbuf, excl)

    def _waits(self, eng, deps, skip_self=False):
        out = []
        seen = self.seen[eng]
        for sem, v in deps.items():
            if skip_self and sem is self.sem.get(eng):
                continue
            if seen.get(sem, 0) < v:
                seen[sem] = v
                out.append((sem, v))
        return out

    def _deps(self, reads, writes):
        deps = {}
        for b in reads:
            _merge(deps, b.w)
            if b.excl:
                _merge(deps, b.r)
        for b in writes:
            _merge(deps, b.w)
            _merge(deps, b.r)
        return deps

    def op(self, eng, fn, reads=(), writes=()):
        deps = self._deps(reads, writes)
        waits = self._waits(eng, deps, skip_self=(eng == "pe"))
        self.count[eng] += 1
        tok = {self.sem[eng]: self.count[eng]}
        self.prog[eng].append((waits, fn, (self.sem[eng], 1)))
        for b in reads:
            _merge(b.r, tok)
        for b in writes:
            b.w = dict(tok)
            b.r = {}

    def dma(self, q, out_ap, in_ap, slot, reads=(), writes=(), **kw):
        if slot.sem is None:
            slot.sem = self.stack.enter_context(self.nc.semaphore("d_" + slot.name))
            self.slots.append(slot)
        deps = self._deps(reads, writes)
        waits = self._waits(q, deps)
        slot.ndma += 1
        tok = {slot.sem: 16 * slot.ndma}

        def fn(e, out_ap=out_ap, in_ap=in_ap, kw=kw):
            return e.dma_start(out=out_ap, in_=in_ap, **kw)

        self.prog[q].append((waits, fn, (slot.sem, 16)))
        for b in reads:
            _merge(b.r, tok)
        for b in writes:
            b.w = dict(tok)
            b.r = {}

    def finish(self, final_bufs):
        deps = {}
        for b in final_bufs:
            _merge(deps, b.w)
        for sl in self.slots:
            _merge(deps, {sl.sem: 16 * sl.ndma})
        fin = self._waits("sp", deps)
        self.prog["sp"].append((fin, None, None))
        nc = self.nc
        prog = self.prog

        def play(e, lst):
            for waits, fn, inc in lst:
                for sem, v in waits:
                    e.wait_ge(sem, v)
                if fn is not None:
                    ins = fn(e)
                    ins.then_inc(inc[0], inc[1])

        with nc.Block() as block:
            @block.tensor
            def _(e):
                play(e, prog["pe"])

            @block.scalar
            def _(e):
                play(e, prog["act"])

            @block.vector
            def _(e):
                play(e, prog["dve"])

            @block.gpsimd
            def _(e):
                play(e, prog["pool"])

            @block.sync
            def _(e):
                play(e, prog["sp"])


class Gemm:
    def __init__(self, S, nK, nslots=3, nps=2, tag="g"):
        self.S = S
        self.nK = nK
        self.wt = [S.sb("%s_w%d" % (tag, i), [128, nK, 128], BF16) for i in range(nslots)]
        self.wb = [S.buf("%s_wb%d" % (tag, i)) for i in range(nslots)]
        self.pt = [S.ps("%s_p%d" % (tag, i)) for i in range(nps)]
        self.pb = [S.buf("%s_pb%d" % (tag, i), True) for i in range(nps)]
        self.wi = 0
        self.pi = 0

    def precast(self, name, w_ap, nblk, nK=None):
        S = self.S
        nK = nK or self.nK
        wb16 = S.nc.dram_tensor(name, [nblk, 128, nK, 128], BF16).ap()
        buf = S.buf(name)
        for nb in range(nblk):
            S.dma("pool", wb16[nb], w_ap[nb], buf)
        buf.w = {buf.sem: 16 * buf.ndma}
        return wb16, buf

    def run(self, at, at_bufs, wsrc, nblocks, T, epi, nK=None, wq="sp", wbuf=None):
        S = self.S
        nK = nK or self.nK
        for nb in range(nblocks):
            wi = self.wi % len(self.wt)
            self.wi += 1
            wt, wb = self.wt[wi], self.wb[wi]
            S.dma(wq, wt[:, 0:nK, :], wsrc(nb), wb, reads=[wbuf] if wbuf is not None else [], writes=[wb])
            pi = self.pi % len(self.pt)
            self.pi += 1
            pt, pb = self.pt[pi], self.pb[pi]
            for kc in range(nK):
                def mm(e, kc=kc, wt=wt, pt=pt):
                    return e.matmul(pt[:, 0:T], wt[:, kc, :], at(kc),
                                    start=(kc == 0), stop=(kc == nK - 1))
                S.op("pe", mm, reads=[wb, at_bufs[kc]], writes=[pb])
            epi(nb, pb, pt[:, 0:T])


def blockw(W, nblk=None):
    K, N = W.shape
    return np.ascontiguousarray(
        W.reshape(K // 128, 128, N // 128, 128).transpose(2, 1, 0, 3))


def emit_B(S, d, D, TT, T, ssm=False, final=False, PD=256):
    nK = D // 128
    nP = PD // 128
    nt = TT // T
    h = S.sb("B_h", [128, nK, T], F32)
    xa = S.sb("B_xa", [128, nK, T], BF16)
    xb = S.sb("B_xb", [128, nK, T], BF16)
    xc = S.sb("B_xc", [128, nK, T], BF16) if ssm else None
    pt = S.sb("B_pT", [128, nP, T], BF16)
    ones = S.sb("B_ones", [128, 128], BF16)
    g2 = S.sb("B_g2", [128, nK], F32)
    gn = S.sb("B_gn", [128, nK], F32)
    bglu = S.sb("B_bglu", [128, nK], F32) if ssm else None
    rstd = S.sb("B_rstd", [128, T], F32)
    och = [S.sb("B_och%d" % i, [128, T], F32) for i in range(2)]
    t1 = [S.sb("B_t1_%d" % i, [128, T], F32) for i in range(2)]
    t2 = [S.sb("B_t2_%d" % i, [128, T], F32) for i in range(2)]
    sq = [S.sb("B_sq%d" % i, [128, T], BF16) for i in range(2)]
    wp = [S.sb("B_wp%d" % i, [128, nP, 128], BF16) for i in range(2)]
    fo = [S.sb("B_fo%d" % i, [128, T], F32) for i in range(2)] if final else None
    b_h = [S.buf("h%d" % i) for i in range(nK)]
    b_xa = [S.buf("xa%d" % i) for i in range(nK)]
    b_xb = [S.buf("xb%d" % i) for i in range(nK)]
    b_xc = [S.buf("xc%d" % i) for i in range(nK)]
    b_pt = S.buf("pt")
    b_const, b_rstd = S.buf("const"), S.buf("rstd")
    b_och = [S.buf("och%d" % i) for i in range(2)]
    b_t1 = [S.buf("t1_%d" % i) for i in range(2)]
    b_t2 = [S.buf("t2_%d" % i) for i in range(2)]
    b_sq = [S.buf("sq%d" % i) for i in range(2)]
    b_wp = [S.buf("wp%d" % i) for i in range(2)]
    b_fo = [S.buf("fo%d" % i) for i in range(2)]
    b_hd, b_hn = S.buf("hdram"), S.buf("hndram")
    pn = S.ps("B_pn")
    b_pn = S.buf("pn", True)
    pp = [S.ps("B_pp%d" % i) for i in range(2)]
    b_pp = [S.buf("pp%d" % i, True) for i in range(2)]
    G = Gemm(S, nK, nslots=3, nps=4 if ssm else 3, tag="B")
    Wz16, b_Wz = G.precast("B_Wz16", d["Wz"], nK)
    if ssm:
        Wglu16, b_Wglu = G.precast("B_Wglu16", d["Wglu"], nK)
    Wout16, b_Wout = G.precast("B_Wout16", d["Wout"], nK)
    Wg16, b_Wg = G.precast("B_Wg16", d["Wg"], nK)
    Wp16, b_Wp = G.precast("B_Wp16", d["Wp"], nK, nK=nP)

    S.op("pool", lambda e: e.memset(ones[:], 1.0), writes=[b_const])
    S.dma("pool", g2[:], d["g2"], b_const, writes=[b_const])
    S.dma("pool", gn[:], d["gn"], b_const, writes=[b_const])
    if ssm:
        S.dma("pool", bglu[:], d["bglu"], b_const, writes=[b_const])

    cnt = {"o": 0, "t": 0, "wp": 0}

    def norm_finish(b_src):
        S.op("dve", lambda e: e.tensor_scalar(rstd[:], pn[:, 0:T], 1.0 / D, 1e-6, ALU.mult, ALU.add),
             reads=[b_pn], writes=[b_rstd])
        S.op("act", lambda e: e.activation(out=rstd[:], in_=rstd[:], func=AF.Sqrt),
             reads=[b_rstd], writes=[b_rstd])
        S.op("dve", lambda e: e.reciprocal(rstd[:], rstd[:]), reads=[b_rstd], writes=[b_rstd])

    def sq_accum(nb):
        i = cnt["t"] % 2
        S.op("act", lambda e: e.activation(out=sq[i][:], in_=h[:, nb, :], func=AF.Square),
             reads=[b_h[nb]], writes=[b_sq[i]])
        S.op("pe", lambda e: e.matmul(pn[:, 0:T], ones[:], sq[i][:], start=(nb == 0), stop=(nb == nK - 1)),
             reads=[b_sq[i], b_const], writes=[b_pn])

    for ti in range(nt):
        ts = slice(ti * T, (ti + 1) * T)
        S.dma("pool", h[:], d["hT"][:, ts].rearrange("(c p) t -> p c t", p=128), b_h[0],
              reads=[b_hd], writes=b_h)
        S.dma("pool", xa[:], d["hnT"][:, ts].rearrange("(c p) t -> p c t", p=128), b_xa[0], writes=b_xa)
        S.dma("pool", pt[:], d["pT"][:, ts].rearrange("(c p) t -> p c t", p=128), b_pt, writes=[b_pt])
        if ssm:
            S.dma("pool", xc[:], d["oT"][:, ts].rearrange("(c p) t -> p c t", p=128), b_xc[0], writes=b_xc)

        def epi1(nb, pb, pap):
            i = cnt["o"] % 2
            cnt["o"] += 1
            S.dma("pool", och[i][:], d["oT"][nb * 128:(nb + 1) * 128, ts], b_och[i], writes=[b_och[i]])
            j = cnt["t"] % 2
            cnt["t"] += 1
            S.op("act", lambda e: e.activation(out=t1[j][:], in_=pap, func=AF.Silu), reads=[pb], writes=[b_t1[j]])
            S.op("dve", lambda e: e.tensor_tensor(xb[:, nb, :], t1[j][:], och[i][:], ALU.mult),
                 reads=[b_t1[j], b_och[i]], writes=[b_xb[nb]])

        def epi1_ssm_glu(nb, pb, pap):
            j = cnt["t"] % 2
            S.op("act", lambda e: e.activation(out=t2[j][:], in_=pap, func=AF.Sigmoid, bias=bglu[:, nb:nb + 1]),
                 reads=[pb, b_const], writes=[b_t2[j]])

        def epi1_ssm_z(nb, pb, pap):
            i = cnt["o"] % 2
            cnt["o"] += 1
            S.dma("pool", och[i][:], d["oT"][nb * 128:(nb + 1) * 128, ts], b_och[i], writes=[b_och[i]])
            j = cnt["t"] % 2
            cnt["t"] += 1
            S.op("act", lambda e: e.activation(out=t1[j][:], in_=pap, func=AF.Silu), reads=[pb], writes=[b_t1[j]])
            S.op("dve", lambda e: e.tensor_tensor(t1[j][:], t1[j][:], t2[j][:], ALU.mult),
                 reads=[b_t1[j], b_t2[j]], writes=[b_t1[j]])
            S.op("dve", lambda e: e.tensor_tensor(xb[:, nb, :], t1[j][:], och[i][:], ALU.mult),
                 reads=[b_t1[j], b_och[i]], writes=[b_xb[nb]])

        if not ssm:
            G.run(lambda kc: xa[:, kc, :], b_xa, lambda nb: Wz16[nb], nK, T, epi1, wbuf=b_Wz)
        else:
            for nb in range(nK):
                G.run(lambda kc: xc[:, kc, :], b_xc, lambda _, nb=nb: Wglu16[nb], 1, T,
                      lambda _, pb, pap, nb=nb: epi1_ssm_glu(nb, pb, pap), wbuf=b_Wglu)
                G.run(lambda kc: xa[:, kc, :], b_xa, lambda _, nb=nb: Wz16[nb], 1, T,
                      lambda _, pb, pap, nb=nb: epi1_ssm_z(nb, pb, pap), wbuf=b_Wz)

        def epi2(nb, pb, pap):
            S.op("dve", lambda e: e.tensor_tensor(h[:, nb, :], h[:, nb, :], pap, ALU.add),
                 reads=[pb, b_h[nb]], writes=[b_h[nb]])
            sq_accum(nb)
            cnt["t"] += 1

        G.run(lambda kc: xb[:, kc, :], b_xb, lambda nb: Wout16[nb], nK, T, epi2, wbuf=b_Wout)
        norm_finish(None)
        for nb in range(nK):
            S.op("dve", lambda e, nb=nb: e.scalar_tensor_tensor(xa[:, nb, :], h[:, nb, :], g2[:, nb:nb + 1], rstd[:],
                                                                 ALU.mult, ALU.mult),
                 reads=[b_h[nb], b_rstd, b_const], writes=[b_xa[nb]])

        def epi3(nb, pb, pap):
            w = cnt["wp"] % 2
            cnt["wp"] += 1
            S.dma("sp", wp[w][:], Wp16[nb], b_wp[w], reads=[b_Wp], writes=[b_wp[w]])
            for kc in range(nP):
                S.op("pe", lambda e, kc=kc: e.matmul(pp[w][:, 0:T], wp[w][:, kc, :], pt[:, kc, :],
                                                     start=(kc == 0), stop=(kc == nP - 1)),
                     reads=[b_wp[w], b_pt], writes=[b_pp[w]])
            j = cnt["t"] % 2
            S.op("act", lambda e: e.activation(out=t1[j][:], in_=pap, func=AF.Sigmoid), reads=[pb], writes=[b_t1[j]])
            S.op("dve", lambda e: e.tensor_tensor(t1[j][:], t1[j][:], pp[w][:, 0:T], ALU.mult),
                 reads=[b_t1[j], b_pp[w]], writes=[b_t1[j]])
            S.op("dve", lambda e: e.tensor_tensor(h[:, nb, :], h[:, nb, :], t1[j][:], ALU.add),
                 reads=[b_t1[j], b_h[nb]], writes=[b_h[nb]])
            sq_accum(nb)
            cnt["t"] += 1

        G.run(lambda kc: xa[:, kc, :], b_xa, lambda nb: Wg16[nb], nK, T, epi3, wbuf=b_Wg)
        norm_finish(None)
        if not final:
            for nb in range(nK):
                S.op("dve", lambda e, nb=nb: e.scalar_tensor_tensor(xb[:, nb, :], h[:, nb, :], gn[:, nb:nb + 1], rstd[:],
                                                                     ALU.mult, ALU.mult),
                     reads=[b_h[nb], b_rstd, b_const], writes=[b_xb[nb]])
            S.dma("pool", d["hnT_out"][:, ts].rearrange("(c p) t -> p c t", p=128), xb[:], b_xb[0],
                  reads=b_xb, writes=[b_hn])
            S.dma("pool", d["hT_out"][:, ts].rearrange("(c p) t -> p c t", p=128), h[:], b_h[0],
                  reads=b_h, writes=[b_hd])
        else:
            for nb in range(nK):
                i = nb % 2
                S.op("dve", lambda e, nb=nb, i=i: e.scalar_tensor_tensor(fo[i][:], h[:, nb, :], gn[:, nb:nb + 1], rstd[:],
                                                                          ALU.mult, ALU.mult),
                     reads=[b_h[nb], b_rstd, b_const], writes=[b_fo[i]])
                S.dma("pool", d["out"][nb * 128:(nb + 1) * 128, ts], fo[i][:], b_fo[i], reads=[b_fo[i]], writes=[b_hn])
    return [b_hd, b_hn]


def emit_N(S, d, D, TT, T):
    nK = D // 128
    h = S.sb("N_h", [128, nK, T], F32)
    xo = S.sb("N_xo", [128, nK, T], BF16)
    sq = [S.sb("N_sq%d" % i, [128, T], BF16) for i in range(2)]
    ones = S.sb("N_ones", [128, 128], BF16)
    g = S.sb("N_g", [128, nK], F32)
    rstd = S.sb("N_rstd", [128, T], F32)
    pn = S.ps("N_pn")
    b_h, b_xo, b_c, b_r, b_out = (S.buf() for _ in range(5))
    b_pn = S.buf("Npn", True)
    b_sq = [S.buf(), S.buf()]
    S.op("pool", lambda e: e.memset(ones[:], 1.0), writes=[b_c])
    S.dma("sp", g[:], d["g"], b_c, writes=[b_c])
    for ti in range(TT // T):
        ts = slice(ti * T, (ti + 1) * T)
        S.dma("sp", h[:], d["hT"][:, ts].rearrange("(c p) t -> p c t", p=128), b_h, writes=[b_h])
        for nb in range(nK):
            i = nb % 2
            S.op("act", lambda e, nb=nb, i=i: e.activation(out=sq[i][:], in_=h[:, nb, :], func=AF.Square),
                 reads=[b_h], writes=[b_sq[i]])
            S.op("pe", lambda e, nb=nb, i=i: e.matmul(pn[:, 0:T], ones[:], sq[i][:], start=(nb == 0), stop=(nb == nK - 1)),
                 reads=[b_sq[i], b_c], writes=[b_pn])
        S.op("dve", lambda e: e.tensor_scalar(rstd[:], pn[:, 0:T], 1.0 / D, 1e-6, ALU.mult, ALU.add),
             reads=[b_pn], writes=[b_r])
        S.op("act", lambda e: e.activation(out=rstd[:], in_=rstd[:], func=AF.Sqrt), reads=[b_r], writes=[b_r])
        S.op("dve", lambda e: e.reciprocal(rstd[:], rstd[:]), reads=[b_r], writes=[b_r])
        for nb in range(nK):
            S.op("dve", lambda e, nb=nb: e.scalar_tensor_tensor(xo[:, nb, :], h[:, nb, :], g[:, nb:nb + 1], rstd[:],
                                                                 ALU.mult, ALU.mult),
                 reads=[b_h, b_r, b_c], writes=[b_xo])
        S.dma("sp", d["hnT_out"][:, ts].rearrange("(c p) t -> p c t", p=128), xo[:], b_xo, reads=[b_xo], writes=[b_out])
    return [b_out]


def emit_Afox(S, d, D, NB, SL, NH):
    nc = S.nc
    nK = D // 128
    T = 512
    nseg = SL // T
    ntt = NB * nseg
    NP = NH * ntt
    assert NP <= 128
    nblk = SL // 128
    scale = 128 ** -0.5
    qkv_d = nc.dram_tensor("fox_qkv", [3 * NH, 128, NB * SL], BF16).ap()
    lf_d = nc.dram_tensor("fox_lf", [NH, ntt, T], F32).ap()
    cum_d = nc.dram_tensor("fox_cum", [NH, ntt, T], F32).ap()
    b_qkv, b_lf, b_cum, b_od = S.buf("qkv_d"), S.buf("lf_d"), S.buf("cum_d"), S.buf("oT_d")

    xa = S.sb("A_xa", [128, nK, T], BF16)
    b_xa = [S.buf("A_xa%d" % i) for i in range(nK)]
    st = [S.sb("A_st%d" % i, [128, T], BF16) for i in range(3)]
    b_st = [S.buf("A_st%d" % i) for i in range(3)]
    lft = [S.sb("A_lf%d" % i, [NH, T], F32) for i in range(2)]
    b_lft = [S.buf("A_lft%d" % i) for i in range(2)]
    negb = S.sb("A_negb", [NH, 1], F32)
    b_c = S.buf("A_const")
    S.dma("pool", negb[:], d["bf"], b_c, writes=[b_c])
    S.op("dve", lambda e: e.tensor_scalar(negb[:], negb[:], -1.0, None, ALU.mult), reads=[b_c], writes=[b_c])
    G = Gemm(S, nK, nslots=3, nps=2, tag="A")
    W16, b_W16 = G.precast("fox_W16", d["W"], 3 * NH + 1)
    cnt = {"st": 0, "lf": 0}
    for tt in range(ntt):
        ts = slice(tt * T, (tt + 1) * T)
        S.dma("pool", xa[:], d["hnT"][:, ts].rearrange("(c p) t -> p c t", p=128), b_xa[0], writes=b_xa)

        def epi(nb, pb, pap, tt=tt, ts=ts):
            if nb < 3 * NH:
                i = cnt["st"] % 3
                cnt["st"] += 1
                if i % 2 == 0:
                    S.op("act", lambda e: e.activation(out=st[i][:], in_=pap, func=AF.Copy), reads=[pb], writes=[b_st[i]])
                else:
                    S.op("dve", lambda e: e.tensor_copy(st[i][:], pap), reads=[pb], writes=[b_st[i]])
                S.dma("pool", qkv_d[nb, :, ts], st[i][:], b_st[i], reads=[b_st[i]], writes=[b_qkv])
            else:
                i = cnt["lf"] % 2
                cnt["lf"] += 1
                S.op("act", lambda e: e.activation(out=lft[i][:], in_=pap[0:NH, :], func=AF.Exp, scale=-1.0,
                                                   bias=negb[:, 0:1]), reads=[pb, b_c], writes=[b_lft[i]])
                S.op("act", lambda e: e.activation(out=lft[i][:], in_=lft[i][:], func=AF.Ln, bias=1.0),
                     reads=[b_lft[i]], writes=[b_lft[i]])
                S.dma("pool", lf_d[:, tt, :], lft[i][:], b_lft[i], reads=[b_lft[i]], writes=[b_lf])

        G.run(lambda kc: xa[:, kc, :], b_xa, lambda nb: W16[nb], 3 * NH + 1, T, epi, wbuf=b_W16)

    LF = S.sb("A_LF", [NP, T], F32)
    onesf = S.sb("A_onesf", [NP, T], F32)
    mtri = S.sb("A_mtri", [NP, NP], F32)
    tot = S.sb("A_tot", [NP, 1], F32)
    off = S.sb("A_off", [NP, 1], F32)
    b_LF, b_m = S.buf("LF"), S.buf("m")
    ptr = [S.ps("A_ptr%d" % i) for i in range(2)]
    b_ptr = [S.buf("A_ptr%d" % i, True) for i in range(2)]
    S.dma("sp", LF[:], lf_d.rearrange("h t f -> (h t) f"), b_LF, reads=[b_lf], writes=[b_LF])
    S.dma("sp", mtri[:], d["mtri"], b_m, writes=[b_m])
    S.op("pool", lambda e: e.memset(onesf[:], 1.0), writes=[b_m])
    LC = S.sb("A_LC", [NP, T], F32)
    b_LC = S.buf("LC")
    S.op("dve", lambda e: e.tensor_tensor_scan(LC[:], onesf[:], LF[:], 0.0, ALU.mult, ALU.subtract),
         reads=[b_LF, b_m], writes=[b_LC])
    S.op("dve", lambda e: e.tensor_copy(tot[:], LC[:, T - 1:T]), reads=[b_LC], writes=[b_m])
    S.op("pe", lambda e: e.matmul(ptr[0][0:NP, 0:1], mtri[:], tot[:], start=True, stop=True),
         reads=[b_m], writes=[b_ptr[0]])
    S.op("dve", lambda e: e.tensor_copy(off[:], ptr[0][0:NP, 0:1]), reads=[b_ptr[0]], writes=[b_m])
    S.op("dve", lambda e: e.tensor_scalar(LC[:], LC[:], off[:, 0:1], None, ALU.add), reads=[b_LC, b_m], writes=[b_LC])
    S.dma("sp", cum_d.rearrange("h t f -> (h t) f"), LC[:], b_LC, reads=[b_LC], writes=[b_cum])

    QT = S.sb("A_QT", [128, SL], BF16)
    KT = S.sb("A_KT", [128, SL], BF16)
    VT = S.sb("A_VT", [128, SL], BF16)
    VP = S.sb("A_VP", [128, nblk, 132], BF16)
    CB = S.sb("A_CB", [128, SL], F32)
    crow = S.sb("A_crow", [nblk, 128], F32)
    negcs = S.sb("A_negcs", [128, nblk], F32)
    kmax = S.sb("A_kmax", [128, 1], F32)
    kmx = S.sb("A_kmx", [128, SL // T], F32)
    identb = S.sb("A_identb", [128, 128], BF16)
    identf = S.sb("A_identf", [128, 128], F32)
    onesb = S.sb("A_onesb", [128, 128], BF16)
    tri = S.sb("A_tri", [128, 128], F32)
    sqt = [S.sb("A_sqt%d" % i, [128, T], BF16) for i in range(2)]
    tmp = [S.sb("A_tmp%d" % i, [128, T], F32) for i in range(3)]
    PT = [S.sb("A_PT%d" % i, [128, T], BF16) for i in range(3)]
    rinv = S.sb("A_rinv", [128, 4], F32)
    ot = [S.sb("A_ot%d" % i, [128, 128], F32) for i in range(2)]
    oT = [S.sb("A_oT%d" % i, [128, T], F32) for i in range(2)]
    b_QT, b_KT, b_VT, b_VP, b_CB, b_ncs, b_km = (S.buf(n) for n in ("QT", "KT", "VT", "VP", "CB", "ncs", "km"))
    b_crow = S.buf("crow")
    b_sqt = [S.buf(), S.buf()]
    b_tmp = [S.buf() for _ in range(3)]
    b_PT = [S.buf() for _ in range(3)]
    b_rinv = S.buf("rinv")
    b_ot = [S.buf(), S.buf()]
    b_oT = [S.buf(), S.buf()]
    pS = G.pt
    b_pS = G.pb
    pO = [S.ps("A_pO%d" % i) for i in range(4)]
    b_pO = [S.buf("A_pO%d" % i, True) for i in range(4)]
    ptrb = ptr[1]
    pTb = pO[3][:].bitcast(BF16)
    b_pTb = b_pO[3]
    S.dma("sp", identb[:], d["identb"], b_c, writes=[b_c])
    S.dma("sp", identf[:], d["identf"], b_c, writes=[b_c])
    S.dma("sp", tri[:], d["tri"], b_c, writes=[b_c])
    S.op("pool", lambda e: e.memset(onesb[:], 1.0), writes=[b_c])
    S.op("pool", lambda e: e.memset(VP[:], 1.0), writes=[b_VP])
    k = {"s": 0, "t": 0, "p": 0, "o": 0, "oT": 0, "sq": 0}
    for b in range(NB):
        for hl in range(NH):
            tsl = slice(b * SL, (b + 1) * SL)
            S.dma("sp", QT[:], qkv_d[hl, :, tsl], b_QT, reads=[b_qkv], writes=[b_QT])
            S.dma("sp", KT[:], qkv_d[NH + hl, :, tsl], b_KT, reads=[b_qkv], writes=[b_KT])
            S.dma("sp", VT[:], qkv_d[2 * NH + hl, :, tsl], b_VT, reads=[b_qkv], writes=[b_VT])
            cum_row = cum_d[hl, b * nseg:(b + 1) * nseg, :]
            S.dma("sp", CB[:], cum_row.rearrange("s f -> (s f)").partition_broadcast(128), b_CB,
                  reads=[b_cum], writes=[b_CB])
            S.dma("sp", crow[:], cum_row.rearrange("s (j p) -> (s j) p", p=128), b_crow, reads=[b_cum], writes=[b_crow])
            S.op("pe", lambda e: e.transpose(ptr[1][:, 0:nblk], crow[:], identf[0:nblk, 0:nblk]),
                 reads=[b_crow, b_c], writes=[b_ptr[1]])
            S.op("dve", lambda e: e.tensor_scalar(negcs[:], ptr[1][:, 0:nblk], -1.0, None, ALU.mult),
                 reads=[b_ptr[1]], writes=[b_ncs])
            for j0 in range(0, nblk, 8):
                nj = min(8, nblk - j0)
                for jj in range(nj):
                    j = j0 + jj
                    S.op("pe", lambda e, j=j, jj=jj: e.transpose(pTb[:, jj * 128:(jj + 1) * 128], VT[:, j * 128:(j + 1) * 128], identb[:]),
                         reads=[b_VT, b_c], writes=[b_pTb])
                S.op("act", lambda e, j0=j0, nj=nj: e.activation(
                    out=VP[:, j0:j0 + nj, 0:128], in_=pTb[:, 0:nj * 128].rearrange("p (j d) -> p j d", d=128), func=AF.Copy),
                    reads=[b_pTb], writes=[b_VP])
            for c in range(SL // T):
                i = k["sq"] % 2
                k["sq"] += 1
                S.op("pool", lambda e, c=c, i=i: e.tensor_tensor(sqt[i][:], KT[:, c * T:(c + 1) * T], KT[:, c * T:(c + 1) * T], ALU.mult),
                     reads=[b_KT], writes=[b_sqt[i]])
                S.op("pe", lambda e, i=i: e.matmul(ptr[0][:, 0:T], onesb[:], sqt[i][:], start=True, stop=True),
                     reads=[b_sqt[i], b_c], writes=[b_ptr[0]])
                S.op("dve", lambda e, c=c: e.reduce_max(kmx[:, c:c + 1], ptr[0][:, 0:T], axis=AX.X),
                     reads=[b_ptr[0]], writes=[b_km])
            S.op("dve", lambda e: e.reduce_max(kmax[:], kmx[:], axis=AX.X), reads=[b_km], writes=[b_km])
            for c in range(SL // T):
                i = k["sq"] % 2
                k["sq"] += 1
                ti = k["t"] % 3
                k["t"] += 1
                S.op("pool", lambda e, c=c, i=i: e.tensor_tensor(sqt[i][:], QT[:, c * T:(c + 1) * T], QT[:, c * T:(c + 1) * T], ALU.mult),
                     reads=[b_QT], writes=[b_sqt[i]])
                S.op("pe", lambda e, i=i: e.matmul(ptr[0][:, 0:T], onesb[:], sqt[i][:], start=True, stop=True),
                     reads=[b_sqt[i], b_c], writes=[b_ptr[0]])
                S.op("act", lambda e, ti=ti: e.activation(out=tmp[ti][:], in_=ptr[0][:, 0:T], func=AF.Sqrt, scale=kmax[:, 0:1]),
                     reads=[b_ptr[0], b_km], writes=[b_tmp[ti]])
                S.op("dve", lambda e, c=c, ti=ti: e.scalar_tensor_tensor(CB[:, c * T:(c + 1) * T], tmp[ti][:], -scale,
                                                                         CB[:, c * T:(c + 1) * T], ALU.mult, ALU.add),
                     reads=[b_tmp[ti], b_CB], writes=[b_CB])
            def emit_qk(I, j):
                r = max(0, j - 4 * I)
                c0 = r * 128
                W_ = T - c0
                si = k["s"] % 2
                k["s"] += 1
                S.op("pe", lambda e: e.matmul(
                    pS[si][:, 0:W_], KT[:, j * 128:(j + 1) * 128], QT[:, I * T + c0:(I + 1) * T], start=True, stop=True),
                    reads=[b_KT, b_QT], writes=[b_pS[si]])
                return (I, j, r, c0, W_, si)

            tiles = [(I, j) for I in range(SL // T) for j in range(4 * I + 4)]
            nxt = emit_qk(*tiles[0])
            for n, (I, j) in enumerate(tiles):
                _, _, r, c0, W_, si = nxt
                if n + 1 < len(tiles):
                    nxt = emit_qk(*tiles[n + 1])
                ti = k["t"] % 3
                k["t"] += 1
                pi = k["p"] % 3
                k["p"] += 1
                S.op("dve", lambda e, I=I, c0=c0, W_=W_, si=si, ti=ti: e.scalar_tensor_tensor(
                    tmp[ti][:, 0:W_], pS[si][:, 0:W_], scale, CB[:, I * T + c0:(I + 1) * T], ALU.mult, ALU.add),
                    reads=[b_pS[si], b_CB], writes=[b_tmp[ti]])
                if j >= 4 * I:
                    S.op("pool", lambda e, ti=ti: e.tensor_tensor(tmp[ti][:, 0:128], tmp[ti][:, 0:128], tri[:], ALU.add),
                         reads=[b_tmp[ti], b_c], writes=[b_tmp[ti]])
                S.op("act", lambda e, j=j, W_=W_, ti=ti, pi=pi: e.activation(
                    out=PT[pi][:, 0:W_], in_=tmp[ti][:, 0:W_], func=AF.Exp, bias=negcs[:, j:j + 1]),
                    reads=[b_tmp[ti], b_ncs], writes=[b_PT[pi]])
                for u in range(r, 4):
                    S.op("pe", lambda e, j=j, u=u, r=r, pi=pi, I=I: e.matmul(
                        pO[u][:, 0:129], PT[pi][:, (u - r) * 128:(u - r + 1) * 128], VP[:, j, 0:129],
                        start=(j == 0), stop=(j == 4 * I + u)),
                        reads=[b_PT[pi], b_VP], writes=[b_pO[u]])
                if j != 4 * I + 3:
                    continue
                oi = k["oT"] % 2
                k["oT"] += 1
                for u in range(4):
                    S.op("dve", lambda e, u=u: e.reciprocal(rinv[:, u:u + 1], pO[u][:, 128:129]),
                         reads=[b_pO[u]], writes=[b_rinv])
                    o_i = k["o"] % 2
                    k["o"] += 1
                    S.op("act", lambda e, u=u, o_i=o_i: e.activation(out=ot[o_i][:], in_=pO[u][:, 0:128], func=AF.Copy,
                                                                   scale=rinv[:, u:u + 1]),
                         reads=[b_pO[u], b_rinv], writes=[b_ot[o_i]])
                    S.op("pe", lambda e, o_i=o_i: e.transpose(ptr[1][:, 0:128], ot[o_i][:], identf[:]),
                         reads=[b_ot[o_i], b_c], writes=[b_ptr[1]])
                    S.op("dve", lambda e, u=u, oi=oi: e.tensor_copy(oT[oi][:, u * 128:(u + 1) * 128], ptr[1][:, 0:128]),
                         reads=[b_ptr[1]], writes=[b_oT[oi]])
                S.dma("sp", d["oT"][hl * 128:(hl + 1) * 128, b * SL + I * T: b * SL + (I + 1) * T], oT[oi][:], b_oT[oi],
                      reads=[b_oT[oi]], writes=[b_od])
    return [b_od]


_STOP = [None]


class _Stop(Exception):
    pass


def _chk(k):
    if _STOP[0] == k:
        raise _Stop()


def emit_Agdn(S, d, D, NB, SL, NH):
    try:
        return _emit_Agdn(S, d, D, NB, SL, NH)
    except _Stop:
        return []


def _emit_Agdn(S, d, D, NB, SL, NH):
    nc = S.nc
    nK = D // 128
    T = 512
    C = 128
    ntt = NB * SL // T
    ngrp = SL // T
    R = NH * NB * SL // C
    qkv_d = nc.dram_tensor("gdn_qkv", [3 * NH, 128, NB * SL], F32).ap()
    bg_d = nc.dram_tensor("gdn_bg", [2, NH, NB * SL], F32).ap()
    gc_d = nc.dram_tensor("gdn_gc", [NH, NB * SL], F32).ap()
    b_qkv, b_bg, b_gc, b_od = S.buf("gqkv"), S.buf("gbg"), S.buf("ggc"), S.buf("goT")
    b_c = S.buf("gconst")

    def V(fn, r=(), w=()):
        S.op("dve", fn, reads=r, writes=w)

    def A_(fn, r=(), w=()):
        S.op("act", fn, reads=r, writes=w)

    def P_(fn, r=(), w=()):
        S.op("pool", fn, reads=r, writes=w)

    def M_(fn, r=(), w=()):
        S.op("pe", fn, reads=r, writes=w)

    xa = S.sb("G_xa", [128, nK, T], BF16)
    b_xa = [S.buf("G_xa%d" % i) for i in range(nK)]
    st = [S.sb("G_st%d" % i, [128, T], F32) for i in range(3)]
    b_st = [S.buf("G_st%d" % i) for i in range(3)]
    g8 = 2 * NH
    gt = [S.sb("G_gt%d" % i, [g8, T], F32) for i in range(2)]
    gs = [S.sb("G_gs%d" % i, [g8, T], F32) for i in range(2)]
    b_gt = [S.buf("G_gt%d" % i) for i in range(2)]
    b_gs = [S.buf("G_gs%d" % i) for i in range(2)]
    gbias = S.sb("G_gbias", [g8, 1], F32)
    gcoef = S.sb("G_gcoef", [g8, 1], F32)
    S.dma("pool", gbias[:], d["gbias"], b_c, writes=[b_c])
    S.dma("pool", gcoef[:], d["galog"], b_c, writes=[b_c])
    A_(lambda e: e.activation(out=gcoef[:], in_=gcoef[:], func=AF.Exp), [b_c], [b_c])
    V(lambda e: e.tensor_scalar(gcoef[:], gcoef[:], -1.0, None, ALU.mult), [b_c], [b_c])
    G = Gemm(S, nK, nslots=3, nps=2, tag="G")
    W16, b_W16 = G.precast("gdn_W16", d["W"], 3 * NH + 1)
    cnt = {"st": 0, "g": 0}
    for tt in range(ntt):
        ts = slice(tt * T, (tt + 1) * T)
        S.dma("pool", xa[:], d["hnT"][:, ts].rearrange("(c p) t -> p c t", p=128), b_xa[0], writes=b_xa)

        def epi(nb, pb, pap, ts=ts):
            if nb < 3 * NH:
                i = cnt["st"] % 3
                cnt["st"] += 1
                if i % 2 == 0:
                    A_(lambda e: e.activation(out=st[i][:], in_=pap, func=AF.Copy), [pb], [b_st[i]])
                else:
                    V(lambda e: e.tensor_copy(st[i][:], pap), [pb], [b_st[i]])
                S.dma("pool", qkv_d[nb, :, ts], st[i][:], b_st[i], reads=[b_st[i]], writes=[b_qkv])
            else:
                i = cnt["g"] % 2
                cnt["g"] += 1
                A_(lambda e: e.activation(out=gs[i][:], in_=pap[0:g8, :], func=AF.Sigmoid), [pb], [b_gs[i]])
                A_(lambda e: e.activation(out=gt[i][:], in_=pap[0:g8, :], func=AF.Exp, bias=gbias[:, 0:1]), [pb, b_c], [b_gt[i]])
                A_(lambda e: e.activation(out=gt[i][:], in_=gt[i][:], func=AF.Ln, bias=1.0), [b_gt[i]], [b_gt[i]])
                V(lambda e: e.tensor_scalar(gt[i][:], gt[i][:], gcoef[:, 0:1], None, ALU.mult), [b_gt[i], b_c], [b_gt[i]])
                S.dma("pool", bg_d[0, :, ts], gs[i][0:NH, :], b_gs[i], reads=[b_gs[i]], writes=[b_bg])
                S.dma("pool", bg_d[1, :, ts], gt[i][NH:g8, :], b_gt[i], reads=[b_gt[i]], writes=[b_bg])

        G.run(lambda kc: xa[:, kc, :], b_xa, lambda nb: W16[nb], 3 * NH + 1, T, epi, wbuf=b_W16)

    _chk(1)
    onesr = S.sb("G_onesr", [128, C], F32)
    P_(lambda e: e.memset(onesr[:], 1.0), [], [b_c])
    gr = [S.sb("G_gr%d" % i, [128, C], F32) for i in range(2)]
    gq = [S.sb("G_gq%d" % i, [128, C], F32) for i in range(2)]
    b_gr = [S.buf(), S.buf()]
    b_gq = [S.buf(), S.buf()]
    g_rows = bg_d[1].rearrange("h (n c) -> (h n) c", c=C)
    gc_rows = gc_d.rearrange("h (n c) -> (h n) c", c=C)
    for r0 in range(0, R, 128):
        nr = min(128, R - r0)
        i = (r0 // 128) % 2
        S.dma("sp", gr[i][0:nr, :], g_rows[r0:r0 + nr, :], b_gr[i], reads=[b_bg], writes=[b_gr[i]])
        V(lambda e, i=i, nr=nr: e.tensor_tensor_scan(gq[i][0:nr, :], onesr[0:nr, :], gr[i][0:nr, :], 0.0, ALU.mult, ALU.add),
          [b_gr[i], b_c], [b_gq[i]])
        S.dma("sp", gc_rows[r0:r0 + nr, :], gq[i][0:nr, :], b_gq[i], reads=[b_gq[i]], writes=[b_gc])

    _chk(2)
    def sbt(name, shape, dt=F32):
        return S.sb("G_" + name, shape, dt), S.buf("G_" + name)

    mstr, _ = sbt("mstr", [128, T]); mup, _ = sbt("mup", [128, T]); id4, _ = sbt("id4", [128, T])
    idf, _ = sbt("idf", [128, 128]); onesf, _ = sbt("onesf", [128, 128]); nwr, _ = sbt("nwr", [128, 128])
    for tle, key in ((mstr, "mstr4"), (mup, "mup4"), (id4, "id4"), (idf, "identf")):
        S.dma("sp", tle[:], d[key], b_c, writes=[b_c])
    S.dma("sp", nwr[:], d["nw"].partition_broadcast(128), b_c, writes=[b_c])
    P_(lambda e: e.memset(onesf[:], 1.0), [], [b_c])
    cw, b_cw = sbt("cw", [128, 3, 4])
    xin = [[sbt("x%d_%d" % (a, i), [128, T + 3]) for a in range(3)] for i in range(2)]
    GB = [sbt("GB%d" % i, [128, T]) for i in range(2)]
    rows = [sbt("rows%d" % i, [8, C]) for i in range(2)]
    yq, b_yq = sbt("yq", [128, T]); yk, b_yk = sbt("yk", [128, T]); yv, b_yv = sbt("yv", [128, T])
    sq, b_sq = sbt("sq", [128, T]); rr, b_rr = sbt("rr", [128, T])
    qn, b_qn = sbt("qn", [128, T]); kn, b_kn = sbt("kn", [128, T])
    kTM, b_kTM = sbt("kTM", [128, 4, C]); vTM, b_vTM = sbt("vTM", [128, 4, C])
    gcol, b_gcol = sbt("gcol", [128, 8])
    cols, b_cols = sbt("cols", [128, 24])
    bv, b_bv = sbt("bv", [128, 4, C]); kbg, b_kbg = sbt("kbg", [128, 4, C]); kdec, b_kdec = sbt("kdec", [128, 4, C])
    t1, b_t1 = sbt("t1", [128, T]); t2, b_t2 = sbt("t2", [128, T])
    E1, b_E1 = sbt("E1", [128, T]); E2, b_E2 = sbt("E2", [128, T]); eGB, b_eGB = sbt("eGB", [128, T])
    aT, b_aT = sbt("aT", [128, T]); qd, b_qd = sbt("qd", [128, T])
    Pm = [sbt("P%d" % i, [128, T]) for i in range(2)]
    PTm = [sbt("PT%d" % i, [128, T]) for i in range(2)]
    TTm = [sbt("TT%d" % i, [128, T]) for i in range(2)]
    u, b_u = sbt("u", [128, 4, C]); wT, b_wT = sbt("wT", [128, T])
    Sst, b_S = sbt("S", [128, C]); vnew, b_vnew = sbt("vnew", [128, C])
    ssq, b_ssq = sbt("ssq", [128, 1]); rstd, b_rstd = sbt("rstd", [128, 1]); junk, b_junk = sbt("junk", [128, C])
    oTM, b_oTM = sbt("oTM", [128, C])
    oFM = [sbt("oFM%d" % i, [128, T]) for i in range(2)]
    pk = [(S.ps("G_pk%d" % i), S.buf("G_pk%d" % i, True)) for i in range(6)]
    pk = [(G.pt[0], G.pb[0]), (G.pt[1], G.pb[1])] + pk
    (pA, b_pA), (pB, b_pB), (pC, b_pC), (pD, b_pD), (pE, b_pE), (pF, b_pF), (pG, b_pG), (pH, b_pH) = pk
    gi = 0
    for b in range(NB):
        for hl in range(NH):
            S.dma("sp", cw[:], d["cw"].rearrange("(a h) p j -> h p a j", h=NH)[hl], b_cw, writes=[b_cw])
            V(lambda e: e.memset(Sst[:], 0.0), [], [b_S])
            for g in range(ngrp):
                par = gi % 2
                gi += 1
                t0 = b * SL + g * T
                for a in range(3):
                    xt, xb_ = xin[par][a]
                    if g == 0:
                        P_(lambda e, xt=xt: e.memset(xt[:, 0:3], 0.0), [], [xb_])
                        S.dma("sp", xt[:, 3:T + 3], qkv_d[a * NH + hl, :, t0:t0 + T], xb_, reads=[b_qkv], writes=[xb_])
                    else:
                        S.dma("sp", xt[:], qkv_d[a * NH + hl, :, t0 - 3:t0 + T], xb_, reads=[b_qkv], writes=[xb_])
                GBt, b_GB = GB[par]
                S.dma("sp", GBt[:], gc_d[hl, t0:t0 + T].partition_broadcast(128), b_GB, reads=[b_gc], writes=[b_GB])
                rw, b_rw = rows[par]
                S.dma("sp", rw[0:4, :], gc_d[hl, t0:t0 + T].rearrange("(n c) -> n c", c=C), b_rw, reads=[b_gc], writes=[b_rw])
                S.dma("sp", rw[4:8, :], bg_d[0, hl, t0:t0 + T].rearrange("(n c) -> n c", c=C), b_rw, reads=[b_bg], writes=[b_rw])
                for a, (yt, yb) in enumerate(((yq, b_yq), (yk, b_yk), (yv, b_yv))):
                    xt, xb_ = xin[par][a]
                    V(lambda e, xt=xt, yt=yt, a=a: e.tensor_scalar(yt[:], xt[:, 3:T + 3], cw[:, a, 3:4], None, ALU.mult),
                      [xb_, b_cw], [yb])
                    for j in range(3):
                        V(lambda e, xt=xt, yt=yt, a=a, j=j: e.scalar_tensor_tensor(
                            yt[:], xt[:, j:j + T], cw[:, a, j:j + 1], yt[:], ALU.mult, ALU.add), [xb_, b_cw, yb], [yb])
                    A_(lambda e, yt=yt: e.activation(out=yt[:], in_=yt[:], func=AF.Silu), [yb], [yb])
                _chk(3)
                for (yt, yb, ot_, ob, sc, pp_, pb_) in ((yq, b_yq, qn, b_qn, 128 ** -0.5, pA, b_pA), (yk, b_yk, kn, b_kn, 1.0, pB, b_pB)):
                    P_(lambda e, yt=yt: e.tensor_tensor(sq[:], yt[:], yt[:], ALU.mult), [yb], [b_sq])
                    M_(lambda e, pp_=pp_: e.matmul(pp_[:, 0:T], onesf[:], sq[:], start=True, stop=True), [b_sq, b_c], [pb_])
                    V(lambda e, pp_=pp_: e.tensor_scalar(rr[:], pp_[:, 0:T], 1e-6, None, ALU.add), [pb_], [b_rr])
                    A_(lambda e: e.activation(out=rr[:], in_=rr[:], func=AF.Sqrt), [b_rr], [b_rr])
                    V(lambda e: e.reciprocal(rr[:], rr[:]), [b_rr], [b_rr])
                    V(lambda e, yt=yt, ot_=ot_, sc=sc: e.scalar_tensor_tensor(ot_[:], yt[:], sc, rr[:], ALU.mult, ALU.mult),
                      [yb, b_rr], [ob])
                _chk(4)
                M_(lambda e, rw=rw: e.transpose(pC[:, 0:8], rw[:], idf[0:8, 0:8]), [b_rw, b_c], [b_pC])
                V(lambda e: e.tensor_copy(gcol[:], pC[:, 0:8]), [b_pC], [b_gcol])
                V(lambda e: e.tensor_scalar(cols[:, 0:4], gcol[:, 4:8], -1.0, None, ALU.mult), [b_gcol], [b_cols])
                V(lambda e: e.tensor_scalar(cols[:, 4:8], gcol[:, 0:4], -1.0, None, ALU.mult), [b_gcol, b_cols], [b_cols])
                A_(lambda e: e.activation(out=cols[:, 8:12], in_=gcol[:, 0:4], func=AF.Exp), [b_gcol, b_cols], [b_cols])
                V(lambda e: e.tensor_tensor(cols[:, 8:12], cols[:, 8:12], gcol[:, 4:8], ALU.mult), [b_gcol, b_cols], [b_cols])
                V(lambda e, GBt=GBt: e.tensor_tensor(cols[:, 12:16], GBt[:].rearrange("p (n c) -> p n c", c=C)[:, :, C - 1], gcol[:, 0:4], ALU.subtract),
                  [b_GB, b_gcol, b_cols], [b_cols])
                A_(lambda e: e.activation(out=cols[:, 12:16], in_=cols[:, 12:16], func=AF.Exp), [b_cols], [b_cols])
                A_(lambda e, GBt=GBt: e.activation(out=cols[:, 16:20], in_=GBt[:].rearrange("p (n c) -> p n c", c=C)[:, :, C - 1], func=AF.Exp),
                   [b_GB, b_cols], [b_cols])
                _chk(5)
                for n in range(4):
                    M_(lambda e, n=n: e.transpose(pA[:, n * C:(n + 1) * C], kn[:, n * C:(n + 1) * C], idf[:]), [b_kn, b_c], [b_pA])
                    M_(lambda e, n=n: e.transpose(pB[:, n * C:(n + 1) * C], yv[:, n * C:(n + 1) * C], idf[:]), [b_yv, b_c], [b_pB])
                A_(lambda e: e.activation(out=kTM[:].rearrange("p n c -> p (n c)"), in_=pA[:, 0:T], func=AF.Copy), [b_pA], [b_kTM])
                A_(lambda e: e.activation(out=vTM[:].rearrange("p n c -> p (n c)"), in_=pB[:, 0:T], func=AF.Copy), [b_pB], [b_vTM])
                for n in range(4):
                    P_(lambda e, n=n: e.tensor_scalar(bv[:, n, :], vTM[:, n, :], gcol[:, 4 + n:5 + n], None, ALU.mult), [b_vTM, b_gcol], [b_bv])
                    P_(lambda e, n=n: e.tensor_scalar(kbg[:, n, :], kTM[:, n, :], cols[:, 8 + n:9 + n], None, ALU.mult), [b_kTM, b_cols], [b_kbg])
                    P_(lambda e, n=n: e.tensor_scalar(kdec[:, n, :], kTM[:, n, :], cols[:, 12 + n:13 + n], None, ALU.mult), [b_kTM, b_cols], [b_kdec])
                _chk(6)
                V(lambda e, GBt=GBt: e.scalar_tensor_tensor(t1[:], GBt[:], -1.0, mstr[:], ALU.mult, ALU.add), [b_GB, b_c], [b_t1])
                V(lambda e, GBt=GBt: e.tensor_tensor(t2[:], GBt[:], mup[:], ALU.add), [b_GB, b_c], [b_t2])
                for n in range(4):
                    cs = slice(n * C, (n + 1) * C)
                    A_(lambda e, n=n, cs=cs: e.activation(out=E1[:, cs], in_=t1[:, cs], func=AF.Exp, bias=gcol[:, n:n + 1]), [b_t1, b_gcol], [b_E1])
                    A_(lambda e, n=n, cs=cs: e.activation(out=E2[:, cs], in_=t2[:, cs], func=AF.Exp, bias=cols[:, 4 + n:5 + n]), [b_t2, b_cols], [b_E2])
                A_(lambda e, GBt=GBt: e.activation(out=eGB[:], in_=GBt[:], func=AF.Exp), [b_GB], [b_eGB])
                V(lambda e: e.tensor_tensor(qd[:], qn[:], eGB[:], ALU.mult), [b_qn, b_eGB], [b_qd])
                _chk(7)
                for n in range(4):
                    cs = slice(n * C, (n + 1) * C)
                    M_(lambda e, cs=cs: e.matmul(pC[:, cs], kn[:, cs], kn[:, cs], start=True, stop=True), [b_kn], [b_pC])
                    M_(lambda e, cs=cs: e.matmul(pD[:, cs], kn[:, cs], qn[:, cs], start=True, stop=True), [b_kn, b_qn], [b_pD])
                _chk(71)
                P0, b_P0 = Pm[0]
                PT0, b_PT0 = PTm[0]
                TT0, b_TT0 = TTm[0]
                for n in range(4):
                    cs = slice(n * C, (n + 1) * C)
                    V(lambda e, n=n, cs=cs: e.scalar_tensor_tensor(P0[:, cs], pC[:, cs], cols[:, n:n + 1], E1[:, cs], ALU.mult, ALU.mult),
                      [b_pC, b_cols, b_E1], [b_P0])
                V(lambda e: e.tensor_tensor(aT[:], pD[:, 0:T], E2[:], ALU.mult), [b_pD, b_E2], [b_aT])
                _chk(72)
                for n in range(4):
                    cs = slice(n * C, (n + 1) * C)
                    M_(lambda e, cs=cs: e.transpose(pE[:, cs], P0[:, cs], idf[:]), [b_P0, b_c], [b_pE])
                A_(lambda e: e.activation(out=PT0[:], in_=pE[:, 0:T], func=AF.Copy), [b_pE], [b_PT0])
                V(lambda e: e.tensor_tensor(TT0[:], pE[:, 0:T], id4[:], ALU.add), [b_pE, b_c], [b_TT0])
                _chk(73)
                cur = 0
                for step in range(6):
                    Pc, b_Pc = Pm[cur]; PTc, b_PTc = PTm[cur]; TTc, b_TTc = TTm[cur]
                    Pn, b_Pn = Pm[1 - cur]; PTn, b_PTn = PTm[1 - cur]; TTn, b_TTn = TTm[1 - cur]
                    last = step == 5
                    for n in range(4):
                        cs = slice(n * C, (n + 1) * C)
                        M_(lambda e, cs=cs, Pc=Pc, PTc=PTc: e.matmul(pC[:, cs], PTc[:, cs], Pc[:, cs], start=True, stop=True), [b_Pc, b_PTc], [b_pC])
                    A_(lambda e, Pn=Pn: e.activation(out=Pn[:], in_=pC[:, 0:T], func=AF.Copy), [b_pC], [b_Pn])
                    if not last:
                        for n in range(4):
                            cs = slice(n * C, (n + 1) * C)
                            M_(lambda e, cs=cs, Pc=Pc, PTc=PTc: e.matmul(pD[:, cs], Pc[:, cs], PTc[:, cs], start=True, stop=True), [b_Pc, b_PTc], [b_pD])
                        V(lambda e, PTn=PTn: e.tensor_copy(PTn[:], pD[:, 0:T]), [b_pD], [b_PTn])
                    for n in range(4):
                        cs = slice(n * C, (n + 1) * C)
                        M_(lambda e, cs=cs, Pn=Pn, TTc=TTc: e.matmul(pE[:, cs], Pn[:, cs], TTc[:, cs], start=True, stop=True), [b_Pn, b_TTc], [b_pE])
                    V(lambda e, TTn=TTn, TTc=TTc: e.tensor_tensor(TTn[:], pE[:, 0:T], TTc[:], ALU.add), [b_pE, b_TTc], [b_TTn])
                    cur = 1 - cur
                    _chk(74 + step)
                TTf, b_TTf = TTm[cur]
                _chk(8)
                for n in range(4):
                    cs = slice(n * C, (n + 1) * C)
                    M_(lambda e, n=n, cs=cs: e.matmul(pA[:, cs], TTf[:, cs], bv[:, n, :], start=True, stop=True), [b_TTf, b_bv], [b_pA])
                    M_(lambda e, n=n, cs=cs: e.matmul(pB[:, cs], kbg[:, n, :], TTf[:, cs], start=True, stop=True), [b_TTf, b_kbg], [b_pB])
                A_(lambda e: e.activation(out=u[:].rearrange("p n c -> p (n c)"), in_=pA[:, 0:T], func=AF.Copy), [b_pA], [b_u])
                V(lambda e: e.tensor_copy(wT[:], pB[:, 0:T]), [b_pB], [b_wT])
                _chk(9)
                oF, b_oF = oFM[par]
                for n in range(4):
                    cs = slice(n * C, (n + 1) * C)
                    M_(lambda e, cs=cs: e.matmul(pF[:, 0:C], wT[:, cs], Sst[:], start=True, stop=True), [b_wT, b_S], [b_pF])
                    V(lambda e, n=n: e.tensor_tensor(vnew[:], u[:, n, :], pF[:, 0:C], ALU.subtract), [b_u, b_pF], [b_vnew])
                    M_(lambda e, cs=cs: e.matmul(pG[:, 0:C], qd[:, cs], Sst[:], start=True, stop=False), [b_qd, b_S], [b_pG])
                    M_(lambda e, cs=cs: e.matmul(pG[:, 0:C], aT[:, cs], vnew[:], start=False, stop=True), [b_aT, b_vnew], [b_pG])
                    M_(lambda e, n=n: e.matmul(pF[:, C:2 * C], kdec[:, n, :], vnew[:], start=True, stop=True), [b_kdec, b_vnew], [b_pF])
                    V(lambda e, n=n: e.scalar_tensor_tensor(Sst[:], Sst[:], cols[:, 16 + n:17 + n], pF[:, C:2 * C], ALU.mult, ALU.add),
                      [b_S, b_cols, b_pF], [b_S])
                    A_(lambda e: e.activation(out=junk[:], in_=pG[:, 0:C], func=AF.Square, accum_out=ssq[:, 0:1]), [b_pG], [b_junk, b_ssq])
                    V(lambda e: e.tensor_scalar(rstd[:], ssq[:], 1.0 / C, 1e-6, ALU.mult, ALU.add), [b_ssq], [b_rstd])
                    A_(lambda e: e.activation(out=rstd[:], in_=rstd[:], func=AF.Sqrt), [b_rstd], [b_rstd])
                    V(lambda e: e.reciprocal(rstd[:], rstd[:]), [b_rstd], [b_rstd])
                    V(lambda e: e.scalar_tensor_tensor(oTM[:], pG[:, 0:C], rstd[:, 0:1], nwr[:], ALU.mult, ALU.mult),
                      [b_pG, b_rstd, b_c], [b_oTM])
                    M_(lambda e, cs=cs: e.transpose(pH[:, cs], oTM[:], idf[:]), [b_oTM, b_c], [b_pH])
                A_(lambda e, oF=oF: e.activation(out=oF[:], in_=pH[:, 0:T], func=AF.Copy), [b_pH], [b_oF])
                S.dma("sp", d["oT"][hl * 128:(hl + 1) * 128, t0:t0 + T], oF[:], b_oF, reads=[b_oF], writes=[b_od])
    return [b_od]


def emit_Assm(S, d, D, NB, SL, NPAIR):
    nc = S.nc
    nK = D // 128
    T = 512
    ntt = NB * SL // T
    nblk = NPAIR // 4
    TWO_PI = 2.0 * np.pi
    uT_d = nc.dram_tensor("ssm_uT", [nblk, 128, NB * SL], F32).ap()
    b_ud, b_od, b_c = S.buf("ssm_ud"), S.buf("ssm_od"), S.buf("ssm_c")

    def V(fn, r=(), w=()):
        S.op("dve", fn, reads=r, writes=w)

    def A_(fn, r=(), w=()):
        S.op("act", fn, reads=r, writes=w)

    def P_(fn, r=(), w=()):
        S.op("pool", fn, reads=r, writes=w)

    def M_(fn, r=(), w=()):
        S.op("pe", fn, reads=r, writes=w)

    xa = S.sb("S_xa", [128, nK, T], BF16)
    b_xa = [S.buf("S_xa%d" % i) for i in range(nK)]
    st = [S.sb("S_st%d" % i, [128, T], F32) for i in range(3)]
    b_st = [S.buf("S_st%d" % i) for i in range(3)]
    G = Gemm(S, nK, nslots=3, nps=2, tag="S")
    W16, b_W16 = G.precast("ssm_W16", d["Wu"], nblk)
    cnt = {"st": 0}
    for tt in range(ntt):
        ts = slice(tt * T, (tt + 1) * T)
        S.dma("pool", xa[:], d["hnT"][:, ts].rearrange("(c p) t -> p c t", p=128), b_xa[0], writes=b_xa)

        def epi(nb, pb, pap, ts=ts):
            i = cnt["st"] % 3
            cnt["st"] += 1
            if i % 2 == 0:
                A_(lambda e: e.activation(out=st[i][:], in_=pap, func=AF.Copy), [pb], [b_st[i]])
            else:
                V(lambda e: e.tensor_copy(st[i][:], pap), [pb], [b_st[i]])
            S.dma("pool", uT_d[nb, :, ts], st[i][:], b_st[i], reads=[b_st[i]], writes=[b_ud])

        G.run(lambda kc: xa[:, kc, :], b_xa, lambda nb: W16[nb], nblk, T, epi, wbuf=b_W16)

    def sbt(name, shape, dt=F32):
        return S.sb("S_" + name, shape, dt), S.buf("S_" + name)

    idf, _ = sbt("idf", [128, 128])
    S.dma("sp", idf[:], d["identf"], b_c, writes=[b_c])
    pr, b_pr = sbt("pr", [128, 40])
    pri, b_pri = S.sb("S_pri", [128, 1], mybir.dt.int32), None
    pin, b_pin = sbt("pin", [128, 3])
    pb4, b_pb4 = sbt("pb4", [128, 4, 16])
    bb, b_bb = sbt("bb", [128, 2, 16])
    blk, b_blk = sbt("blk", [128, 4, 32])
    BBt, b_BBt = sbt("BBt", [32, 2, 128])
    dsk, b_dsk = sbt("dsk", [32, 1]); dd, b_dd = sbt("dd", [32, 32])
    cosT, b_cos = sbt("cosT", [128, T]); sinT, b_sin = sbt("sinT", [128, T]); rT, b_rT = sbt("rT", [128, T])
    ut = [sbt("u%d" % i, [32, T]) for i in range(2)]
    z1, b_z1 = sbt("z1", [128, T]); z2, b_z2 = sbt("z2", [128, T])
    zre, b_zre = sbt("zre", [128, T]); zim, b_zim = sbt("zim", [128, T])
    wre, b_wre = sbt("wre", [128, T]); wim, b_wim = sbt("wim", [128, T])
    x1, b_x1 = sbt("x1", [128, T]); x2, b_x2 = sbt("x2", [128, T])
    xre, b_xre = sbt("xre", [128, T]); xim, b_xim = sbt("xim", [128, T])
    car, b_car = sbt("car", [128, 2])
    g1, b_g1 = sbt("g1", [32, T]); g2, b_g2 = sbt("g2", [32, T])
    yo = [sbt("yo%d" % i, [32, T]) for i in range(2)]
    pbu = [(S.ps("S_pbr%d" % i), S.buf("S_pbr%d" % i, True)) for i in range(2)]
    pbi = [(S.ps("S_pbi%d" % i), S.buf("S_pbi%d" % i, True)) for i in range(2)]
    py, b_py = S.ps("S_py"), S.buf("S_py", True)
    pt_, b_pt = G.pt[0], G.pb[0]
    c = lambda i: pr[:, i:i + 1]
    it = 0
    for j in range(NPAIR):
        S.dma("sp", pin[:, 0:1], d["lre"][j], b_pin, writes=[b_pin])
        S.dma("sp", pin[:, 1:2], d["lim"][j], b_pin, writes=[b_pin])
        S.dma("sp", pin[:, 2:3], d["lst"][j], b_pin, writes=[b_pin])
        for q, key in enumerate(("bre", "bim", "cre", "cim")):
            S.dma("sp", pb4[:, q, :], d[key][j], b_pb4, writes=[b_pb4])
        S.dma("sp", dsk[:], d["dsk"][j], b_dsk, writes=[b_dsk])
        R_, W_ = [b_pin, b_pr], [b_pr]
        A_(lambda e: e.activation(out=c(0), in_=pin[:, 2:3], func=AF.Exp), R_, W_)
        V(lambda e: e.tensor_tensor(c(1), pin[:, 0:1], c(0), ALU.mult), R_, W_)
        A_(lambda e: e.activation(out=c(2), in_=c(1), func=AF.Exp), R_, W_)
        V(lambda e: e.tensor_tensor(c(3), pin[:, 1:2], c(0), ALU.mult), R_, W_)
        V(lambda e: e.tensor_scalar(c(4), c(3), 1.0 / TWO_PI, None, ALU.mult), R_, W_)
        V(lambda e: e.tensor_copy(pri[:], c(4)), R_, W_)
        V(lambda e: e.tensor_copy(c(5), pri[:]), R_, W_)
        V(lambda e: e.scalar_tensor_tensor(c(6), c(5), -TWO_PI, c(3), ALU.mult, ALU.add), R_, W_)
        V(lambda e: e.tensor_scalar(c(7), c(6), 0.5, None, ALU.mult), R_, W_)
        V(lambda e: e.tensor_scalar(c(8), c(7), -1.0, None, ALU.mult), R_, W_)
        V(lambda e: e.tensor_tensor(c(8), c(8), c(7), ALU.max), R_, W_)
        V(lambda e: e.tensor_scalar(c(8), c(8), -1.0, np.pi / 2, ALU.mult, ALU.add), R_, W_)
        A_(lambda e: e.activation(out=c(9), in_=c(7), func=AF.Sin), R_, W_)
        A_(lambda e: e.activation(out=c(10), in_=c(8), func=AF.Sin), R_, W_)
        V(lambda e: e.tensor_tensor(c(11), c(9), c(10), ALU.mult), R_, W_)
        V(lambda e: e.tensor_scalar(c(11), c(11), 2.0, None, ALU.mult), R_, W_)
        V(lambda e: e.tensor_tensor(c(12), c(10), c(10), ALU.mult), R_, W_)
        V(lambda e: e.tensor_tensor(c(13), c(9), c(9), ALU.mult), R_, W_)
        V(lambda e: e.tensor_tensor(c(12), c(12), c(13), ALU.subtract), R_, W_)
        V(lambda e: e.tensor_tensor(c(14), c(2), c(12), ALU.mult), R_, W_)
        V(lambda e: e.tensor_tensor(c(15), c(2), c(11), ALU.mult), R_, W_)
        V(lambda e: e.tensor_tensor(c(16), pin[:, 0:1], pin[:, 0:1], ALU.mult), R_, W_)
        V(lambda e: e.tensor_tensor(c(17), pin[:, 1:2], pin[:, 1:2], ALU.mult), R_, W_)
        V(lambda e: e.tensor_tensor(c(16), c(16), c(17), ALU.add), R_, W_)
        V(lambda e: e.reciprocal(c(16), c(16)), R_, W_)
        V(lambda e: e.tensor_scalar(c(17), c(14), -1.0, None, ALU.add), R_, W_)
        V(lambda e: e.tensor_tensor(c(18), c(17), pin[:, 0:1], ALU.mult), R_, W_)
        V(lambda e: e.tensor_tensor(c(19), c(15), pin[:, 1:2], ALU.mult), R_, W_)
        V(lambda e: e.tensor_tensor(c(18), c(18), c(19), ALU.add), R_, W_)
        V(lambda e: e.tensor_tensor(c(18), c(18), c(16), ALU.mult), R_, W_)
        V(lambda e: e.tensor_tensor(c(19), c(15), pin[:, 0:1], ALU.mult), R_, W_)
        V(lambda e: e.tensor_tensor(c(20), c(17), pin[:, 1:2], ALU.mult), R_, W_)
        V(lambda e: e.tensor_tensor(c(19), c(19), c(20), ALU.subtract), R_, W_)
        V(lambda e: e.tensor_tensor(c(19), c(19), c(16), ALU.mult), R_, W_)
        V(lambda e: e.tensor_scalar(c(20), c(19), -1.0, None, ALU.mult), R_, W_)
        R2 = [b_pr, b_pb4, b_bb]
        V(lambda e: e.tensor_scalar(bb[:, 0, :], pb4[:, 0, :], c(18), None, ALU.mult), R2, [b_bb])
        V(lambda e: e.scalar_tensor_tensor(bb[:, 0, :], pb4[:, 1, :], c(20), bb[:, 0, :], ALU.mult, ALU.add), R2, [b_bb])
        V(lambda e: e.tensor_scalar(bb[:, 1, :], pb4[:, 1, :], c(18), None, ALU.mult), R2, [b_bb])
        V(lambda e: e.scalar_tensor_tensor(bb[:, 1, :], pb4[:, 0, :], c(19), bb[:, 1, :], ALU.mult, ALU.add), R2, [b_bb])
        R3 = [b_bb, b_pb4, b_blk]
        V(lambda e: e.memset(blk[:], 0.0), R3, [b_blk])
        for q, (src, sgn) in enumerate(((bb[:, 0, :], 1.0), (bb[:, 1, :], 1.0), (pb4[:, 2, :], 1.0), (pb4[:, 3, :], -1.0))):
            V(lambda e, q=q, src=src, sgn=sgn: e.tensor_scalar(blk[0:64, q, 0:16], src[0:64, :], sgn, None, ALU.mult), R3, [b_blk])
            V(lambda e, q=q, src=src, sgn=sgn: e.tensor_scalar(blk[64:128, q, 16:32], src[64:128, :], sgn, None, ALU.mult), R3, [b_blk])
        for q in range(2):
            M_(lambda e, q=q: e.transpose(pt_[0:32, q * 128:(q + 1) * 128], blk[:, q, :], idf[:]), [b_blk, b_c], [b_pt])
        V(lambda e: e.tensor_copy(BBt[:].rearrange("p q s -> p (q s)"), pt_[0:32, 0:256]), [b_pt], [b_BBt])
        V(lambda e: e.tensor_scalar(dd[:], idf[0:32, 0:32], dsk[:, 0:1], None, ALU.mult), [b_dsk, b_c], [b_dd])
        RT = [b_pr, b_cos, b_sin]
        V(lambda e: e.tensor_copy(cosT[:, 0:1], c(12)), RT, [b_cos])
        V(lambda e: e.tensor_copy(sinT[:, 0:1], c(11)), RT, [b_sin])
        m = 1
        while m < T:
            V(lambda e, m=m: e.tensor_scalar(c(21), sinT[:, m - 1:m], -1.0, None, ALU.mult), RT, [b_pr])
            V(lambda e, m=m: e.tensor_scalar(cosT[:, m:2 * m], cosT[:, 0:m], cosT[:, m - 1:m], None, ALU.mult), RT, [b_cos])
            V(lambda e, m=m: e.scalar_tensor_tensor(cosT[:, m:2 * m], sinT[:, 0:m], c(21), cosT[:, m:2 * m], ALU.mult, ALU.add), RT, [b_cos])
            V(lambda e, m=m: e.tensor_scalar(sinT[:, m:2 * m], sinT[:, 0:m], cosT[:, m - 1:m], None, ALU.mult), RT, [b_sin])
            V(lambda e, m=m: e.scalar_tensor_tensor(sinT[:, m:2 * m], cosT[:, 0:m], sinT[:, m - 1:m], sinT[:, m:2 * m], ALU.mult, ALU.add), RT, [b_sin])
            m *= 2
        V(lambda e: e.memset(rT[:], 1.0), [b_rT], [b_rT])
        V(lambda e: e.tensor_scalar(rT[:], rT[:], c(2), None, ALU.mult), [b_pr, b_rT], [b_rT])
        blkno, prow = j // 4, 32 * (j % 4)
        for b in range(NB):
            for ti in range(SL // T):
                t0 = b * SL + ti * T
                par = it % 2
                it += 1
                u_, b_u = ut[par]
                S.dma("sp", u_[:], uT_d[blkno, prow:prow + 32, t0:t0 + T], b_u, reads=[b_ud], writes=[b_u])
                (pre, b_pre), (pim, b_pim) = pbu[par], pbi[par]
                M_(lambda e, u_=u_, pre=pre: e.matmul(pre[:, 0:T], BBt[:, 0, :], u_[:], start=True, stop=True), [b_BBt, b_u], [b_pre])
                M_(lambda e, u_=u_, pim=pim: e.matmul(pim[:, 0:T], BBt[:, 1, :], u_[:], start=True, stop=True), [b_BBt, b_u], [b_pim])
                V(lambda e, pre=pre: e.tensor_tensor(z1[:], pre[:, 0:T], cosT[:], ALU.mult), [b_pre, b_cos], [b_z1])
                V(lambda e, pim=pim: e.tensor_tensor(z2[:], pim[:, 0:T], sinT[:], ALU.mult), [b_pim, b_sin], [b_z2])
                P_(lambda e: e.tensor_tensor(zre[:], z1[:], z2[:], ALU.add), [b_z1, b_z2], [b_zre])
                V(lambda e, pim=pim: e.tensor_tensor(z1[:], pim[:, 0:T], cosT[:], ALU.mult), [b_pim, b_cos, b_zre], [b_z1])
                V(lambda e, pre=pre: e.tensor_tensor(z2[:], pre[:, 0:T], sinT[:], ALU.mult), [b_pre, b_sin, b_zre], [b_z2])
                P_(lambda e: e.tensor_tensor(zim[:], z1[:], z2[:], ALU.subtract), [b_z1, b_z2], [b_zim])
                if ti == 0:
                    V(lambda e: e.memset(car[:], 0.0), [b_car], [b_car])
                V(lambda e: e.tensor_tensor_scan(wre[:], rT[:], zre[:], car[:, 0:1], ALU.mult, ALU.add), [b_rT, b_zre, b_car], [b_wre])
                V(lambda e: e.tensor_tensor_scan(wim[:], rT[:], zim[:], car[:, 1:2], ALU.mult, ALU.add), [b_rT, b_zim, b_car], [b_wim])
                P_(lambda e: e.tensor_tensor(x1[:], wre[:], cosT[:], ALU.mult), [b_wre, b_cos], [b_x1])
                P_(lambda e: e.tensor_tensor(x2[:], wim[:], sinT[:], ALU.mult), [b_wim, b_sin], [b_x2])
                P_(lambda e: e.tensor_tensor(xre[:], x1[:], x2[:], ALU.subtract), [b_x1, b_x2], [b_xre])
                P_(lambda e: e.tensor_tensor(x1[:], wim[:], cosT[:], ALU.mult), [b_wim, b_cos, b_xre], [b_x1])
                P_(lambda e: e.tensor_tensor(x2[:], wre[:], sinT[:], ALU.mult), [b_wre, b_sin, b_xre], [b_x2])
                P_(lambda e: e.tensor_tensor(xim[:], x1[:], x2[:], ALU.add), [b_x1, b_x2], [b_xim])
                V(lambda e: e.tensor_copy(car[:, 0:1], xre[:, T - 1:T]), [b_xre, b_car], [b_car])
                V(lambda e: e.tensor_copy(car[:, 1:2], xim[:, T - 1:T]), [b_xim, b_car], [b_car])
                M_(lambda e: e.matmul(py[0:32, 0:T], blk[:, 2, :], xre[:], start=True, stop=False), [b_blk, b_xre], [b_py])
                M_(lambda e: e.matmul(py[0:32, 0:T], blk[:, 3, :], xim[:], start=False, stop=False), [b_blk, b_xim], [b_py])
                M_(lambda e, u_=u_: e.matmul(py[0:32, 0:T], dd[:], u_[:], start=False, stop=True), [b_dd, b_u], [b_py])
                yo_, b_yo = yo[par]
                A_(lambda e: e.activation(out=g1[:], in_=py[0:32, 0:T], func=AF.Square), [b_py], [b_g1])
                V(lambda e: e.tensor_scalar(g1[:], g1[:], 0.044715, 1.0, ALU.mult, ALU.add), [b_g1], [b_g1])
                V(lambda e: e.tensor_tensor(g1[:], g1[:], py[0:32, 0:T], ALU.mult), [b_g1, b_py], [b_g1])
                A_(lambda e: e.activation(out=g2[:], in_=g1[:], func=AF.Tanh, scale=float(np.sqrt(2.0 / np.pi))), [b_g1], [b_g2])
                V(lambda e: e.tensor_scalar(g2[:], g2[:], 1.0, 0.5, ALU.add, ALU.mult), [b_g2], [b_g2])
                V(lambda e, yo_=yo_: e.tensor_tensor(yo_[:], g2[:], py[0:32, 0:T], ALU.mult), [b_g2, b_py], [b_yo])
                S.dma("sp", d["oT"][j * 32:(j + 1) * 32, t0:t0 + T], yo_[:], b_yo, reads=[b_yo], writes=[b_od])
    return [b_od]


D_MODEL, BATCH, SEQ, DEPTH, PLE = 4096, 2, 8192, 4, 256
NTOK = BATCH * SEQ
NCB = 8
TT = NTOK // NCB
HEADS, W_MIX = 32, 4096
HG = NCORES // BATCH
NH_LOC = HEADS // HG
CW = W_MIX // HG


def _new_nc():
    return bass.Bass("TRN2", target_bir_lowering=False)


def _decl(nc, ins, outs):
    d = {}
    for name, (shape, dt) in ins.items():
        d[name] = nc.dram_tensor(name, list(shape), dt, kind="ExternalInput").ap()
    for name, (shape, dt) in outs.items():
        d[name] = nc.dram_tensor(name, list(shape), dt, kind="ExternalOutput").ap()
    return d


def _build(emit, ins, outs, *args, **kw):
    nc = _new_nc()
    d = _decl(nc, ins, outs)
    with contextlib.ExitStack() as st:
        S = Sched(nc, st)
        fin = emit(S, d, *args, **kw)
        S.finish(fin)
    return nc


def _launch(nc, in_maps):
    res = run_bass_kernel_spmd(nc, in_maps, core_ids=list(range(len(in_maps))))
    return res.results


def colvec(g):
    return np.ascontiguousarray(np.asarray(g, np.float32).reshape(-1, 128).T)


def fox_consts(NH, NB, nseg):
    NP = NH * NB * nseg
    m = np.zeros((NP, NP), np.float32)
    for h in range(NH):
        for b in range(NB):
            base = (h * NB + b) * nseg
            for s1 in range(nseg):
                m[base + s1, base + s1 + 1:base + nseg] = 1.0
    p = np.arange(128)
    tri = np.where(p[None, :] >= p[:, None], 0.0, -1e30).astype(np.float32)
    return {"mtri": m, "tri": tri, "identf": np.eye(128, dtype=np.float32),
            "identb": np.eye(128).astype(ml_dtypes.bfloat16)}


def gdn_consts():
    p = np.arange(128)
    mstr = np.where(p[:, None] > p[None, :], 0.0, -1e30).astype(np.float32)
    mup = np.where(p[None, :] >= p[:, None], 0.0, -1e30).astype(np.float32)
    return {"mstr4": np.tile(mstr, (1, 4)), "mup4": np.tile(mup, (1, 4)),
            "id4": np.tile(np.eye(128, dtype=np.float32), (1, 4)), "identf": np.eye(128, dtype=np.float32)}


def kernel(x, p, norm_mix, fox_w_in, fox_b_f, fox_w_out, gdn_w_in, gdn_conv, gdn_a_log,
           gdn_dt_bias, gdn_norm, gdn_w_out, ssm_w_in, ssm_lam_re, ssm_lam_im, ssm_b_re,
           ssm_b_im, ssm_c_re, ssm_c_im, ssm_log_step, ssm_d, ssm_w_glu, ssm_b_glu, ssm_w_out,
           norm_ple, ple_w_proj, ple_w_gate, final_norm):
    f32 = np.float32
    D, nK = D_MODEL, D_MODEL // 128
    x = np.asarray(x, f32)
    p = np.asarray(p, f32)
    tok = [slice(c * TT, (c + 1) * TT) for c in range(NCB)]
    hT = np.ascontiguousarray(x.reshape(NTOK, D).T)
    hT_sh = [np.ascontiguousarray(hT[:, t]) for t in tok]
    del hT
    wblk = (nK, 128, nK, 128)

    ncN = _build(emit_N, {"hT": ((D, TT), F32), "g": ((128, nK), F32)}, {"hnT_out": ((D, TT), BF16)}, D, TT, 512)
    g0 = colvec(norm_mix[0])
    r = _launch(ncN, [{"hT": hT_sh[c], "g": g0} for c in range(NCB)])
    hn_sh = [r[c]["hnT_out"] for c in range(NCB)]

    b_ins = {"hT": ((D, TT), F32), "hnT": ((D, TT), BF16), "oT": ((D, TT), F32), "pT": ((PLE, TT), F32),
             "g2": ((128, nK), F32), "gn": ((128, nK), F32), "Wz": (wblk, F32), "Wout": (wblk, F32),
             "Wg": (wblk, F32), "Wp": ((nK, 128, PLE // 128, 128), F32)}
    progs = {}
    out = None
    for i in range(DEPTH):
        kind, j = i % 3, i // 3
        hn_full = np.concatenate(hn_sh, axis=1)
        hn_b = [np.ascontiguousarray(hn_full[:, bi * SEQ:(bi + 1) * SEQ]) for bi in range(BATCH)]
        if kind == 0:
            w_in = np.asarray(fox_w_in[j], f32)
            if "fox" not in progs:
                NP = NH_LOC * (SEQ // 512)
                progs["fox"] = _build(
                    emit_Afox,
                    {"hnT": ((D, SEQ), BF16), "W": ((3 * NH_LOC + 1, 128, nK, 128), F32), "bf": ((NH_LOC, 1), F32),
                     "mtri": ((NP, NP), F32), "tri": ((128, 128), F32), "identf": ((128, 128), F32),
                     "identb": ((128, 128), BF16)},
                    {"oT": ((NH_LOC * 128, SEQ), F32)}, D, 1, SEQ, NH_LOC)
            cst = fox_consts(NH_LOC, 1, SEQ // 512)
            maps = []
            for c in range(NCORES):
                bi, hg = c // HG, c % HG
                cs = slice(CW * hg, CW * (hg + 1))
                wf = np.zeros((D, 128), f32)
                wf[:, :NH_LOC] = w_in[:, 4 * W_MIX + NH_LOC * hg: 4 * W_MIX + NH_LOC * (hg + 1)]
                Wc = np.concatenate([w_in[:, 0 * W_MIX:1 * W_MIX][:, cs], w_in[:, 1 * W_MIX:2 * W_MIX][:, cs],
                                     w_in[:, 2 * W_MIX:3 * W_MIX][:, cs], wf], axis=1)
                mm = {"hnT": hn_b[bi], "W": blockw(Wc),
                      "bf": np.asarray(fox_b_f[j], f32)[NH_LOC * hg:NH_LOC * (hg + 1)].reshape(NH_LOC, 1)}
                mm.update(cst)
                maps.append(mm)
            r = _launch(progs["fox"], maps)
            Wz = blockw(w_in[:, 3 * W_MIX:4 * W_MIX])
            Wout = blockw(np.asarray(fox_w_out[j], f32))
        elif kind == 1:
            w_in = np.asarray(gdn_w_in[j], f32)
            if "gdn" not in progs:
                progs["gdn"] = _build(
                    emit_Agdn,
                    {"hnT": ((D, SEQ), BF16), "W": ((3 * NH_LOC + 1, 128, nK, 128), F32), "cw": ((3 * NH_LOC, 128, 4), F32),
                     "gbias": ((2 * NH_LOC, 1), F32), "galog": ((2 * NH_LOC, 1), F32), "nw": ((128,), F32),
                     "mstr4": ((128, 512), F32), "mup4": ((128, 512), F32), "id4": ((128, 512), F32),
                     "identf": ((128, 128), F32)},
                    {"oT": ((NH_LOC * 128, SEQ), F32)}, D, 1, SEQ, NH_LOC)
            cst = gdn_consts()
            conv = np.asarray(gdn_conv[j], f32)
            maps = []
            for c in range(NCORES):
                bi, hg = c // HG, c % HG
                cs = slice(CW * hg, CW * (hg + 1))
                hs = slice(NH_LOC * hg, NH_LOC * (hg + 1))
                wg = np.zeros((D, 128), f32)
                wg[:, :NH_LOC] = w_in[:, 4 * W_MIX:4 * W_MIX + HEADS][:, hs]
                wg[:, NH_LOC:2 * NH_LOC] = w_in[:, 4 * W_MIX + HEADS:4 * W_MIX + 2 * HEADS][:, hs]
                Wc = np.concatenate([w_in[:, 0 * W_MIX:1 * W_MIX][:, cs], w_in[:, 1 * W_MIX:2 * W_MIX][:, cs],
                                     w_in[:, 2 * W_MIX:3 * W_MIX][:, cs], wg], axis=1)
                cwc = np.concatenate([conv[:, 0 * W_MIX:1 * W_MIX][:, cs], conv[:, 1 * W_MIX:2 * W_MIX][:, cs],
                                      conv[:, 2 * W_MIX:3 * W_MIX][:, cs]], axis=1)
                zer = np.zeros(NH_LOC, f32)
                mm = {"hnT": hn_b[bi], "W": blockw(Wc), "cw": np.ascontiguousarray(cwc.T.reshape(3 * NH_LOC, 128, 4)),
                      "gbias": np.concatenate([zer, np.asarray(gdn_dt_bias[j], f32)[hs]]).reshape(-1, 1),
                      "galog": np.concatenate([zer, np.asarray(gdn_a_log[j], f32)[hs]]).reshape(-1, 1),
                      "nw": np.asarray(gdn_norm[j], f32)}
                mm.update(cst)
                maps.append(mm)
            r = _launch(progs["gdn"], maps)
            Wz = blockw(w_in[:, 3 * W_MIX:4 * W_MIX])
            Wout = blockw(np.asarray(gdn_w_out[j], f32))
        else:
            w_in = np.asarray(ssm_w_in[j], f32)
            NPAIR = CW // 32
            if "ssm" not in progs:
                ins = {"hnT": ((D, SEQ), BF16), "Wu": ((NPAIR // 4, 128, nK, 128), F32), "identf": ((128, 128), F32),
                       "dsk": ((NPAIR, 32, 1), F32)}
                for k in ("lre", "lim", "lst"):
                    ins[k] = ((NPAIR, 128, 1), F32)
                for k in ("bre", "bim", "cre", "cim"):
                    ins[k] = ((NPAIR, 128, 16), F32)
                progs["ssm"] = _build(emit_Assm, ins, {"oT": ((NPAIR * 32, SEQ), F32)}, D, 1, SEQ, NPAIR)
            maps = []
            for c in range(NCORES):
                bi, hg = c // HG, c % HG
                cs = slice(CW * hg, CW * (hg + 1))
                gs = slice(2 * NPAIR * hg, 2 * NPAIR * (hg + 1))
                mm = {"hnT": hn_b[bi], "Wu": blockw(w_in[:, :W_MIX][:, cs]), "identf": np.eye(128, dtype=f32),
                      "lre": np.ascontiguousarray(np.asarray(ssm_lam_re[j], f32)[gs]).reshape(NPAIR, 128, 1),
                      "lim": np.ascontiguousarray(np.asarray(ssm_lam_im[j], f32)[gs]).reshape(NPAIR, 128, 1),
                      "lst": np.repeat(np.asarray(ssm_log_step[j], f32)[gs], 64).reshape(NPAIR, 128, 1),
                      "bre": np.ascontiguousarray(np.asarray(ssm_b_re[j], f32)[gs]).reshape(NPAIR, 128, 16),
                      "bim": np.ascontiguousarray(np.asarray(ssm_b_im[j], f32)[gs]).reshape(NPAIR, 128, 16),
                      "cre": np.ascontiguousarray(np.asarray(ssm_c_re[j], f32)[gs].transpose(0, 2, 1)).reshape(NPAIR, 128, 16),
                      "cim": np.ascontiguousarray(np.asarray(ssm_c_im[j], f32)[gs].transpose(0, 2, 1)).reshape(NPAIR, 128, 16),
                      "dsk": np.ascontiguousarray(np.asarray(ssm_d[j], f32)[cs]).reshape(NPAIR, 32, 1)}
                maps.append(mm)
            r = _launch(progs["ssm"], maps)
            Wz = blockw(w_in[:, W_MIX:2 * W_MIX])
            Wout = blockw(np.asarray(ssm_w_out[j], f32))
        del hn_full, hn_b
        oT_full = np.concatenate([np.concatenate([r[bi * HG + hg]["oT"] for hg in range(HG)], axis=0)
                                  for bi in range(BATCH)], axis=1)
        final = i == DEPTH - 1
        ssm = kind == 2
        key = ("B", ssm, final)
        if key not in progs:
            ins = dict(b_ins)
            if ssm:
                ins["Wglu"] = (wblk, F32)
                ins["bglu"] = ((128, nK), F32)
            outs = {"out": ((D, TT), F32)} if final else {"hT_out": ((D, TT), F32), "hnT_out": ((D, TT), BF16)}
            progs[key] = _build(emit_B, ins, outs, D, TT, 512, ssm=ssm, final=final, PD=PLE)
        pT = np.ascontiguousarray(p[i].reshape(NTOK, PLE).T)
        shared = {"g2": colvec(norm_ple[i]), "gn": colvec(final_norm if final else norm_mix[i + 1]),
                  "Wz": Wz, "Wout": Wout, "Wg": blockw(np.asarray(ple_w_gate[i], f32)),
                  "Wp": blockw(np.asarray(ple_w_proj[i], f32))}
        if ssm:
            shared["Wglu"] = blockw(np.asarray(ssm_w_glu[j], f32))
            shared["bglu"] = colvec(ssm_b_glu[j])
        maps = []
        for c in range(NCB):
            mm = {"hT": hT_sh[c], "hnT": hn_sh[c], "oT": np.ascontiguousarray(oT_full[:, tok[c]]),
                  "pT": np.ascontiguousarray(pT[:, tok[c]])}
            mm.update(shared)
            maps.append(mm)
        del oT_full
        r = _launch(progs[key], maps)
        if final:
            outT = np.concatenate([r[c]["out"] for c in range(NCB)], axis=1)
            out = np.ascontiguousarray(outT.T).reshape(BATCH, SEQ, D)
        else:
            hT_sh = [r[c]["hT_out"] for c in range(NCB)]
            hn_sh = [r[c]["hnT_out"] for c in range(NCB)]
    return out.astype(np.float32)
```

```python
import contextlib
import numpy as np
import ml_dtypes
import concourse.bass as bass
import concourse.mybir as mybir
from concourse.bass_utils import run_bass_kernel_spmd

F32 = mybir.dt.float32
BF16 = mybir.dt.bfloat16
AF = mybir.ActivationFunctionType
ALU = mybir.AluOpType
AX = mybir.AxisListType
NCORES = 8


class Buf:
    __slots__ = ("name", "w", "r", "sem", "ndma", "excl")

    def __init__(self, name, excl=False):
        self.name = name
        self.excl = excl
        self.w = {}
        self.r = {}
        self.sem = None
        self.ndma = 0


def _merge(dst, src):
    for k, v in src.items():
        if dst.get(k, 0) < v:
            dst[k] = v


class Sched:
    ENG = ("pe", "act", "dve", "pool", "sp")

    def __init__(self, nc, stack):
        self.nc = nc
        self.stack = stack
        self.prog = {e: [] for e in self.ENG}
        self.count = {e: 0 for e in self.ENG}
        self.seen = {e: {} for e in self.ENG}
        self.sem = {}
        for e in ("pe", "act", "dve", "pool"):
            self.sem[e] = stack.enter_context(nc.semaphore("c_" + e))
        self.nbuf = 0
        self.slots = []

    def sb(self, name, shape, dt):
        return self.stack.enter_context(self.nc.sbuf_tensor(name, list(shape), dt))

    def ps(self, name, shape=(128, 512), dt=F32):
        return self.stack.enter_context(self.nc.psum_tensor(name, list(shape), dt))

    def buf(self, name=None, excl=False):
        self.nbuf += 1
        return Buf(name or "b%d" % self.nbuf, excl)

    def _waits(self, eng, deps, skip_self=False):
        out = []
        seen = self.seen[eng]
        for sem, v in deps.items():
            if skip_self and sem is self.sem.get(eng):
                continue
            if seen.get(sem, 0) < v:
                seen[sem] = v
                out.append((sem, v))
        return out

    def _deps(self, reads, writes):
        deps = {}
        for b in reads:
            _merge(deps, b.w)
            if b.excl:
                _merge(deps, b.r)
        for b in writes:
            _merge(deps, b.w)
            _merge(deps, b.r)
        return deps

    def op(self, eng, fn, reads=(), writes=()):
        deps = self._deps(reads, writes)
        waits = self._waits(eng, deps, skip_self=(eng == "pe"))
        self.count[eng] += 1
        tok = {self.sem[eng]: self.count[eng]}
        self.prog[eng].append((waits, fn, (self.sem[eng], 1)))
        for b in reads:
            _merge(b.r, tok)
        for b in writes:
            b.w = dict(tok)
            b.r = {}

    def dma(self, q, out_ap, in_ap, slot, reads=(), writes=(), **kw):
        if slot.sem is None:
            slot.sem = self.stack.enter_context(self.nc.semaphore("d_" + slot.name))
            self.slots.append(slot)
        deps = self._deps(reads, writes)
        waits = self._waits(q, deps)
        slot.ndma += 1
        tok = {slot.sem: 16 * slot.ndma}

        def fn(e, out_ap=out_ap, in_ap=in_ap, kw=kw):
            return e.dma_start(out=out_ap, in_=in_ap, **kw)

        self.prog[q].append((waits, fn, (slot.sem, 16)))
        for b in reads:
            _merge(b.r, tok)
        for b in writes:
            b.w = dict(tok)
            b.r = {}

    def finish(self, final_bufs):
        deps = {}
        for b in final_bufs:
            _merge(deps, b.w)
        for sl in self.slots:
            _merge(deps, {sl.sem: 16 * sl.ndma})
        fin = self._waits("sp", deps)
        self.prog["sp"].append((fin, None, None))
        nc = self.nc
        prog = self.prog

        def play(e, lst):
            for waits, fn, inc in lst:
                for sem, v in waits:
                    e.wait_ge(sem, v)
                if fn is not None:
                    ins = fn(e)
                    ins.then_inc(inc[0], inc[1])

        with nc.Block() as block:
            @block.tensor
            def _(e):
                play(e, prog["pe"])

            @block.scalar
            def _(e):
                play(e, prog["act"])

            @block.vector
            def _(e):
                play(e, prog["dve"])

            @block.gpsimd
            def _(e):
                play(e, prog["pool"])

            @block.sync
            def _(e):
                play(e, prog["sp"])


class Gemm:
    def __init__(self, S, nK, nslots=3, nps=2, tag="g"):
        self.S = S
        self.nK = nK
        self.wt = [S.sb("%s_w%d" % (tag, i), [128, nK, 128], BF16) for i in range(nslots)]
        self.wb = [S.buf("%s_wb%d" % (tag, i)) for i in range(nslots)]
        self.pt = [S.ps("%s_p%d" % (tag, i)) for i in range(nps)]
        self.pb = [S.buf("%s_pb%d" % (tag, i), True) for i in range(nps)]
        self.wi = 0
        self.pi = 0

    def precast(self, name, w_ap, nblk, nK=None):
        S = self.S
        nK = nK or self.nK
        wb16 = S.nc.dram_tensor(name, [nblk, 128, nK, 128], BF16).ap()
        buf = S.buf(name)
        for nb in range(nblk):
            S.dma("pool", wb16[nb], w_ap[nb], buf)
        buf.w = {buf.sem: 16 * buf.ndma}
        return wb16, buf

    def run(self, at, at_bufs, wsrc, nblocks, T, epi, nK=None, wq="sp", wbuf=None):
        S = self.S
        nK = nK or self.nK
        for nb in range(nblocks):
            wi = self.wi % len(self.wt)
            self.wi += 1
            wt, wb = self.wt[wi], self.wb[wi]
            S.dma(wq, wt[:, 0:nK, :], wsrc(nb), wb, reads=[wbuf] if wbuf is not None else [], writes=[wb])
            pi = self.pi % len(self.pt)
            self.pi += 1
            pt, pb = self.pt[pi], self.pb[pi]
            for kc in range(nK):
                def mm(e, kc=kc, wt=wt, pt=pt):
                    return e.matmul(pt[:, 0:T], wt[:, kc, :], at(kc),
                                    start=(kc == 0), stop=(kc == nK - 1))
                S.op("pe", mm, reads=[wb, at_bufs[kc]], writes=[pb])
            epi(nb, pb, pt[:, 0:T])


def blockw(W, nblk=None):
    K, N = W.shape
    return np.ascontiguousarray(
        W.reshape(K // 128, 128, N // 128, 128).transpose(2, 1, 0, 3))


def emit_B(S, d, D, TT, T, ssm=False, final=False, PD=256):
    nK = D // 128
    nP = PD // 128
    nt = TT // T
    h = S.sb("B_h", [128, nK, T], F32)
    xa = S.sb("B_xa", [128, nK, T], BF16)
    xb = S.sb("B_xb", [128, nK, T], BF16)
    xc = S.sb("B_xc", [128, nK, T], BF16) if ssm else None
    pt = S.sb("B_pT", [128, nP, T], BF16)
    ones = S.sb("B_ones", [128, 128], BF16)
    g2 = S.sb("B_g2", [128, nK], F32)
    gn = S.sb("B_gn", [128, nK], F32)
    bglu = S.sb("B_bglu", [128, nK], F32) if ssm else None
    rstd = S.sb("B_rstd", [128, T], F32)
    och = [S.sb("B_och%d" % i, [128, T], F32) for i in range(2)]
    t1 = [S.sb("B_t1_%d" % i, [128, T], F32) for i in range(2)]
    t2 = [S.sb("B_t2_%d" % i, [128, T], F32) for i in range(2)]
    sq = [S.sb("B_sq%d" % i, [128, T], BF16) for i in range(2)]
    wp = [S.sb("B_wp%d" % i, [128, nP, 128], BF16) for i in range(2)]
    fo = [S.sb("B_fo%d" % i, [128, T], F32) for i in range(2)] if final else None
    b_h = [S.buf("h%d" % i) for i in range(nK)]
    b_xa = [S.buf("xa%d" % i) for i in range(nK)]
    b_xb = [S.buf("xb%d" % i) for i in range(nK)]
    b_xc = [S.buf("xc%d" % i) for i in range(nK)]
    b_pt = S.buf("pt")
    b_const, b_rstd = S.buf("const"), S.buf("rstd")
    b_och = [S.buf("och%d" % i) for i in range(2)]
    b_t1 = [S.buf("t1_%d" % i) for i in range(2)]
    b_t2 = [S.buf("t2_%d" % i) for i in range(2)]
    b_sq = [S.buf("sq%d" % i) for i in range(2)]
    b_wp = [S.buf("wp%d" % i) for i in range(2)]
    b_fo = [S.buf("fo%d" % i) for i in range(2)]
    b_hd, b_hn = S.buf("hdram"), S.buf("hndram")
    pn = S.ps("B_pn")
    b_pn = S.buf("pn", True)
    pp = [S.ps("B_pp%d" % i) for i in range(2)]
    b_pp = [S.buf("pp%d" % i, True) for i in range(2)]
    G = Gemm(S, nK, nslots=3, nps=4 if ssm else 3, tag="B")
    Wz16, b_Wz = G.precast("B_Wz16", d["Wz"], nK)
    if ssm:
        Wglu16, b_Wglu = G.precast("B_Wglu16", d["Wglu"], nK)
    Wout16, b_Wout = G.precast("B_Wout16", d["Wout"], nK)
    Wg16, b_Wg = G.precast("B_Wg16", d["Wg"], nK)
    Wp16, b_Wp = G.precast("B_Wp16", d["Wp"], nK, nK=nP)

    S.op("pool", lambda e: e.memset(ones[:], 1.0), writes=[b_const])
    S.dma("pool", g2[:], d["g2"], b_const, writes=[b_const])
    S.dma("pool", gn[:], d["gn"], b_const, writes=[b_const])
    if ssm:
        S.dma("pool", bglu[:], d["bglu"], b_const, writes=[b_const])

    cnt = {"o": 0, "t": 0, "wp": 0}

    def norm_finish(b_src):
        S.op("dve", lambda e: e.tensor_scalar(rstd[:], pn[:, 0:T], 1.0 / D, 1e-6, ALU.mult, ALU.add),
             reads=[b_pn], writes=[b_rstd])
        S.op("act", lambda e: e.activation(out=rstd[:], in_=rstd[:], func=AF.Sqrt),
             reads=[b_rstd], writes=[b_rstd])
        S.op("dve", lambda e: e.reciprocal(rstd[:], rstd[:]), reads=[b_rstd], writes=[b_rstd])

    def sq_accum(nb):
        i = cnt["t"] % 2
        S.op("act", lambda e: e.activation(out=sq[i][:], in_=h[:, nb, :], func=AF.Square),
             reads=[b_h[nb]], writes=[b_sq[i]])
        S.op("pe", lambda e: e.matmul(pn[:, 0:T], ones[:], sq[i][:], start=(nb == 0), stop=(nb == nK - 1)),
             reads=[b_sq[i], b_const], writes=[b_pn])

    for ti in range(nt):
        ts = slice(ti * T, (ti + 1) * T)
        S.dma("pool", h[:], d["hT"][:, ts].rearrange("(c p) t -> p c t", p=128), b_h[0],
              reads=[b_hd], writes=b_h)
        S.dma("pool", xa[:], d["hnT"][:, ts].rearrange("(c p) t -> p c t", p=128), b_xa[0], writes=b_xa)
        S.dma("pool", pt[:], d["pT"][:, ts].rearrange("(c p) t -> p c t", p=128), b_pt, writes=[b_pt])
        if ssm:
            S.dma("pool", xc[:], d["oT"][:, ts].rearrange("(c p) t -> p c t", p=128), b_xc[0], writes=b_xc)

        def epi1(nb, pb, pap):
            i = cnt["o"] % 2
            cnt["o"] += 1
            S.dma("pool", och[i][:], d["oT"][nb * 128:(nb + 1) * 128, ts], b_och[i], writes=[b_och[i]])
            j = cnt["t"] % 2
            cnt["t"] += 1
            S.op("act", lambda e: e.activation(out=t1[j][:], in_=pap, func=AF.Silu), reads=[pb], writes=[b_t1[j]])
            S.op("dve", lambda e: e.tensor_tensor(xb[:, nb, :], t1[j][:], och[i][:], ALU.mult),
                 reads=[b_t1[j], b_och[i]], writes=[b_xb[nb]])

        def epi1_ssm_glu(nb, pb, pap):
            j = cnt["t"] % 2
            S.op("act", lambda e: e.activation(out=t2[j][:], in_=pap, func=AF.Sigmoid, bias=bglu[:, nb:nb + 1]),
                 reads=[pb, b_const], writes=[b_t2[j]])

        def epi1_ssm_z(nb, pb, pap):
            i = cnt["o"] % 2
            cnt["o"] += 1
            S.dma("pool", och[i][:], d["oT"][nb * 128:(nb + 1) * 128, ts], b_och[i], writes=[b_och[i]])
            j = cnt["t"] % 2
            cnt["t"] += 1
            S.op("act", lambda e: e.activation(out=t1[j][:], in_=pap, func=AF.Silu), reads=[pb], writes=[b_t1[j]])
            S.op("dve", lambda e: e.tensor_tensor(t1[j][:], t1[j][:], t2[j][:], ALU.mult),
                 reads=[b_t1[j], b_t2[j]], writes=[b_t1[j]])
            S.op("dve", lambda e: e.tensor_tensor(xb[:, nb, :], t1[j][:], och[i][:], ALU.mult),
                 reads=[b_t1[j], b_och[i]], writes=[b_xb[nb]])

        if not ssm:
            G.run(lambda kc: xa[:, kc, :], b_xa, lambda nb: Wz16[nb], nK, T, epi1, wbuf=b_Wz)
        else:
            for nb in range(nK):
                G.run(lambda kc: xc[:, kc, :], b_xc, lambda _, nb=nb: Wglu16[nb], 1, T,
                      lambda _, pb, pap, nb=nb: epi1_ssm_glu(nb, pb, pap), wbuf=b_Wglu)
                G.run(lambda kc: xa[:, kc, :], b_xa, lambda _, nb=nb: Wz16[nb], 1, T,
                      lambda _, pb, pap, nb=nb: epi1_ssm_z(nb, pb, pap), wbuf=b_Wz)

        def epi2(nb, pb, pap):
            S.op("dve", lambda e: e.tensor_tensor(h[:, nb, :], h[:, nb, :], pap, ALU.add),
                 reads=[pb, b_h[nb]], writes=[b_h[nb]])
            sq_accum(nb)
            cnt["t"] += 1

        G.run(lambda kc: xb[:, kc, :], b_xb, lambda nb: Wout16[nb], nK, T, epi2, wbuf=b_Wout)
        norm_finish(None)
        for nb in range(nK):
            S.op("dve", lambda e, nb=nb: e.scalar_tensor_tensor(xa[:, nb, :], h[:, nb, :], g2[:, nb:nb + 1], rstd[:],
                                                                 ALU.mult, ALU.mult),
                 reads=[b_h[nb], b_rstd, b_const], writes=[b_xa[nb]])

        def epi3(nb, pb, pap):
            w = cnt["wp"] % 2
            cnt["wp"] += 1
            S.dma("sp", wp[w][:], Wp16[nb], b_wp[w], reads=[b_Wp], writes=[b_wp[w]])
            for kc in range(nP):
                S.op("pe", lambda e, kc=kc: e.matmul(pp[w][:, 0:T], wp[w][:, kc, :], pt[:, kc, :],
                                                     start=(kc == 0), stop=(kc == nP - 1)),
                     reads=[b_wp[w], b_pt], writes=[b_pp[w]])
            j = cnt["t"] % 2
            S.op("act", lambda e: e.activation(out=t1[j][:], in_=pap, func=AF.Sigmoid), reads=[pb], writes=[b_t1[j]])
            S.op("dve", lambda e: e.tensor_tensor(t1[j][:], t1[j][:], pp[w][:, 0:T], ALU.mult),
                 reads=[b_t1[j], b_pp[w]], writes=[b_t1[j]])
            S.op("dve", lambda e: e.tensor_tensor(h[:, nb, :], h[:, nb, :], t1[j][:], ALU.add),
                 reads=[b_t1[j], b_h[nb]], writes=[b_h[nb]])
            sq_accum(nb)
            cnt["t"] += 1

        G.run(lambda kc: xa[:, kc, :], b_xa, lambda nb: Wg16[nb], nK, T, epi3, wbuf=b_Wg)
        norm_finish(None)
        if not final:
            for nb in range(nK):
                S.op("dve", lambda e, nb=nb: e.scalar_tensor_tensor(xb[:, nb, :], h[:, nb, :], gn[:, nb:nb + 1], rstd[:],
                                                                     ALU.mult, ALU.mult),
                     reads=[b_h[nb], b_rstd, b_const], writes=[b_xb[nb]])
            S.dma("pool", d["hnT_out"][:, ts].rearrange("(c p) t -> p c t", p=128), xb[:], b_xb[0],
                  reads=b_xb, writes=[b_hn])
            S.dma("pool", d["hT_out"][:, ts].rearrange("(c p) t -> p c t", p=128), h[:], b_h[0],
                  reads=b_h, writes=[b_hd])
        else:
            for nb in range(nK):
                i = nb % 2
                S.op("dve", lambda e, nb=nb, i=i: e.scalar_tensor_tensor(fo[i][:], h[:, nb, :], gn[:, nb:nb + 1], rstd[:],
                                                                          ALU.mult, ALU.mult),
                     reads=[b_h[nb], b_rstd, b_const], writes=[b_fo[i]])
                S.dma("pool", d["out"][nb * 128:(nb + 1) * 128, ts], fo[i][:], b_fo[i], reads=[b_fo[i]], writes=[b_hn])
    return [b_hd, b_hn]


def emit_N(S, d, D, TT, T):
    nK = D // 128
    h = S.sb("N_h", [128, nK, T], F32)
    xo = S.sb("N_xo", [128, nK, T], BF16)
    sq = [S.sb("N_sq%d" % i, [128, T], BF16) for i in range(2)]
    ones = S.sb("N_ones", [128, 128], BF16)
    g = S.sb("N_g", [128, nK], F32)
    rstd = S.sb("N_rstd", [128, T], F32)
    pn = S.ps("N_pn")
    b_h, b_xo, b_c, b_r, b_out = (S.buf() for _ in range(5))
    b_pn = S.buf("Npn", True)
    b_sq = [S.buf(), S.buf()]
    S.op("pool", lambda e: e.memset(ones[:], 1.0), writes=[b_c])
    S.dma("sp", g[:], d["g"], b_c, writes=[b_c])
    for ti in range(TT // T):
        ts = slice(ti * T, (ti + 1) * T)
        S.dma("sp", h[:], d["hT"][:, ts].rearrange("(c p) t -> p c t", p=128), b_h, writes=[b_h])
        for nb in range(nK):
            i = nb % 2
            S.op("act", lambda e, nb=nb, i=i: e.activation(out=sq[i][:], in_=h[:, nb, :], func=AF.Square),
                 reads=[b_h], writes=[b_sq[i]])
            S.op("pe", lambda e, nb=nb, i=i: e.matmul(pn[:, 0:T], ones[:], sq[i][:], start=(nb == 0), stop=(nb == nK - 1)),
                 reads=[b_sq[i], b_c], writes=[b_pn])
        S.op("dve", lambda e: e.tensor_scalar(rstd[:], pn[:, 0:T], 1.0 / D, 1e-6, ALU.mult, ALU.add),
             reads=[b_pn], writes=[b_r])
        S.op("act", lambda e: e.activation(out=rstd[:], in_=rstd[:], func=AF.Sqrt), reads=[b_r], writes=[b_r])
        S.op("dve", lambda e: e.reciprocal(rstd[:], rstd[:]), reads=[b_r], writes=[b_r])
        for nb in range(nK):
            S.op("dve", lambda e, nb=nb: e.scalar_tensor_tensor(xo[:, nb, :], h[:, nb, :], g[:, nb:nb + 1], rstd[:],
                                                                 ALU.mult, ALU.mult),
                 reads=[b_h, b_r, b_c], writes=[b_xo])
        S.dma("sp", d["hnT_out"][:, ts].rearrange("(c p) t -> p c t", p=128), xo[:], b_xo, reads=[b_xo], writes=[b_out])
    return [b_out]


def emit_Afox(S, d, D, NB, SL, NH):
    nc = S.nc
    nK = D // 128
    T = 512
    nseg = SL // T
    ntt = NB * nseg
    NP = NH * ntt
    assert NP <= 128
    nblk = SL // 128
    scale = 128 ** -0.5
    qkv_d = nc.dram_tensor("fox_qkv", [3 * NH, 128, NB * SL], BF16).ap()
    lf_d = nc.dram_tensor("fox_lf", [NH, ntt, T], F32).ap()
    cum_d = nc.dram_tensor("fox_cum", [NH, ntt, T], F32).ap()
    b_qkv, b_lf, b_cum, b_od = S.buf("qkv_d"), S.buf("lf_d"), S.buf("cum_d"), S.buf("oT_d")

    xa = S.sb("A_xa", [128, nK, T], BF16)
    b_xa = [S.buf("A_xa%d" % i) for i in range(nK)]
    st = [S.sb("A_st%d" % i, [128, T], BF16) for i in range(3)]
    b_st = [S.buf("A_st%d" % i) for i in range(3)]
    lft = [S.sb("A_lf%d" % i, [NH, T], F32) for i in range(2)]
    b_lft = [S.buf("A_lft%d" % i) for i in range(2)]
    negb = S.sb("A_negb", [NH, 1], F32)
    b_c = S.buf("A_const")
    S.dma("pool", negb[:], d["bf"], b_c, writes=[b_c])
    S.op("dve", lambda e: e.tensor_scalar(negb[:], negb[:], -1.0, None, ALU.mult), reads=[b_c], writes=[b_c])
    G = Gemm(S, nK, nslots=3, nps=2, tag="A")
    W16, b_W16 = G.precast("fox_W16", d["W"], 3 * NH + 1)
    cnt = {"st": 0, "lf": 0}
    for tt in range(ntt):
        ts = slice(tt * T, (tt + 1) * T)
        S.dma("pool", xa[:], d["hnT"][:, ts].rearrange("(c p) t -> p c t", p=128), b_xa[0], writes=b_xa)

        def epi(nb, pb, pap, tt=tt, ts=ts):
            if nb < 3 * NH:
                i = cnt["st"] % 3
                cnt["st"] += 1
                if i % 2 == 0:
                    S.op("act", lambda e: e.activation(out=st[i][:], in_=pap, func=AF.Copy), reads=[pb], writes=[b_st[i]])
                else:
                    S.op("dve", lambda e: e.tensor_copy(st[i][:], pap), reads=[pb], writes=[b_st[i]])
                S.dma("pool", qkv_d[nb, :, ts], st[i][:], b_st[i], reads=[b_st[i]], writes=[b_qkv])
            else:
                i = cnt["lf"] % 2
                cnt["lf"] += 1
                S.op("act", lambda e: e.activation(out=lft[i][:], in_=pap[0:NH, :], func=AF.Exp, scale=-1.0,
                                                   bias=negb[:, 0:1]), reads=[pb, b_c], writes=[b_lft[i]])
                S.op("act", lambda e: e.activation(out=lft[i][:], in_=lft[i][:], func=AF.Ln, bias=1.0),
                     reads=[b_lft[i]], writes=[b_lft[i]])
                S.dma("pool", lf_d[:, tt, :], lft[i][:], b_lft[i], reads=[b_lft[i]], writes=[b_lf])

        G.run(lambda kc: xa[:, kc, :], b_xa, lambda nb: W16[nb], 3 * NH + 1, T, epi, wbuf=b_W16)

    LF = S.sb("A_LF", [NP, T], F32)
    onesf = S.sb("A_onesf", [NP, T], F32)
    mtri = S.sb("A_mtri", [NP, NP], F32)
    tot = S.sb("A_tot", [NP, 1], F32)
    off = S.sb("A_off", [NP, 1], F32)
    b_LF, b_m = S.buf("LF"), S.buf("m")
    ptr = [S.ps("A_ptr%d" % i) for i in range(2)]
    b_ptr = [S.buf("A_ptr%d" % i, True) for i in range(2)]
    S.dma("sp", LF[:], lf_d.rearrange("h t f -> (h t) f"), b_LF, reads=[b_lf], writes=[b_LF])
    S.dma("sp", mtri[:], d["mtri"], b_m, writes=[b_m])
    S.op("pool", lambda e: e.memset(onesf[:], 1.0), writes=[b_m])
    LC = S.sb("A_LC", [NP, T], F32)
    b_LC = S.buf("LC")
    S.op("dve", lambda e: e.tensor_tensor_scan(LC[:], onesf[:], LF[:], 0.0, ALU.mult, ALU.subtract),
         reads=[b_LF, b_m], writes=[b_LC])
    S.op("dve", lambda e: e.tensor_copy(tot[:], LC[:, T - 1:T]), reads=[b_LC], writes=[b_m])
    S.op("pe", lambda e: e.matmul(ptr[0][0:NP, 0:1], mtri[:], tot[:], start=True, stop=True),
         reads=[b_m], writes=[b_ptr[0]])
    S.op("dve", lambda e: e.tensor_copy(off[:], ptr[0][0:NP, 0:1]), reads=[b_ptr[0]], writes=[b_m])
    S.op("dve", lambda e: e.tensor_scalar(LC[:], LC[:], off[:, 0:1], None, ALU.add), reads=[b_LC, b_m], writes=[b_LC])
    S.dma("sp", cum_d.rearrange("h t f -> (h t) f"), LC[:], b_LC, reads=[b_LC], writes=[b_cum])

    QT = S.sb("A_QT", [128, SL], BF16)
    KT = S.sb("A_KT", [128, SL], BF16)
    VT = S.sb("A_VT", [128, SL], BF16)
    VP = S.sb("A_VP", [128, nblk, 132], BF16)
    CB = S.sb("A_CB", [128, SL], F32)
    crow = S.sb("A_crow", [nblk, 128], F32)
    negcs = S.sb("A_negcs", [128, nblk], F32)
    kmax = S.sb("A_kmax", [128, 1], F32)
    kmx = S.sb("A_kmx", [128, SL // T], F32)
    identb = S.sb("A_identb", [128, 128], BF16)
    identf = S.sb("A_identf", [128, 128], F32)
    onesb = S.sb("A_onesb", [128, 128], BF16)
    tri = S.sb("A_tri", [128, 128], F32)
    sqt = [S.sb("A_sqt%d" % i, [128, T], BF16) for i in range(2)]
    tmp = [S.sb("A_tmp%d" % i, [128, T], F32) for i in range(3)]
    PT = [S.sb("A_PT%d" % i, [128, T], BF16) for i in range(3)]
    rinv = S.sb("A_rinv", [128, 4], F32)
    ot = [S.sb("A_ot%d" % i, [128, 128], F32) for i in range(2)]
    oT = [S.sb("A_oT%d" % i, [128, T], F32) for i in range(2)]
    b_QT, b_KT, b_VT, b_VP, b_CB, b_ncs, b_km = (S.buf(n) for n in ("QT", "KT", "VT", "VP", "CB", "ncs", "km"))
    b_crow = S.buf("crow")
    b_sqt = [S.buf(), S.buf()]
    b_tmp = [S.buf() for _ in range(3)]
    b_PT = [S.buf() for _ in range(3)]
    b_rinv = S.buf("rinv")
    b_ot = [S.buf(), S.buf()]
    b_oT = [S.buf(), S.buf()]
    pS = G.pt
    b_pS = G.pb
    pO = [S.ps("A_pO%d" % i) for i in range(4)]
    b_pO = [S.buf("A_pO%d" % i, True) for i in range(4)]
    ptrb = ptr[1]
    pTb = pO[3][:].bitcast(BF16)
    b_pTb = b_pO[3]
    S.dma("sp", identb[:], d["identb"], b_c, writes=[b_c])
    S.dma("sp", identf[:], d["identf"], b_c, writes=[b_c])
    S.dma("sp", tri[:], d["tri"], b_c, writes=[b_c])
    S.op("pool", lambda e: e.memset(onesb[:], 1.0), writes=[b_c])
    S.op("pool", lambda e: e.memset(VP[:], 1.0), writes=[b_VP])
    k = {"s": 0, "t": 0, "p": 0, "o": 0, "oT": 0, "sq": 0}
    for b in range(NB):
        for hl in range(NH):
            tsl = slice(b * SL, (b + 1) * SL)
            S.dma("sp", QT[:], qkv_d[hl, :, tsl], b_QT, reads=[b_qkv], writes=[b_QT])
            S.dma("sp", KT[:], qkv_d[NH + hl, :, tsl], b_KT, reads=[b_qkv], writes=[b_KT])
            S.dma("sp", VT[:], qkv_d[2 * NH + hl, :, tsl], b_VT, reads=[b_qkv], writes=[b_VT])
            cum_row = cum_d[hl, b * nseg:(b + 1) * nseg, :]
            S.dma("sp", CB[:], cum_row.rearrange("s f -> (s f)").partition_broadcast(128), b_CB,
                  reads=[b_cum], writes=[b_CB])
            S.dma("sp", crow[:], cum_row.rearrange("s (j p) -> (s j) p", p=128), b_crow, reads=[b_cum], writes=[b_crow])
            S.op("pe", lambda e: e.transpose(ptr[1][:, 0:nblk], crow[:], identf[0:nblk, 0:nblk]),
                 reads=[b_crow, b_c], writes=[b_ptr[1]])
            S.op("dve", lambda e: e.tensor_scalar(negcs[:], ptr[1][:, 0:nblk], -1.0, None, ALU.mult),
                 reads=[b_ptr[1]], writes=[b_ncs])
            for j0 in range(0, nblk, 8):
                nj = min(8, nblk - j0)
                for jj in range(nj):
                    j = j0 + jj
                    S.op("pe", lambda e, j=j, jj=jj: e.transpose(pTb[:, jj * 128:(jj + 1) * 128], VT[:, j * 128:(j + 1) * 128], identb[:]),
                         reads=[b_VT, b_c], writes=[b_pTb])
                S.op("act", lambda e, j0=j0, nj=nj: e.activation(
                    out=VP[:, j0:j0 + nj, 0:128], in_=pTb[:, 0:nj * 128].rearrange("p (j d) -> p j d", d=128), func=AF.Copy),
                    reads=[b_pTb], writes=[b_VP])
            for c in range(SL // T):
                i = k["sq"] % 2
                k["sq"] += 1
                S.op("pool", lambda e, c=c, i=i: e.tensor_tensor(sqt[i][:], KT[:, c * T:(c + 1) * T], KT[:, c * T:(c + 1) * T], ALU.mult),
                     reads=[b_KT], writes=[b_sqt[i]])
                S.op("pe", lambda e, i=i: e.matmul(ptr[0][:, 0:T], onesb[:], sqt[i][:], start=True, stop=True),
                     reads=[b_sqt[i], b_c], writes=[b_ptr[0]])
                S.op("dve", lambda e, c=c: e.reduce_max(kmx[:, c:c + 1], ptr[0][:, 0:T], axis=AX.X),
                     reads=[b_ptr[0]], writes=[b_km])
            S.op("dve", lambda e: e.reduce_max(kmax[:], kmx[:], axis=AX.X), reads=[b_km], writes=[b_km])
            for c in range(SL // T):
                i = k["sq"] % 2
                k["sq"] += 1
                ti = k["t"] % 3
                k["t"] += 1
                S.op("pool", lambda e, c=c, i=i: e.tensor_tensor(sqt[i][:], QT[:, c * T:(c + 1) * T], QT[:, c * T:(c + 1) * T], ALU.mult),
                     reads=[b_QT], writes=[b_sqt[i]])
                S.op("pe", lambda e, i=i: e.matmul(ptr[0][:, 0:T], onesb[:], sqt[i][:], start=True, stop=True),
                     reads=[b_sqt[i], b_c], writes=[b_ptr[0]])
                S.op("act", lambda e, ti=ti: e.activation(out=tmp[ti][:], in_=ptr[0][:, 0:T], func=AF.Sqrt, scale=kmax[:, 0:1]),
                     reads=[b_ptr[0], b_km], writes=[b_tmp[ti]])
                S.op("dve", lambda e, c=c, ti=ti: e.scalar_tensor_tensor(CB[:, c * T:(c + 1) * T], tmp[ti][:], -scale,
                                                                         CB[:, c * T:(c + 1) * T], ALU.mult, ALU.add),
                     reads=[b_tmp[ti], b_CB], writes=[b_CB])
            def emit_qk(I, j):
                r = max(0, j - 4 * I)
                c0 = r * 128
                W_ = T - c0
                si = k["s"] % 2
                k["s"] += 1
                S.op("pe", lambda e: e.matmul(
                    pS[si][:, 0:W_], KT[:, j * 128:(j + 1) * 128], QT[:, I * T + c0:(I + 1) * T], start=True, stop=True),
                    reads=[b_KT, b_QT], writes=[b_pS[si]])
                return (I, j, r, c0, W_, si)

            tiles = [(I, j) for I in range(SL // T) for j in range(4 * I + 4)]
            nxt = emit_qk(*tiles[0])
            for n, (I, j) in enumerate(tiles):
                _, _, r, c0, W_, si = nxt
                if n + 1 < len(tiles):
                    nxt = emit_qk(*tiles[n + 1])
                ti = k["t"] % 3
                k["t"] += 1
                pi = k["p"] % 3
                k["p"] += 1
                S.op("dve", lambda e, I=I, c0=c0, W_=W_, si=si, ti=ti: e.scalar_tensor_tensor(
                    tmp[ti][:, 0:W_], pS[si][:, 0:W_], scale, CB[:, I * T + c0:(I + 1) * T], ALU.mult, ALU.add),
                    reads=[b_pS[si], b_CB], writes=[b_tmp[ti]])
                if j >= 4 * I:
                    S.op("pool", lambda e, ti=ti: e.tensor_tensor(tmp[ti][:, 0:128], tmp[ti][:, 0:128], tri[:], ALU.add),
                         reads=[b_tmp[ti], b_c], writes=[b_tmp[ti]])
                S.op("act", lambda e, j=j, W_=W_, ti=ti, pi=pi: e.activation(
                    out=PT[pi][:, 0:W_], in_=tmp[ti][:, 0:W_], func=AF.Exp, bias=negcs[:, j:j + 1]),
                    reads=[b_tmp[ti], b_ncs], writes=[b_PT[pi]])
                for u in range(r, 4):
                    S.op("pe", lambda e, j=j, u=u, r=r, pi=pi, I=I: e.matmul(
                        pO[u][:, 0:129], PT[pi][:, (u - r) * 128:(u - r + 1) * 128], VP[:, j, 0:129],
                        start=(j == 0), stop=(j == 4 * I + u)),
                        reads=[b_PT[pi], b_VP], writes=[b_pO[u]])
                if j != 4 * I + 3:
                    continue
                oi = k["oT"] % 2
                k["oT"] += 1
                for u in range(4):
                    S.op("dve", lambda e, u=u: e.reciprocal(rinv[:, u:u + 1], pO[u][:, 128:129]),
                         reads=[b_pO[u]], writes=[b_rinv])
                    o_i = k["o"] % 2
                    k["o"] += 1
                    S.op("act", lambda e, u=u, o_i=o_i: e.activation(out=ot[o_i][:], in_=pO[u][:, 0:128], func=AF.Copy,
                                                                   scale=rinv[:, u:u + 1]),
                         reads=[b_pO[u], b_rinv], writes=[b_ot[o_i]])
                    S.op("pe", lambda e, o_i=o_i: e.transpose(ptr[1][:, 0:128], ot[o_i][:], identf[:]),
                         reads=[b_ot[o_i], b_c], writes=[b_ptr[1]])
                    S.op("dve", lambda e, u=u, oi=oi: e.tensor_copy(oT[oi][:, u * 128:(u + 1) * 128], ptr[1][:, 0:128]),
                         reads=[b_ptr[1]], writes=[b_oT[oi]])
                S.dma("sp", d["oT"][hl * 128:(hl + 1) * 128, b * SL + I * T: b * SL + (I + 1) * T], oT[oi][:], b_oT[oi],
                      reads=[b_oT[oi]], writes=[b_od])
    return [b_od]


_STOP = [None]


class _Stop(Exception):
    pass


def _chk(k):
    if _STOP[0] == k:
        raise _Stop()


def emit_Agdn(S, d, D, NB, SL, NH):
    try:
        return _emit_Agdn(S, d, D, NB, SL, NH)
    except _Stop:
        return []


def _emit_Agdn(S, d, D, NB, SL, NH):
    nc = S.nc
    nK = D // 128
    T = 512
    C = 128
    ntt = NB * SL // T
    ngrp = SL // T
    R = NH * NB * SL // C
    qkv_d = nc.dram_tensor("gdn_qkv", [3 * NH, 128, NB * SL], F32).ap()
    bg_d = nc.dram_tensor("gdn_bg", [2, NH, NB * SL], F32).ap()
    gc_d = nc.dram_tensor("gdn_gc", [NH, NB * SL], F32).ap()
    b_qkv, b_bg, b_gc, b_od = S.buf("gqkv"), S.buf("gbg"), S.buf("ggc"), S.buf("goT")
    b_c = S.buf("gconst")

    def V(fn, r=(), w=()):
        S.op("dve", fn, reads=r, writes=w)

    def A_(fn, r=(), w=()):
        S.op("act", fn, reads=r, writes=w)

    def P_(fn, r=(), w=()):
        S.op("pool", fn, reads=r, writes=w)

    def M_(fn, r=(), w=()):
        S.op("pe", fn, reads=r, writes=w)

    xa = S.sb("G_xa", [128, nK, T], BF16)
    b_xa = [S.buf("G_xa%d" % i) for i in range(nK)]
    st = [S.sb("G_st%d" % i, [128, T], F32) for i in range(3)]
    b_st = [S.buf("G_st%d" % i) for i in range(3)]
    g8 = 2 * NH
    gt = [S.sb("G_gt%d" % i, [g8, T], F32) for i in range(2)]
    gs = [S.sb("G_gs%d" % i, [g8, T], F32) for i in range(2)]
    b_gt = [S.buf("G_gt%d" % i) for i in range(2)]
    b_gs = [S.buf("G_gs%d" % i) for i in range(2)]
    gbias = S.sb("G_gbias", [g8, 1], F32)
    gcoef = S.sb("G_gcoef", [g8, 1], F32)
    S.dma("pool", gbias[:], d["gbias"], b_c, writes=[b_c])
    S.dma("pool", gcoef[:], d["galog"], b_c, writes=[b_c])
    A_(lambda e: e.activation(out=gcoef[:], in_=gcoef[:], func=AF.Exp), [b_c], [b_c])
    V(lambda e: e.tensor_scalar(gcoef[:], gcoef[:], -1.0, None, ALU.mult), [b_c], [b_c])
    G = Gemm(S, nK, nslots=3, nps=2, tag="G")
    W16, b_W16 = G.precast("gdn_W16", d["W"], 3 * NH + 1)
    cnt = {"st": 0, "g": 0}
    for tt in range(ntt):
        ts = slice(tt * T, (tt + 1) * T)
        S.dma("pool", xa[:], d["hnT"][:, ts].rearrange("(c p) t -> p c t", p=128), b_xa[0], writes=b_xa)

        def epi(nb, pb, pap, ts=ts):
            if nb < 3 * NH:
                i = cnt["st"] % 3
                cnt["st"] += 1
                if i % 2 == 0:
                    A_(lambda e: e.activation(out=st[i][:], in_=pap, func=AF.Copy), [pb], [b_st[i]])
                else:
                    V(lambda e: e.tensor_copy(st[i][:], pap), [pb], [b_st[i]])
                S.dma("pool", qkv_d[nb, :, ts], st[i][:], b_st[i], reads=[b_st[i]], writes=[b_qkv])
            else:
                i = cnt["g"] % 2
                cnt["g"] += 1
                A_(lambda e: e.activation(out=gs[i][:], in_=pap[0:g8, :], func=AF.Sigmoid), [pb], [b_gs[i]])
                A_(lambda e: e.activation(out=gt[i][:], in_=pap[0:g8, :], func=AF.Exp, bias=gbias[:, 0:1]), [pb, b_c], [b_gt[i]])
                A_(lambda e: e.activation(out=gt[i][:], in_=gt[i][:], func=AF.Ln, bias=1.0), [b_gt[i]], [b_gt[i]])
                V(lambda e: e.tensor_scalar(gt[i][:], gt[i][:], gcoef[:, 0:1], None, ALU.mult), [b_gt[i], b_c], [b_gt[i]])
                S.dma("pool", bg_d[0, :, ts], gs[i][0:NH, :], b_gs[i], reads=[b_gs[i]], writes=[b_bg])
                S.dma("pool", bg_d[1, :, ts], gt[i][NH:g8, :], b_gt[i], reads=[b_gt[i]], writes=[b_bg])

        G.run(lambda kc: xa[:, kc, :], b_xa, lambda nb: W16[nb], 3 * NH + 1, T, epi, wbuf=b_W16)

    _chk(1)
    onesr = S.sb("G_onesr", [128, C], F32)
    P_(lambda e: e.memset(onesr[:], 1.0), [], [b_c])
    gr = [S.sb("G_gr%d" % i, [128, C], F32) for i in range(2)]
    gq = [S.sb("G_gq%d" % i, [128, C], F32) for i in range(2)]
    b_gr = [S.buf(), S.buf()]
    b_gq = [S.buf(), S.buf()]
    g_rows = bg_d[1].rearrange("h (n c) -> (h n) c", c=C)
    gc_rows = gc_d.rearrange("h (n c) -> (h n) c", c=C)
    for r0 in range(0, R, 128):
        nr = min(128, R - r0)
        i = (r0 // 128) % 2
        S.dma("sp", gr[i][0:nr, :], g_rows[r0:r0 + nr, :], b_gr[i], reads=[b_bg], writes=[b_gr[i]])
        V(lambda e, i=i, nr=nr: e.tensor_tensor_scan(gq[i][0:nr, :], onesr[0:nr, :], gr[i][0:nr, :], 0.0, ALU.mult, ALU.add),
          [b_gr[i], b_c], [b_gq[i]])
        S.dma("sp", gc_rows[r0:r0 + nr, :], gq[i][0:nr, :], b_gq[i], reads=[b_gq[i]], writes=[b_gc])

    _chk(2)
    def sbt(name, shape, dt=F32):
        return S.sb("G_" + name, shape, dt), S.buf("G_" + name)

    mstr, _ = sbt("mstr", [128, T]); mup, _ = sbt("mup", [128, T]); id4, _ = sbt("id4", [128, T])
    idf, _ = sbt("idf", [128, 128]); onesf, _ = sbt("onesf", [128, 128]); nwr, _ = sbt("nwr", [128, 128])
    for tle, key in ((mstr, "mstr4"), (mup, "mup4"), (id4, "id4"), (idf, "identf")):
        S.dma("sp", tle[:], d[key], b_c, writes=[b_c])
    S.dma("sp", nwr[:], d["nw"].partition_broadcast(128), b_c, writes=[b_c])
    P_(lambda e: e.memset(onesf[:], 1.0), [], [b_c])
    cw, b_cw = sbt("cw", [128, 3, 4])
    xin = [[sbt("x%d_%d" % (a, i), [128, T + 3]) for a in range(3)] for i in range(2)]
    GB = [sbt("GB%d" % i, [128, T]) for i in range(2)]
    rows = [sbt("rows%d" % i, [8, C]) for i in range(2)]
    yq, b_yq = sbt("yq", [128, T]); yk, b_yk = sbt("yk", [128, T]); yv, b_yv = sbt("yv", [128, T])
    sq, b_sq = sbt("sq", [128, T]); rr, b_rr = sbt("rr", [128, T])
    qn, b_qn = sbt("qn", [128, T]); kn, b_kn = sbt("kn", [128, T])
    kTM, b_kTM = sbt("kTM", [128, 4, C]); vTM, b_vTM = sbt("vTM", [128, 4, C])
    gcol, b_gcol = sbt("gcol", [128, 8])
    cols, b_cols = sbt("cols", [128, 24])
    bv, b_bv = sbt("bv", [128, 4, C]); kbg, b_kbg = sbt("kbg", [128, 4, C]); kdec, b_kdec = sbt("kdec", [128, 4, C])
    t1, b_t1 = sbt("t1", [128, T]); t2, b_t2 = sbt("t2", [128, T])
    E1, b_E1 = sbt("E1", [128, T]); E2, b_E2 = sbt("E2", [128, T]); eGB, b_eGB = sbt("eGB", [128, T])
    aT, b_aT = sbt("aT", [128, T]); qd, b_qd = sbt("qd", [128, T])
    Pm = [sbt("P%d" % i, [128, T]) for i in range(2)]
    PTm = [sbt("PT%d" % i, [128, T]) for i in range(2)]
    TTm = [sbt("TT%d" % i, [128, T]) for i in range(2)]
    u, b_u = sbt("u", [128, 4, C]); wT, b_wT = sbt("wT", [128, T])
    Sst, b_S = sbt("S", [128, C]); vnew, b_vnew = sbt("vnew", [128, C])
    ssq, b_ssq = sbt("ssq", [128, 1]); rstd, b_rstd = sbt("rstd", [128, 1]); junk, b_junk = sbt("junk", [128, C])
    oTM, b_oTM = sbt("oTM", [128, C])
    oFM = [sbt("oFM%d" % i, [128, T]) for i in range(2)]
    pk = [(S.ps("G_pk%d" % i), S.buf("G_pk%d" % i, True)) for i in range(6)]
    pk = [(G.pt[0], G.pb[0]), (G.pt[1], G.pb[1])] + pk
    (pA, b_pA), (pB, b_pB), (pC, b_pC), (pD, b_pD), (pE, b_pE), (pF, b_pF), (pG, b_pG), (pH, b_pH) = pk
    gi = 0
    for b in range(NB):
        for hl in range(NH):
            S.dma("sp", cw[:], d["cw"].rearrange("(a h) p j -> h p a j", h=NH)[hl], b_cw, writes=[b_cw])
            V(lambda e: e.memset(Sst[:], 0.0), [], [b_S])
            for g in range(ngrp):
                par = gi % 2
                gi += 1
                t0 = b * SL + g * T
                for a in range(3):
                    xt, xb_ = xin[par][a]
                    if g == 0:
                        P_(lambda e, xt=xt: e.memset(xt[:, 0:3], 0.0), [], [xb_])
                        S.dma("sp", xt[:, 3:T + 3], qkv_d[a * NH + hl, :, t0:t0 + T], xb_, reads=[b_qkv], writes=[xb_])
                    else:
                        S.dma("sp", xt[:], qkv_d[a * NH + hl, :, t0 - 3:t0 + T], xb_, reads=[b_qkv], writes=[xb_])
                GBt, b_GB = GB[par]
                S.dma("sp", GBt[:], gc_d[hl, t0:t0 + T].partition_broadcast(128), b_GB, reads=[b_gc], writes=[b_GB])
                rw, b_rw = rows[par]
                S.dma("sp", rw[0:4, :], gc_d[hl, t0:t0 + T].rearrange("(n c) -> n c", c=C), b_rw, reads=[b_gc], writes=[b_rw])
                S.dma("sp", rw[4:8, :], bg_d[0, hl, t0:t0 + T].rearrange("(n c) -> n c", c=C), b_rw, reads=[b_bg], writes=[b_rw])
                for a, (yt, yb) in enumerate(((yq, b_yq), (yk, b_yk), (yv, b_yv))):
                    xt, xb_ = xin[par][a]
                    V(lambda e, xt=xt, yt=yt, a=a: e.tensor_scalar(yt[:], xt[:, 3:T + 3], cw[:, a, 3:4], None, ALU.mult),
                      [xb_, b_cw], [yb])
                    for j in range(3):
                        V(lambda e, xt=xt, yt=yt, a=a, j=j: e.scalar_tensor_tensor(
                            yt[:], xt[:, j:j + T], cw[:, a, j:j + 1], yt[:], ALU.mult, ALU.add), [xb_, b_cw, yb], [yb])
                    A_(lambda e, yt=yt: e.activation(out=yt[:], in_=yt[:], func=AF.Silu), [yb], [yb])
                _chk(3)
                for (yt, yb, ot_, ob, sc, pp_, pb_) in ((yq, b_yq, qn, b_qn, 128 ** -0.5, pA, b_pA), (yk, b_yk, kn, b_kn, 1.0, pB, b_pB)):
                    P_(lambda e, yt=yt: e.tensor_tensor(sq[:], yt[:], yt[:], ALU.mult), [yb], [b_sq])
                    M_(lambda e, pp_=pp_: e.matmul(pp_[:, 0:T], onesf[:], sq[:], start=True, stop=True), [b_sq, b_c], [pb_])
                    V(lambda e, pp_=pp_: e.tensor_scalar(rr[:], pp_[:, 0:T], 1e-6, None, ALU.add), [pb_], [b_rr])
                    A_(lambda e: e.activation(out=rr[:], in_=rr[:], func=AF.Sqrt), [b_rr], [b_rr])
                    V(lambda e: e.reciprocal(rr[:], rr[:]), [b_rr], [b_rr])
                    V(lambda e, yt=yt, ot_=ot_, sc=sc: e.scalar_tensor_tensor(ot_[:], yt[:], sc, rr[:], ALU.mult, ALU.mult),
                      [yb, b_rr], [ob])
                _chk(4)
                M_(lambda e, rw=rw: e.transpose(pC[:, 0:8], rw[:], idf[0:8, 0:8]), [b_rw, b_c], [b_pC])
                V(lambda e: e.tensor_copy(gcol[:], pC[:, 0:8]), [b_pC], [b_gcol])
                V(lambda e: e.tensor_scalar(cols[:, 0:4], gcol[:, 4:8], -1.0, None, ALU.mult), [b_gcol], [b_cols])
                V(lambda e: e.tensor_scalar(cols[:, 4:8], gcol[:, 0:4], -1.0, None, ALU.mult), [b_gcol, b_cols], [b_cols])
                A_(lambda e: e.activation(out=cols[:, 8:12], in_=gcol[:, 0:4], func=AF.Exp), [b_gcol, b_cols], [b_cols])
                V(lambda e: e.tensor_tensor(cols[:, 8:12], cols[:, 8:12], gcol[:, 4:8], ALU.mult), [b_gcol, b_cols], [b_cols])
                V(lambda e, GBt=GBt: e.tensor_tensor(cols[:, 12:16], GBt[:].rearrange("p (n c) -> p n c", c=C)[:, :, C - 1], gcol[:, 0:4], ALU.subtract),
                  [b_GB, b_gcol, b_cols], [b_cols])
                A_(lambda e: e.activation(out=cols[:, 12:16], in_=cols[:, 12:16], func=AF.Exp), [b_cols], [b_cols])
                A_(lambda e, GBt=GBt: e.activation(out=cols[:, 16:20], in_=GBt[:].rearrange("p (n c) -> p n c", c=C)[:, :, C - 1], func=AF.Exp),
                   [b_GB, b_cols], [b_cols])
                _chk(5)
                for n in range(4):
                    M_(lambda e, n=n: e.transpose(pA[:, n * C:(n + 1) * C], kn[:, n * C:(n + 1) * C], idf[:]), [b_kn, b_c], [b_pA])
                    M_(lambda e, n=n: e.transpose(pB[:, n * C:(n + 1) * C], yv[:, n * C:(n + 1) * C], idf[:]), [b_yv, b_c], [b_pB])
                A_(lambda e: e.activation(out=kTM[:].rearrange("p n c -> p (n c)"), in_=pA[:, 0:T], func=AF.Copy), [b_pA], [b_kTM])
                A_(lambda e: e.activation(out=vTM[:].rearrange("p n c -> p (n c)"), in_=pB[:, 0:T], func=AF.Copy), [b_pB], [b_vTM])
                for n in range(4):
                    P_(lambda e, n=n: e.tensor_scalar(bv[:, n, :], vTM[:, n, :], gcol[:, 4 + n:5 + n], None, ALU.mult), [b_vTM, b_gcol], [b_bv])
                    P_(lambda e, n=n: e.tensor_scalar(kbg[:, n, :], kTM[:, n, :], cols[:, 8 + n:9 + n], None, ALU.mult), [b_kTM, b_cols], [b_kbg])
                    P_(lambda e, n=n: e.tensor_scalar(kdec[:, n, :], kTM[:, n, :], cols[:, 12 + n:13 + n], None, ALU.mult), [b_kTM, b_cols], [b_kdec])
                _chk(6)
                V(lambda e, GBt=GBt: e.scalar_tensor_tensor(t1[:], GBt[:], -1.0, mstr[:], ALU.mult, ALU.add), [b_GB, b_c], [b_t1])
                V(lambda e, GBt=GBt: e.tensor_tensor(t2[:], GBt[:], mup[:], ALU.add), [b_GB, b_c], [b_t2])
                for n in range(4):
                    cs = slice(n * C, (n + 1) * C)
                    A_(lambda e, n=n, cs=cs: e.activation(out=E1[:, cs], in_=t1[:, cs], func=AF.Exp, bias=gcol[:, n:n + 1]), [b_t1, b_gcol], [b_E1])
                    A_(lambda e, n=n, cs=cs: e.activation(out=E2[:, cs], in_=t2[:, cs], func=AF.Exp, bias=cols[:, 4 + n:5 + n]), [b_t2, b_cols], [b_E2])
                A_(lambda e, GBt=GBt: e.activation(out=eGB[:], in_=GBt[:], func=AF.Exp), [b_GB], [b_eGB])
                V(lambda e: e.tensor_tensor(qd[:], qn[:], eGB[:], ALU.mult), [b_qn, b_eGB], [b_qd])
                _chk(7)
                for n in range(4):
                    cs = slice(n * C, (n + 1) * C)
                    M_(lambda e, cs=cs: e.matmul(pC[:, cs], kn[:, cs], kn[:, cs], start=True, stop=True), [b_kn], [b_pC])
                    M_(lambda e, cs=cs: e.matmul(pD[:, cs], kn[:, cs], qn[:, cs], start=True, stop=True), [b_kn, b_qn], [b_pD])
                _chk(71)
                P0, b_P0 = Pm[0]
                PT0, b_PT0 = PTm[0]
                TT0, b_TT0 = TTm[0]
                for n in range(4):
                    cs = slice(n * C, (n + 1) * C)
                    V(lambda e, n=n, cs=cs: e.scalar_tensor_tensor(P0[:, cs], pC[:, cs], cols[:, n:n + 1], E1[:, cs], ALU.mult, ALU.mult),
                      [b_pC, b_cols, b_E1], [b_P0])
                V(lambda e: e.tensor_tensor(aT[:], pD[:, 0:T], E2[:], ALU.mult), [b_pD, b_E2], [b_aT])
                _chk(72)
                for n in range(4):
                    cs = slice(n * C, (n + 1) * C)
                    M_(lambda e, cs=cs: e.transpose(pE[:, cs], P0[:, cs], idf[:]), [b_P0, b_c], [b_pE])
                A_(lambda e: e.activation(out=PT0[:], in_=pE[:, 0:T], func=AF.Copy), [b_pE], [b_PT0])
                V(lambda e: e.tensor_tensor(TT0[:], pE[:, 0:T], id4[:], ALU.add), [b_pE, b_c], [b_TT0])
                _chk(73)
                cur = 0
                for step in range(6):
                    Pc, b_Pc = Pm[cur]; PTc, b_PTc = PTm[cur]; TTc, b_TTc = TTm[cur]
                    Pn, b_Pn = Pm[1 - cur]; PTn, b_PTn = PTm[1 - cur]; TTn, b_TTn = TTm[1 - cur]
                    last = step == 5
                    for n in range(4):
                        cs = slice(n * C, (n + 1) * C)
                        M_(lambda e, cs=cs, Pc=Pc, PTc=PTc: e.matmul(pC[:, cs], PTc[:, cs], Pc[:, cs], start=True, stop=True), [b_Pc, b_PTc], [b_pC])
                    A_(lambda e, Pn=Pn: e.activation(out=Pn[:], in_=pC[:, 0:T], func=AF.Copy), [b_pC], [b_Pn])
                    if not last:
                        for n in range(4):
                            cs = slice(n * C, (n + 1) * C)
                            M_(lambda e, cs=cs, Pc=Pc, PTc=PTc: e.matmul(pD[:, cs], Pc[:, cs], PTc[:, cs], start=True, stop=True), [b_Pc, b_PTc], [b_pD])
                        V(lambda e, PTn=PTn: e.tensor_copy(PTn[:], pD[:, 0:T]), [b_pD], [b_PTn])
                    for n in range(4):
                        cs = slice(n * C, (n + 1) * C)
                        M_(lambda e, cs=cs, Pn=Pn, TTc=TTc: e.matmul(pE[:, cs], Pn[:, cs], TTc[:, cs], start=True, stop=True), [b_Pn, b_TTc], [b_pE])
                    V(lambda e, TTn=TTn, TTc=TTc: e.tensor_tensor(TTn[:], pE[:, 0:T], TTc[:], ALU.add), [b_pE, b_TTc], [b_TTn])
                    cur = 1 - cur
                    _chk(74 + step)
                TTf, b_TTf = TTm[cur]
                _chk(8)
                for n in range(4):
                    cs = slice(n * C, (n + 1) * C)
                    M_(lambda e, n=n, cs=cs: e.matmul(pA[:, cs], TTf[:, cs], bv[:, n, :], start=True, stop=True), [b_TTf, b_bv], [b_pA])
                    M_(lambda e, n=n, cs=cs: e.matmul(pB[:, cs], kbg[:, n, :], TTf[:, cs], start=True, stop=True), [b_TTf, b_kbg], [b_pB])
                A_(lambda e: e.activation(out=u[:].rearrange("p n c -> p (n c)"), in_=pA[:, 0:T], func=AF.Copy), [b_pA], [b_u])
                V(lambda e: e.tensor_copy(wT[:], pB[:, 0:T]), [b_pB], [b_wT])
                _chk(9)
                oF, b_oF = oFM[par]
                for n in range(4):
                    cs = slice(n * C, (n + 1) * C)
                    M_(lambda e, cs=cs: e.matmul(pF[:, 0:C], wT[:, cs], Sst[:], start=True, stop=True), [b_wT, b_S], [b_pF])
                    V(lambda e, n=n: e.tensor_tensor(vnew[:], u[:, n, :], pF[:, 0:C], ALU.subtract), [b_u, b_pF], [b_vnew])
                    M_(lambda e, cs=cs: e.matmul(pG[:, 0:C], qd[:, cs], Sst[:], start=True, stop=False), [b_qd, b_S], [b_pG])
                    M_(lambda e, cs=cs: e.matmul(pG[:, 0:C], aT[:, cs], vnew[:], start=False, stop=True), [b_aT, b_vnew], [b_pG])
                    M_(lambda e, n=n: e.matmul(pF[:, C:2 * C], kdec[:, n, :], vnew[:], start=True, stop=True), [b_kdec, b_vnew], [b_pF])
                    V(lambda e, n=n: e.scalar_tensor_tensor(Sst[:], Sst[:], cols[:, 16 + n:17 + n], pF[:, C:2 * C], ALU.mult, ALU.add),
                      [b_S, b_cols, b_pF], [b_S])
                    A_(lambda e: e.activation(out=junk[:], in_=pG[:, 0:C], func=AF.Square, accum_out=ssq[:, 0:1]), [b_pG], [b_junk, b_ssq])
                    V(lambda e: e.tensor_scalar(rstd[:], ssq[:], 1.0 / C, 1e-6, ALU.mult, ALU.add), [b_ssq], [b_rstd])
                    A_(lambda e: e.activation(out=rstd[:], in_=rstd[:], func=AF.Sqrt), [b_rstd], [b_rstd])
                    V(lambda e: e.reciprocal(rstd[:], rstd[:]), [b_rstd], [b_rstd])
                    V(lambda e: e.scalar_tensor_tensor(oTM[:], pG[:, 0:C], rstd[:, 0:1], nwr[:], ALU.mult, ALU.mult),
                      [b_pG, b_rstd, b_c], [b_oTM])
                    M_(lambda e, cs=cs: e.transpose(pH[:, cs], oTM[:], idf[:]), [b_oTM, b_c], [b_pH])
                A_(lambda e, oF=oF: e.activation(out=oF[:], in_=pH[:, 0:T], func=AF.Copy), [b_pH], [b_oF])
                S.dma("sp", d["oT"][hl * 128:(hl + 1) * 128, t0:t0 + T], oF[:], b_oF, reads=[b_oF], writes=[b_od])
    return [b_od]


def emit_Assm(S, d, D, NB, SL, NPAIR):
    nc = S.nc
    nK = D // 128
    T = 512
    ntt = NB * SL // T
    nblk = NPAIR // 4
    TWO_PI = 2.0 * np.pi
    uT_d = nc.dram_tensor("ssm_uT", [nblk, 128, NB * SL], F32).ap()
    b_ud, b_od, b_c = S.buf("ssm_ud"), S.buf("ssm_od"), S.buf("ssm_c")

    def V(fn, r=(), w=()):
        S.op("dve", fn, reads=r, writes=w)

    def A_(fn, r=(), w=()):
        S.op("act", fn, reads=r, writes=w)

    def P_(fn, r=(), w=()):
        S.op("pool", fn, reads=r, writes=w)

    def M_(fn, r=(), w=()):
        S.op("pe", fn, reads=r, writes=w)

    xa = S.sb("S_xa", [128, nK, T], BF16)
    b_xa = [S.buf("S_xa%d" % i) for i in range(nK)]
    st = [S.sb("S_st%d" % i, [128, T], F32) for i in range(3)]
    b_st = [S.buf("S_st%d" % i) for i in range(3)]
    G = Gemm(S, nK, nslots=3, nps=2, tag="S")
    W16, b_W16 = G.precast("ssm_W16", d["Wu"], nblk)
    cnt = {"st": 0}
    for tt in range(ntt):
        ts = slice(tt * T, (tt + 1) * T)
        S.dma("pool", xa[:], d["hnT"][:, ts].rearrange("(c p) t -> p c t", p=128), b_xa[0], writes=b_xa)

        def epi(nb, pb, pap, ts=ts):
            i = cnt["st"] % 3
            cnt["st"] += 1
            if i % 2 == 0:
                A_(lambda e: e.activation(out=st[i][:], in_=pap, func=AF.Copy), [pb], [b_st[i]])
            else:
                V(lambda e: e.tensor_copy(st[i][:], pap), [pb], [b_st[i]])
            S.dma("pool", uT_d[nb, :, ts], st[i][:], b_st[i], reads=[b_st[i]], writes=[b_ud])

        G.run(lambda kc: xa[:, kc, :], b_xa, lambda nb: W16[nb], nblk, T, epi, wbuf=b_W16)

    def sbt(name, shape, dt=F32):
        return S.sb("S_" + name, shape, dt), S.buf("S_" + name)

    idf, _ = sbt("idf", [128, 128])
    S.dma("sp", idf[:], d["identf"], b_c, writes=[b_c])
    pr, b_pr = sbt("pr", [128, 40])
    pri, b_pri = S.sb("S_pri", [128, 1], mybir.dt.int32), None
    pin, b_pin = sbt("pin", [128, 3])
    pb4, b_pb4 = sbt("pb4", [128, 4, 16])
    bb, b_bb = sbt("bb", [128, 2, 16])
    blk, b_blk = sbt("blk", [128, 4, 32])
    BBt, b_BBt = sbt("BBt", [32, 2, 128])
    dsk, b_dsk = sbt("dsk", [32, 1]); dd, b_dd = sbt("dd", [32, 32])
    cosT, b_cos = sbt("cosT", [128, T]); sinT, b_sin = sbt("sinT", [128, T]); rT, b_rT = sbt("rT", [128, T])
    ut = [sbt("u%d" % i, [32, T]) for i in range(2)]
    z1, b_z1 = sbt("z1", [128, T]); z2, b_z2 = sbt("z2", [128, T])
    z3, b_z3 = sbt("z3", [128, T]); z4, b_z4 = sbt("z4", [128, T])
    x3, b_x3 = sbt("x3", [128, T]); x4, b_x4 = sbt("x4", [128, T])
    zres = [sbt("zre%d" % i, [128, T]) for i in range(2)]
    zims = [sbt("zim%d" % i, [128, T]) for i in range(2)]
    wre, b_wre = sbt("wre", [128, T]); wim, b_wim = sbt("wim", [128, T])
    x1, b_x1 = sbt("x1", [128, T]); x2, b_x2 = sbt("x2", [128, T])
    xre, b_xre = sbt("xre", [128, T]); xim, b_xim = sbt("xim", [128, T])
    car, b_car = sbt("car", [128, 2])
    g1, b_g1 = sbt("g1", [32, T]); g2, b_g2 = sbt("g2", [32, T])
    yo = [sbt("yo%d" % i, [32, T]) for i in range(2)]
    pbu = [(S.ps("S_pbr%d" % i), S.buf("S_pbr%d" % i, True)) for i in range(2)]
    pbi = [(S.ps("S_pbi%d" % i), S.buf("S_pbi%d" % i, True)) for i in range(2)]
    pys = [(S.ps("S_py%d" % i), S.buf("S_py%d" % i, True)) for i in range(2)]
    pt_, b_pt = G.pt[0], G.pb[0]
    c = lambda i: pr[:, i:i + 1]
    it = 0
    for j in range(NPAIR):
        S.dma("sp", pin[:, 0:1], d["lre"][j], b_pin, writes=[b_pin])
        S.dma("sp", pin[:, 1:2], d["lim"][j], b_pin, writes=[b_pin])
        S.dma("sp", pin[:, 2:3], d["lst"][j], b_pin, writes=[b_pin])
        for q, key in enumerate(("bre", "bim", "cre", "cim")):
            S.dma("sp", pb4[:, q, :], d[key][j], b_pb4, writes=[b_pb4])
        S.dma("sp", dsk[:], d["dsk"][j], b_dsk, writes=[b_dsk])
        R_, W_ = [b_pin, b_pr], [b_pr]
        A_(lambda e: e.activation(out=c(0), in_=pin[:, 2:3], func=AF.Exp), R_, W_)
        V(lambda e: e.tensor_tensor(c(1), pin[:, 0:1], c(0), ALU.mult), R_, W_)
        A_(lambda e: e.activation(out=c(2), in_=c(1), func=AF.Exp), R_, W_)
        V(lambda e: e.tensor_tensor(c(3), pin[:, 1:2], c(0), ALU.mult), R_, W_)
        V(lambda e: e.tensor_scalar(c(4), c(3), 1.0 / TWO_PI, None, ALU.mult), R_, W_)
        V(lambda e: e.tensor_copy(pri[:], c(4)), R_, W_)
        V(lambda e: e.tensor_copy(c(5), pri[:]), R_, W_)
        V(lambda e: e.scalar_tensor_tensor(c(6), c(5), -TWO_PI, c(3), ALU.mult, ALU.add), R_, W_)
        V(lambda e: e.tensor_scalar(c(7), c(6), 0.5, None, ALU.mult), R_, W_)
        V(lambda e: e.tensor_scalar(c(8), c(7), -1.0, None, ALU.mult), R_, W_)
        V(lambda e: e.tensor_tensor(c(8), c(8), c(7), ALU.max), R_, W_)
        V(lambda e: e.tensor_scalar(c(8), c(8), -1.0, np.pi / 2, ALU.mult, ALU.add), R_, W_)
        A_(lambda e: e.activation(out=c(9), in_=c(7), func=AF.Sin), R_, W_)
        A_(lambda e: e.activation(out=c(10), in_=c(8), func=AF.Sin), R_, W_)
        V(lambda e: e.tensor_tensor(c(11), c(9), c(10), ALU.mult), R_, W_)
        V(lambda e: e.tensor_scalar(c(11), c(11), 2.0, None, ALU.mult), R_, W_)
        V(lambda e: e.tensor_tensor(c(12), c(10), c(10), ALU.mult), R_, W_)
        V(lambda e: e.tensor_tensor(c(13), c(9), c(9), ALU.mult), R_, W_)
        V(lambda e: e.tensor_tensor(c(12), c(12), c(13), ALU.subtract), R_, W_)
        V(lambda e: e.tensor_tensor(c(14), c(2), c(12), ALU.mult), R_, W_)
        V(lambda e: e.tensor_tensor(c(15), c(2), c(11), ALU.mult), R_, W_)
        V(lambda e: e.tensor_tensor(c(16), pin[:, 0:1], pin[:, 0:1], ALU.mult), R_, W_)
        V(lambda e: e.tensor_tensor(c(17), pin[:, 1:2], pin[:, 1:2], ALU.mult), R_, W_)
        V(lambda e: e.tensor_tensor(c(16), c(16), c(17), ALU.add), R_, W_)
        V(lambda e: e.reciprocal(c(16), c(16)), R_, W_)
        V(lambda e: e.tensor_scalar(c(17), c(14), -1.0, None, ALU.add), R_, W_)
        V(lambda e: e.tensor_tensor(c(18), c(17), pin[:, 0:1], ALU.mult), R_, W_)
        V(lambda e: e.tensor_tensor(c(19), c(15), pin[:, 1:2], ALU.mult), R_, W_)
        V(lambda e: e.tensor_tensor(c(18), c(18), c(19), ALU.add), R_, W_)
        V(lambda e: e.tensor_tensor(c(18), c(18), c(16), ALU.mult), R_, W_)
        V(lambda e: e.tensor_tensor(c(19), c(15), pin[:, 0:1], ALU.mult), R_, W_)
        V(lambda e: e.tensor_tensor(c(20), c(17), pin[:, 1:2], ALU.mult), R_, W_)
        V(lambda e: e.tensor_tensor(c(19), c(19), c(20), ALU.subtract), R_, W_)
        V(lambda e: e.tensor_tensor(c(19), c(19), c(16), ALU.mult), R_, W_)
        V(lambda e: e.tensor_scalar(c(20), c(19), -1.0, None, ALU.mult), R_, W_)
        R2 = [b_pr, b_pb4, b_bb]
        V(lambda e: e.tensor_scalar(bb[:, 0, :], pb4[:, 0, :], c(18), None, ALU.mult), R2, [b_bb])
        V(lambda e: e.scalar_tensor_tensor(bb[:, 0, :], pb4[:, 1, :], c(20), bb[:, 0, :], ALU.mult, ALU.add), R2, [b_bb])
        V(lambda e: e.tensor_scalar(bb[:, 1, :], pb4[:, 1, :], c(18), None, ALU.mult), R2, [b_bb])
        V(lambda e: e.scalar_tensor_tensor(bb[:, 1, :], pb4[:, 0, :], c(19), bb[:, 1, :], ALU.mult, ALU.add), R2, [b_bb])
        R3 = [b_bb, b_pb4, b_blk]
        V(lambda e: e.memset(blk[:], 0.0), R3, [b_blk])
        for q, (src, sgn) in enumerate(((bb[:, 0, :], 1.0), (bb[:, 1, :], 1.0), (pb4[:, 2, :], 1.0), (pb4[:, 3, :], -1.0))):
            V(lambda e, q=q, src=src, sgn=sgn: e.tensor_scalar(blk[0:64, q, 0:16], src[0:64, :], sgn, None, ALU.mult), R3, [b_blk])
            V(lambda e, q=q, src=src, sgn=sgn: e.tensor_scalar(blk[64:128, q, 16:32], src[64:128, :], sgn, None, ALU.mult), R3, [b_blk])
        for q in range(2):
            M_(lambda e, q=q: e.transpose(pt_[0:32, q * 128:(q + 1) * 128], blk[:, q, :], idf[:]), [b_blk, b_c], [b_pt])
        V(lambda e: e.tensor_copy(BBt[:].rearrange("p q s -> p (q s)"), pt_[0:32, 0:256]), [b_pt], [b_BBt])
        V(lambda e: e.tensor_scalar(dd[:], idf[0:32, 0:32], dsk[:, 0:1], None, ALU.mult), [b_dsk, b_c], [b_dd])
        RT = [b_pr, b_cos, b_sin]
        V(lambda e: e.tensor_copy(cosT[:, 0:1], c(12)), RT, [b_cos])
        V(lambda e: e.tensor_copy(sinT[:, 0:1], c(11)), RT, [b_sin])
        m = 1
        while m < T:
            V(lambda e, m=m: e.tensor_scalar(c(21), sinT[:, m - 1:m], -1.0, None, ALU.mult), RT, [b_pr])
            V(lambda e, m=m: e.tensor_scalar(cosT[:, m:2 * m], cosT[:, 0:m], cosT[:, m - 1:m], None, ALU.mult), RT, [b_cos])
            V(lambda e, m=m: e.scalar_tensor_tensor(cosT[:, m:2 * m], sinT[:, 0:m], c(21), cosT[:, m:2 * m], ALU.mult, ALU.add), RT, [b_cos])
            V(lambda e, m=m: e.tensor_scalar(sinT[:, m:2 * m], sinT[:, 0:m], cosT[:, m - 1:m], None, ALU.mult), RT, [b_sin])
            V(lambda e, m=m: e.scalar_tensor_tensor(sinT[:, m:2 * m], cosT[:, 0:m], sinT[:, m - 1:m], sinT[:, m:2 * m], ALU.mult, ALU.add), RT, [b_sin])
            m *= 2
        V(lambda e: e.memset(rT[:], 1.0), [b_rT], [b_rT])
        V(lambda e: e.tensor_scalar(rT[:], rT[:], c(2), None, ALU.mult), [b_pr, b_rT], [b_rT])
        blkno, prow = j // 4, 32 * (j % 4)
        steps = [(b, ti) for b in range(NB) for ti in range(SL // T)]

        def stage1(b, ti):
            nonlocal it
            t0 = b * SL + ti * T
            par = it % 2
            it += 1
            u_, b_u = ut[par]
            S.dma("sp", u_[:], uT_d[blkno, prow:prow + 32, t0:t0 + T], b_u, reads=[b_ud], writes=[b_u])
            (pre, b_pre), (pim, b_pim) = pbu[par], pbi[par]
            (zre, b_zre), (zim, b_zim) = zres[par], zims[par]
            M_(lambda e: e.matmul(pre[:, 0:T], BBt[:, 0, :], u_[:], start=True, stop=True), [b_BBt, b_u], [b_pre])
            M_(lambda e: e.matmul(pim[:, 0:T], BBt[:, 1, :], u_[:], start=True, stop=True), [b_BBt, b_u], [b_pim])
            V(lambda e: e.tensor_tensor(z1[:], pre[:, 0:T], cosT[:], ALU.mult), [b_pre, b_cos], [b_z1])
            V(lambda e: e.tensor_tensor(z3[:], pim[:, 0:T], cosT[:], ALU.mult), [b_pim, b_cos], [b_z3])
            V(lambda e: e.tensor_tensor(z2[:], pim[:, 0:T], sinT[:], ALU.mult), [b_pim, b_sin], [b_z2])
            V(lambda e: e.tensor_tensor(z4[:], pre[:, 0:T], sinT[:], ALU.mult), [b_pre, b_sin], [b_z4])
            P_(lambda e: e.tensor_tensor(zre[:], z1[:], z2[:], ALU.add), [b_z1, b_z2], [b_zre])
            P_(lambda e: e.tensor_tensor(zim[:], z3[:], z4[:], ALU.subtract), [b_z3, b_z4], [b_zim])
            return (t0, par, u_, b_u, zre, b_zre, zim, b_zim)

        def stage2(ti, st_):
            t0, par, u_, b_u, zre, b_zre, zim, b_zim = st_
            if ti == 0:
                V(lambda e: e.memset(car[:], 0.0), [b_car], [b_car])
            V(lambda e: e.tensor_tensor_scan(wre[:], rT[:], zre[:], car[:, 0:1], ALU.mult, ALU.add), [b_rT, b_zre, b_car], [b_wre])
            V(lambda e: e.tensor_tensor_scan(wim[:], rT[:], zim[:], car[:, 1:2], ALU.mult, ALU.add), [b_rT, b_zim, b_car], [b_wim])
            V(lambda e: e.tensor_tensor(x1[:], wre[:], cosT[:], ALU.mult), [b_wre, b_cos], [b_x1])
            V(lambda e: e.tensor_tensor(x3[:], wim[:], cosT[:], ALU.mult), [b_wim, b_cos], [b_x3])
            V(lambda e: e.tensor_tensor(x2[:], wim[:], sinT[:], ALU.mult), [b_wim, b_sin], [b_x2])
            V(lambda e: e.tensor_tensor(x4[:], wre[:], sinT[:], ALU.mult), [b_wre, b_sin], [b_x4])
            V(lambda e: e.tensor_tensor(xre[:], x1[:], x2[:], ALU.subtract), [b_x1, b_x2], [b_xre])
            V(lambda e: e.tensor_tensor(xim[:], x3[:], x4[:], ALU.add), [b_x3, b_x4], [b_xim])
            V(lambda e: e.tensor_copy(car[:, 0:1], xre[:, T - 1:T]), [b_xre, b_car], [b_car])
            V(lambda e: e.tensor_copy(car[:, 1:2], xim[:, T - 1:T]), [b_xim, b_car], [b_car])
            py, b_py = pys[par]
            M_(lambda e: e.matmul(py[0:32, 0:T], blk[:, 2, :], xre[:], start=True, stop=False), [b_blk, b_xre], [b_py])
            M_(lambda e: e.matmul(py[0:32, 0:T], blk[:, 3, :], xim[:], start=False, stop=False), [b_blk, b_xim], [b_py])
            M_(lambda e: e.matmul(py[0:32, 0:T], dd[:], u_[:], start=False, stop=True), [b_dd, b_u], [b_py])
            return lambda: stage3(t0, par, py, b_py)

        def stage3(t0, par, py, b_py):
            yo_, b_yo = yo[par]
            A_(lambda e: e.activation(out=g1[:], in_=py[0:32, 0:T], func=AF.Square), [b_py], [b_g1])
            V(lambda e: e.tensor_scalar(g1[:], g1[:], 0.044715, 1.0, ALU.mult, ALU.add), [b_g1], [b_g1])
            V(lambda e: e.tensor_tensor(g1[:], g1[:], py[0:32, 0:T], ALU.mult), [b_g1, b_py], [b_g1])
            A_(lambda e: e.activation(out=g2[:], in_=g1[:], func=AF.Tanh, scale=float(np.sqrt(2.0 / np.pi))), [b_g1], [b_g2])
            V(lambda e: e.tensor_scalar(g2[:], g2[:], 1.0, 0.5, ALU.add, ALU.mult), [b_g2], [b_g2])
            V(lambda e: e.tensor_tensor(yo_[:], g2[:], py[0:32, 0:T], ALU.mult), [b_g2, b_py], [b_yo])
            S.dma("sp", d["oT"][j * 32:(j + 1) * 32, t0:t0 + T], yo_[:], b_yo, reads=[b_yo], writes=[b_od])

        nxt = stage1(*steps[0])
        prev3 = None
        for n, (b, ti) in enumerate(steps):
            cur = nxt
            if n + 1 < len(steps):
                nxt = stage1(*steps[n + 1])
            s3 = stage2(ti, cur)
            if prev3 is not None:
                prev3()
            prev3 = s3
        prev3()
    return [b_od]


D_MODEL, BATCH, SEQ, DEPTH, PLE = 4096, 2, 8192, 4, 256
NTOK = BATCH * SEQ
NCB = 8
TT = NTOK // NCB
HEADS, W_MIX = 32, 4096
HG = NCORES // BATCH
NH_LOC = HEADS // HG
CW = W_MIX // HG


def _new_nc():
    return bass.Bass("TRN2", target_bir_lowering=False)


def _decl(nc, ins, outs):
    d = {}
    for name, (shape, dt) in ins.items():
        d[name] = nc.dram_tensor(name, list(shape), dt, kind="ExternalInput").ap()
    for name, (shape, dt) in outs.items():
        d[name] = nc.dram_tensor(name, list(shape), dt, kind="ExternalOutput").ap()
    return d


def _build(emit, ins, outs, *args, **kw):
    nc = _new_nc()
    d = _decl(nc, ins, outs)
    with contextlib.ExitStack() as st:
        S = Sched(nc, st)
        fin = emit(S, d, *args, **kw)
        S.finish(fin)
    return nc


def _launch(nc, in_maps):
    res = run_bass_kernel_spmd(nc, in_maps, core_ids=list(range(len(in_maps))))
    return res.results


def colvec(g):
    return np.ascontiguousarray(np.asarray(g, np.float32).reshape(-1, 128).T)


def fox_consts(NH, NB, nseg):
    NP = NH * NB * nseg
    m = np.zeros((NP, NP), np.float32)
    for h in range(NH):
        for b in range(NB):
            base = (h * NB + b) * nseg
            for s1 in range(nseg):
                m[base + s1, base + s1 + 1:base + nseg] = 1.0
    p = np.arange(128)
    tri = np.where(p[None, :] >= p[:, None], 0.0, -1e30).astype(np.float32)
    return {"mtri": m, "tri": tri, "identf": np.eye(128, dtype=np.float32),
            "identb": np.eye(128).astype(ml_dtypes.bfloat16)}


def gdn_consts():
    p = np.arange(128)
    mstr = np.where(p[:, None] > p[None, :], 0.0, -1e30).astype(np.float32)
    mup = np.where(p[None, :] >= p[:, None], 0.0, -1e30).astype(np.float32)
    return {"mstr4": np.tile(mstr, (1, 4)), "mup4": np.tile(mup, (1, 4)),
            "id4": np.tile(np.eye(128, dtype=np.float32), (1, 4)), "identf": np.eye(128, dtype=np.float32)}


def kernel(x, p, norm_mix, fox_w_in, fox_b_f, fox_w_out, gdn_w_in, gdn_conv, gdn_a_log,
           gdn_dt_bias, gdn_norm, gdn_w_out, ssm_w_in, ssm_lam_re, ssm_lam_im, ssm_b_re,
           ssm_b_im, ssm_c_re, ssm_c_im, ssm_log_step, ssm_d, ssm_w_glu, ssm_b_glu, ssm_w_out,
           norm_ple, ple_w_proj, ple_w_gate, final_norm):
    f32 = np.float32
    D, nK = D_MODEL, D_MODEL // 128
    x = np.asarray(x, f32)
    p = np.asarray(p, f32)
    tok = [slice(c * TT, (c + 1) * TT) for c in range(NCB)]
    hT = np.ascontiguousarray(x.reshape(NTOK, D).T)
    hT_sh = [np.ascontiguousarray(hT[:, t]) for t in tok]
    del hT
    wblk = (nK, 128, nK, 128)

    ncN = _build(emit_N, {"hT": ((D, TT), F32), "g": ((128, nK), F32)}, {"hnT_out": ((D, TT), BF16)}, D, TT, 512)
    g0 = colvec(norm_mix[0])
    r = _launch(ncN, [{"hT": hT_sh[c], "g": g0} for c in range(NCB)])
    hn_sh = [r[c]["hnT_out"] for c in range(NCB)]

    b_ins = {"hT": ((D, TT), F32), "hnT": ((D, TT), BF16), "oT": ((D, TT), F32), "pT": ((PLE, TT), F32),
             "g2": ((128, nK), F32), "gn": ((128, nK), F32), "Wz": (wblk, F32), "Wout": (wblk, F32),
             "Wg": (wblk, F32), "Wp": ((nK, 128, PLE // 128, 128), F32)}
    progs = {}
    out = None
    for i in range(DEPTH):
        kind, j = i % 3, i // 3
        hn_full = np.concatenate(hn_sh, axis=1)
        hn_b = [np.ascontiguousarray(hn_full[:, bi * SEQ:(bi + 1) * SEQ]) for bi in range(BATCH)]
        if kind == 0:
            w_in = np.asarray(fox_w_in[j], f32)
            if "fox" not in progs:
                NP = NH_LOC * (SEQ // 512)
                progs["fox"] = _build(
                    emit_Afox,
                    {"hnT": ((D, SEQ), BF16), "W": ((3 * NH_LOC + 1, 128, nK, 128), F32), "bf": ((NH_LOC, 1), F32),
                     "mtri": ((NP, NP), F32), "tri": ((128, 128), F32), "identf": ((128, 128), F32),
                     "identb": ((128, 128), BF16)},
                    {"oT": ((NH_LOC * 128, SEQ), F32)}, D, 1, SEQ, NH_LOC)
            cst = fox_consts(NH_LOC, 1, SEQ // 512)
            maps = []
            for c in range(NCORES):
                bi, hg = c // HG, c % HG
                cs = slice(CW * hg, CW * (hg + 1))
                wf = np.zeros((D, 128), f32)
                wf[:, :NH_LOC] = w_in[:, 4 * W_MIX + NH_LOC * hg: 4 * W_MIX + NH_LOC * (hg + 1)]
                Wc = np.concatenate([w_in[:, 0 * W_MIX:1 * W_MIX][:, cs], w_in[:, 1 * W_MIX:2 * W_MIX][:, cs],
                                     w_in[:, 2 * W_MIX:3 * W_MIX][:, cs], wf], axis=1)
                mm = {"hnT": hn_b[bi], "W": blockw(Wc),
                      "bf": np.asarray(fox_b_f[j], f32)[NH_LOC * hg:NH_LOC * (hg + 1)].reshape(NH_LOC, 1)}
                mm.update(cst)
                maps.append(mm)
            r = _launch(progs["fox"], maps)
            Wz = blockw(w_in[:, 3 * W_MIX:4 * W_MIX])
            Wout = blockw(np.asarray(fox_w_out[j], f32))
        elif kind == 1:
            w_in = np.asarray(gdn_w_in[j], f32)
            if "gdn" not in progs:
                progs["gdn"] = _build(
                    emit_Agdn,
                    {"hnT": ((D, SEQ), BF16), "W": ((3 * NH_LOC + 1, 128, nK, 128), F32), "cw": ((3 * NH_LOC, 128, 4), F32),
                     "gbias": ((2 * NH_LOC, 1), F32), "galog": ((2 * NH_LOC, 1), F32), "nw": ((128,), F32),
                     "mstr4": ((128, 512), F32), "mup4": ((128, 512), F32), "id4": ((128, 512), F32),
                     "identf": ((128, 128), F32)},
                    {"oT": ((NH_LOC * 128, SEQ), F32)}, D, 1, SEQ, NH_LOC)
            cst = gdn_consts()
            conv = np.asarray(gdn_conv[j], f32)
            maps = []
            for c in range(NCORES):
                bi, hg = c // HG, c % HG
                cs = slice(CW * hg, CW * (hg + 1))
                hs = slice(NH_LOC * hg, NH_LOC * (hg + 1))
                wg = np.zeros((D, 128), f32)
                wg[:, :NH_LOC] = w_in[:, 4 * W_MIX:4 * W_MIX + HEADS][:, hs]
                wg[:, NH_LOC:2 * NH_LOC] = w_in[:, 4 * W_MIX + HEADS:4 * W_MIX + 2 * HEADS][:, hs]
                Wc = np.concatenate([w_in[:, 0 * W_MIX:1 * W_MIX][:, cs], w_in[:, 1 * W_MIX:2 * W_MIX][:, cs],
                                     w_in[:, 2 * W_MIX:3 * W_MIX][:, cs], wg], axis=1)
                cwc = np.concatenate([conv[:, 0 * W_MIX:1 * W_MIX][:, cs], conv[:, 1 * W_MIX:2 * W_MIX][:, cs],
                                      conv[:, 2 * W_MIX:3 * W_MIX][:, cs]], axis=1)
                zer = np.zeros(NH_LOC, f32)
                mm = {"hnT": hn_b[bi], "W": blockw(Wc), "cw": np.ascontiguousarray(cwc.T.reshape(3 * NH_LOC, 128, 4)),
                      "gbias": np.concatenate([zer, np.asarray(gdn_dt_bias[j], f32)[hs]]).reshape(-1, 1),
                      "galog": np.concatenate([zer, np.asarray(gdn_a_log[j], f32)[hs]]).reshape(-1, 1),
                      "nw": np.asarray(gdn_norm[j], f32)}
                mm.update(cst)
                maps.append(mm)
            r = _launch(progs["gdn"], maps)
            Wz = blockw(w_in[:, 3 * W_MIX:4 * W_MIX])
            Wout = blockw(np.asarray(gdn_w_out[j], f32))
        else:
            w_in = np.asarray(ssm_w_in[j], f32)
            NPAIR = CW // 32
            if "ssm" not in progs:
                ins = {"hnT": ((D, SEQ), BF16), "Wu": ((NPAIR // 4, 128, nK, 128), F32), "identf": ((128, 128), F32),
                       "dsk": ((NPAIR, 32, 1), F32)}
                for k in ("lre", "lim", "lst"):
                    ins[k] = ((NPAIR, 128, 1), F32)
                for k in ("bre", "bim", "cre", "cim"):
                    ins[k] = ((NPAIR, 128, 16), F32)
                progs["ssm"] = _build(emit_Assm, ins, {"oT": ((NPAIR * 32, SEQ), F32)}, D, 1, SEQ, NPAIR)
            maps = []
            for c in range(NCORES):
                bi, hg = c // HG, c % HG
                cs = slice(CW * hg, CW * (hg + 1))
                gs = slice(2 * NPAIR * hg, 2 * NPAIR * (hg + 1))
                mm = {"hnT": hn_b[bi], "Wu": blockw(w_in[:, :W_MIX][:, cs]), "identf": np.eye(128, dtype=f32),
                      "lre": np.ascontiguousarray(np.asarray(ssm_lam_re[j], f32)[gs]).reshape(NPAIR, 128, 1),
                      "lim": np.ascontiguousarray(np.asarray(ssm_lam_im[j], f32)[gs]).reshape(NPAIR, 128, 1),
                      "lst": np.repeat(np.asarray(ssm_log_step[j], f32)[gs], 64).reshape(NPAIR, 128, 1),
                      "bre": np.ascontiguousarray(np.asarray(ssm_b_re[j], f32)[gs]).reshape(NPAIR, 128, 16),
                      "bim": np.ascontiguousarray(np.asarray(ssm_b_im[j], f32)[gs]).reshape(NPAIR, 128, 16),
                      "cre": np.ascontiguousarray(np.asarray(ssm_c_re[j], f32)[gs].transpose(0, 2, 1)).reshape(NPAIR, 128, 16),
                      "cim": np.ascontiguousarray(np.asarray(ssm_c_im[j], f32)[gs].transpose(0, 2, 1)).reshape(NPAIR, 128, 16),
                      "dsk": np.ascontiguousarray(np.asarray(ssm_d[j], f32)[cs]).reshape(NPAIR, 32, 1)}
                maps.append(mm)
            r = _launch(progs["ssm"], maps)
            Wz = blockw(w_in[:, W_MIX:2 * W_MIX])
            Wout = blockw(np.asarray(ssm_w_out[j], f32))
        del hn_full, hn_b
        oT_full = np.concatenate([np.concatenate([r[bi * HG + hg]["oT"] for hg in range(HG)], axis=0)
                                  for bi in range(BATCH)], axis=1)
        final = i == DEPTH - 1
        ssm = kind == 2
        key = ("B", ssm, final)
        if key not in progs:
            ins = dict(b_ins)
            if ssm:
                ins["Wglu"] = (wblk, F32)
                ins["bglu"] = ((128, nK), F32)
            outs = {"out": ((D, TT), F32)} if final else {"hT_out": ((D, TT), F32), "hnT_out": ((D, TT), BF16)}
            progs[key] = _build(emit_B, ins, outs, D, TT, 512, ssm=ssm, final=final, PD=PLE)
        pT = np.ascontiguousarray(p[i].reshape(NTOK, PLE).T)
        shared = {"g2": colvec(norm_ple[i]), "gn": colvec(final_norm if final else norm_mix[i + 1]),
                  "Wz": Wz, "Wout": Wout, "Wg": blockw(np.asarray(ple_w_gate[i], f32)),
                  "Wp": blockw(np.asarray(ple_w_proj[i], f32))}
        if ssm:
            shared["Wglu"] = blockw(np.asarray(ssm_w_glu[j], f32))
            shared["bglu"] = colvec(ssm_b_glu[j])
        maps = []
        for c in range(NCB):
            mm = {"hT": hT_sh[c], "hnT": hn_sh[c], "oT": np.ascontiguousarray(oT_full[:, tok[c]]),
                  "pT": np.ascontiguousarray(pT[:, tok[c]])}
            mm.update(shared)
            maps.append(mm)
        del oT_full
        r = _launch(progs[key], maps)
        if final:
            outT = np.concatenate([r[c]["out"] for c in range(NCB)], axis=1)
            out = np.ascontiguousarray(outT.T).reshape(BATCH, SEQ, D)
        else:
            hT_sh = [r[c]["hT_out"] for c in range(NCB)]
            hn_sh = [r[c]["hnT_out"] for c in range(NCB)]
    return out.astype(np.float32)
```
